# Optimizing a Trainium2 kernel written in Bass

```python
import math
import jax, jax.numpy as jnp
from jax import lax
import numpy as np

D_MODEL = 1024
BATCH = 4
SEQ = 4096
DEPTH = 2

HEAD_DIM = 64
SSM_HEADS = 16
SSM_HEAD_DIM = HEAD_DIM
SSM_D_INNER = SSM_HEADS * SSM_HEAD_DIM
SSM_GROUPS = 2
SSM_HEADS_PER_GROUP = SSM_HEADS // SSM_GROUPS
SSM_STATE = 128
SSM_CONV = 4
SSM_CHUNK = 128
SSM_CONV_DIM = SSM_D_INNER + 2 * SSM_GROUPS * SSM_STATE
SB_HEADS = 8
SB_WIDTH = SB_HEADS * HEAD_DIM
DIFF_HEADS = 4
DIFF_QK_DIM = HEAD_DIM
DIFF_V_DIM = 2 * HEAD_DIM
DIFF_QK_WIDTH = DIFF_HEADS * 2 * DIFF_QK_DIM
DIFF_WIDTH = DIFF_HEADS * DIFF_V_DIM
MIX_WIDTH = SSM_D_INNER + SB_WIDTH + DIFF_WIDTH
IN_SPLITS = (SSM_D_INNER, SSM_CONV_DIM, SSM_HEADS,
             SB_WIDTH, SB_WIDTH, SB_WIDTH,
             DIFF_QK_WIDTH, DIFF_QK_WIDTH, DIFF_WIDTH)
IN_DIM = sum(IN_SPLITS)
D_FF = 2816
FFN_HALF = 0.5
N_REL_BUCKETS = 32
REL_MAX_DIST = 128
Q_BLOCK = 128
N_MOD = 9
EPS = 1e-6

kernel_name = "hybrid_ssd_stickbreak_diffattn_macaron_block"


def _rmsnorm(x, g):
    xf = x.astype(jnp.float32)
    y = xf * lax.rsqrt(jnp.mean(xf * xf, axis=-1, keepdims=True) + EPS)
    return (y * g.astype(jnp.float32)).astype(x.dtype)


def _modulate(h, shift, scale):
    return h * (1 + scale[:, None, :]) + shift[:, None, :]


def _swiglu(h, w13, w2):
    a, u = jnp.split(h @ w13, 2, axis=-1)
    return (jax.nn.silu(a) * u) @ w2


def _causal_depthwise_conv(u, w, b):
    k = w.shape[-1]
    rhs = jnp.transpose(w)[:, None, :].astype(u.dtype)
    out = lax.conv_general_dilated(u, rhs, window_strides=(1,), padding=[(k - 1, 0)],
                                   dimension_numbers=("NWC", "WIO", "NWC"),
                                   feature_group_count=u.shape[-1])
    return out + b.astype(u.dtype)


def _t5_bucket(dist):
    max_exact = N_REL_BUCKETS // 2
    d = jnp.maximum(dist.astype(jnp.float32), float(max_exact))
    large = max_exact + (jnp.log(d / max_exact) / math.log(REL_MAX_DIST / max_exact)
                         * (N_REL_BUCKETS - max_exact)).astype(jnp.int32)
    large = jnp.minimum(large, N_REL_BUCKETS - 1)
    return jnp.where(dist < max_exact, dist, large)


def _sweep_query_blocks(fn, seq):
    starts = jnp.arange(seq // Q_BLOCK, dtype=jnp.int32) * Q_BLOCK
    out = lax.map(fn, starts)
    nb, b, qb = out.shape[:3]
    return jnp.moveaxis(out, 0, 1).reshape((b, nb * qb) + out.shape[3:])


def _ssd_mixer(z, xbc, dt_raw, conv_w, conv_b, dt_bias, a_log, d_skip, norm_g):
    f32 = jnp.float32
    b, s, _ = z.shape
    nc = s // SSM_CHUNK
    xbc = jax.nn.silu(_causal_depthwise_conv(xbc, conv_w, conv_b))
    xs, bm, cm = jnp.split(xbc, [SSM_D_INNER, SSM_D_INNER + SSM_GROUPS * SSM_STATE], axis=-1)
    dt = jax.nn.softplus(dt_raw.astype(f32) + dt_bias.astype(f32))
    a = -jnp.exp(a_log.astype(f32))
    xh = xs.astype(f32).reshape(b, s, SSM_HEADS, SSM_HEAD_DIM)
    xd = (xh * dt[..., None]).reshape(b, nc, SSM_CHUNK, SSM_GROUPS, SSM_HEADS_PER_GROUP, SSM_HEAD_DIM)
    bc = bm.astype(f32).reshape(b, nc, SSM_CHUNK, SSM_GROUPS, SSM_STATE)
    cc = cm.astype(f32).reshape(b, nc, SSM_CHUNK, SSM_GROUPS, SSM_STATE)
    da = (dt * a).reshape(b, nc, SSM_CHUNK, SSM_GROUPS, SSM_HEADS_PER_GROUP)
    da = jnp.transpose(da, (0, 3, 4, 1, 2))
    a_cum = jnp.cumsum(da, axis=-1)
    causal = jnp.tril(jnp.ones((SSM_CHUNK, SSM_CHUNK), dtype=bool))
    seg = a_cum[..., :, None] - a_cum[..., None, :]
    decay = jnp.exp(jnp.where(causal, seg, -jnp.inf))
    cb = jnp.einsum("bclgn,bcsgn->bgcls", cc, bc)
    y_diag = jnp.einsum("bgecls,bcsgep->bclgep", cb[:, :, None] * decay, xd)
    decay_to_end = jnp.exp(a_cum[..., -1:] - a_cum)
    chunk_states = jnp.einsum("bclgn,bgecl,bclgep->bcgepn", bc, decay_to_end, xd)
    chunk_decay = jnp.exp(a_cum[..., -1])

    def carry_state(h, inp):
        st, dec = inp
        return h * dec[..., None, None] + st, h

    h0 = jnp.zeros_like(chunk_states[:, 0])
    _, prev = lax.scan(carry_state, h0,
                       (jnp.moveaxis(chunk_states, 1, 0), jnp.moveaxis(chunk_decay, -1, 0)))
    prev = jnp.moveaxis(prev, 0, 1)
    y_off = jnp.einsum("bclgn,bcgepn,bgecl->bclgep", cc, prev, jnp.exp(a_cum))
    y = (y_diag + y_off).reshape(b, s, SSM_HEADS, SSM_HEAD_DIM) + d_skip.astype(f32)[:, None] * xh
    y = y.reshape(b, s, SSM_D_INNER) * jax.nn.silu(z.astype(f32))
    yg = y.reshape(b, s, SSM_GROUPS, SSM_D_INNER // SSM_GROUPS)
    yg = yg * lax.rsqrt(jnp.mean(yg * yg, axis=-1, keepdims=True) + EPS)
    return (yg.reshape(b, s, SSM_D_INNER) * norm_g.astype(f32)).astype(z.dtype)


def _stick_breaking_attention(q, k, v):
    b, s, h, d = q.shape
    scale = d ** -0.5
    kpos = jnp.arange(s, dtype=jnp.int32)

    def block(start):
        qb = lax.dynamic_slice_in_dim(q, start, Q_BLOCK, axis=1)
        qpos = start + jnp.arange(Q_BLOCK, dtype=jnp.int32)
        strict = kpos[None, :] < qpos[:, None]
        logits = jnp.einsum("bqhd,bkhd->bhqk", qb, k).astype(jnp.float32) * scale
        log_1m_beta = jnp.where(strict, jax.nn.log_sigmoid(-logits), 0.0)
        log_stick = lax.cumsum(log_1m_beta, axis=3, reverse=True) - log_1m_beta
        weights = jnp.where(strict, jnp.exp(jax.nn.log_sigmoid(logits) + log_stick), 0.0)
        return jnp.einsum("bhqk,bkhd->bqhd", weights.astype(v.dtype), v)

    return _sweep_query_blocks(block, s).reshape(b, s, h * d)


def _diff_attention(q, k, v, rel_bias, lam, lambda_init, subln_g):
    b, s, h, _, dk = q.shape
    scale = dk ** -0.5
    kpos = jnp.arange(s, dtype=jnp.int32)

    def block(start):
        qb = lax.dynamic_slice_in_dim(q, start, Q_BLOCK, axis=1)
        qpos = start + jnp.arange(Q_BLOCK, dtype=jnp.int32)
        causal = kpos[None, :] <= qpos[:, None]
        bucket = _t5_bucket(jnp.maximum(qpos[:, None] - kpos[None, :], 0))
        bias = jnp.transpose(rel_bias[bucket], (2, 0, 1)).astype(jnp.float32)
        logits = jnp.einsum("bqhmd,bkhmd->bmhqk", qb, k).astype(jnp.float32) * scale + bias
        p = jax.nn.softmax(jnp.where(causal, logits, -jnp.inf), axis=-1)
        attn = p[:, 0] - lam * p[:, 1]
        return jnp.einsum("bhqk,bkhe->bqhe", attn.astype(v.dtype), v)

    o = _sweep_query_blocks(block, s)
    o = _rmsnorm(o, subln_g) * (1.0 - lambda_init)
    return o.reshape(b, s, h * DIFF_V_DIM)


def _hybrid_mixer(h, w_in, conv_w, conv_b, dt_bias, a_log, d_skip, ssm_norm,
                  lq1, lk1, lq2, lk2, subln_g, rel_bias, w_out, lambda_init):
    b, s, _ = h.shape
    offsets = [int(o) for o in np.cumsum(IN_SPLITS)[:-1]]
    z, xbc, dt_raw, sq, sk, sv, dq, dkk, dv = jnp.split(h @ w_in, offsets, axis=-1)
    y_ssm = _ssd_mixer(z, xbc, dt_raw, conv_w, conv_b, dt_bias, a_log, d_skip, ssm_norm)
    y_sb = _stick_breaking_attention(sq.reshape(b, s, SB_HEADS, HEAD_DIM),
                                     sk.reshape(b, s, SB_HEADS, HEAD_DIM),
                                     sv.reshape(b, s, SB_HEADS, HEAD_DIM))
    lam = (jnp.exp(jnp.sum(lq1.astype(jnp.float32) * lk1.astype(jnp.float32)))
           - jnp.exp(jnp.sum(lq2.astype(jnp.float32) * lk2.astype(jnp.float32))) + lambda_init)
    y_diff = _diff_attention(dq.reshape(b, s, DIFF_HEADS, 2, DIFF_QK_DIM),
                             dkk.reshape(b, s, DIFF_HEADS, 2, DIFF_QK_DIM),
                             dv.reshape(b, s, DIFF_HEADS, DIFF_V_DIM),
                             rel_bias, lam, lambda_init, subln_g)
    y = jnp.concatenate([y_ssm.astype(h.dtype), y_sb.astype(h.dtype), y_diff.astype(h.dtype)], axis=-1)
    return y @ w_out


def setup_inputs(seed: int = 0) -> dict:
    key = jax.random.key(seed)
    ks = jax.random.split(key, 32)
    f32 = jnp.float32

    def nrm(k, shape, scale):
        return jax.random.normal(k, shape, f32) * scale

    def gain(k, shape):
        return 1.0 + 0.01 * jax.random.normal(k, shape, f32)

    gate_rows = jnp.zeros((N_MOD, D_MODEL), f32).at[2::3].set(1.0).reshape(-1)
    dt = jnp.exp(jax.random.uniform(ks[9], (DEPTH, SSM_HEADS), f32)
                 * (math.log(0.1) - math.log(0.001)) + math.log(0.001))
    dt = jnp.maximum(dt, 1e-4)
    return {
        "x": jax.random.normal(ks[0], (BATCH, SEQ, D_MODEL), f32),
        "c": jax.random.normal(ks[1], (BATCH, D_MODEL), f32),
        "ada_w": nrm(ks[2], (DEPTH, D_MODEL, N_MOD * D_MODEL), 0.1 * D_MODEL ** -0.5),
        "ada_b": 0.01 * jax.random.normal(ks[3], (DEPTH, N_MOD * D_MODEL), f32) + gate_rows,
        "ffn1_norm": gain(ks[4], (DEPTH, D_MODEL)),
        "ffn1_w13": nrm(ks[5], (DEPTH, D_MODEL, 2 * D_FF), D_MODEL ** -0.5),
        "ffn1_w2": nrm(ks[6], (DEPTH, D_FF, D_MODEL), D_FF ** -0.5),
        "mix_norm": gain(ks[7], (DEPTH, D_MODEL)),
        "w_in": nrm(ks[8], (DEPTH, D_MODEL, IN_DIM), D_MODEL ** -0.5),
        "ssm_conv_w": nrm(ks[10], (DEPTH, SSM_CONV_DIM, SSM_CONV), SSM_CONV ** -0.5),
        "ssm_conv_b": nrm(ks[11], (DEPTH, SSM_CONV_DIM), 0.01),
        "ssm_dt_bias": dt + jnp.log(-jnp.expm1(-dt)),
        "ssm_a_log": jnp.log(jax.random.uniform(ks[12], (DEPTH, SSM_HEADS), f32, 1.0, 16.0)),
        "ssm_d": gain(ks[13], (DEPTH, SSM_HEADS)),
        "ssm_norm": gain(ks[14], (DEPTH, SSM_D_INNER)),
        "diff_lambda_q1": nrm(ks[15], (DEPTH, DIFF_QK_DIM), 0.1),
        "diff_lambda_k1": nrm(ks[16], (DEPTH, DIFF_QK_DIM), 0.1),
        "diff_lambda_q2": nrm(ks[17], (DEPTH, DIFF_QK_DIM), 0.1),
        "diff_lambda_k2": nrm(ks[18], (DEPTH, DIFF_QK_DIM), 0.1),
        "diff_subln": gain(ks[19], (DEPTH, DIFF_V_DIM)),
        "rel_bias": nrm(ks[20], (N_REL_BUCKETS, DIFF_HEADS), 0.5),
        "w_out": nrm(ks[21], (DEPTH, MIX_WIDTH, D_MODEL), MIX_WIDTH ** -0.5),
        "ffn2_norm": gain(ks[22], (DEPTH, D_MODEL)),
        "ffn2_w13": nrm(ks[23], (DEPTH, D_MODEL, 2 * D_FF), D_MODEL ** -0.5),
        "ffn2_w2": nrm(ks[24], (DEPTH, D_FF, D_MODEL), D_FF ** -0.5),
        "final_norm": gain(ks[25], (D_MODEL,)),
    }


def reference(x, c, ada_w, ada_b, ffn1_norm, ffn1_w13, ffn1_w2, mix_norm, w_in,
              ssm_conv_w, ssm_conv_b, ssm_dt_bias, ssm_a_log, ssm_d, ssm_norm,
              diff_lambda_q1, diff_lambda_k1, diff_lambda_q2, diff_lambda_k2, diff_subln,
              rel_bias, w_out, ffn2_norm, ffn2_w13, ffn2_w2, final_norm):
    b = x.shape[0]
    cond = jax.nn.silu(c)
    for l in range(DEPTH):
        mod = (cond @ ada_w[l] + ada_b[l]).reshape(b, N_MOD, D_MODEL)
        h = _modulate(_rmsnorm(x, ffn1_norm[l]), mod[:, 0], mod[:, 1])
        x = x + FFN_HALF * mod[:, 2][:, None, :] * _swiglu(h, ffn1_w13[l], ffn1_w2[l])
        lambda_init = 0.8 - 0.6 * math.exp(-0.3 * l)
        h = _modulate(_rmsnorm(x, mix_norm[l]), mod[:, 3], mod[:, 4])
        y = _hybrid_mixer(h, w_in[l], ssm_conv_w[l], ssm_conv_b[l], ssm_dt_bias[l], ssm_a_log[l],
                          ssm_d[l], ssm_norm[l], diff_lambda_q1[l], diff_lambda_k1[l],
                          diff_lambda_q2[l], diff_lambda_k2[l], diff_subln[l], rel_bias,
                          w_out[l], lambda_init)
        x = x + mod[:, 5][:, None, :] * y
        h = _modulate(_rmsnorm(x, ffn2_norm[l]), mod[:, 6], mod[:, 7])
        x = x + FFN_HALF * mod[:, 8][:, None, :] * _swiglu(h, ffn2_w13[l], ffn2_w2[l])
    return _rmsnorm(x, final_norm)
```

```python
import math
from contextlib import ExitStack

import numpy as np
import concourse.bass as bass
import concourse.mybir as mybir
from concourse.bass_utils import run_bass_kernel_spmd

F32 = mybir.dt.float32
BF16 = mybir.dt.bfloat16
AF = mybir.ActivationFunctionType
ALU = mybir.AluOpType
AX = mybir.AxisListType

D = 1024
KD = 8
DFF = 2816
NJ = 22
NJH = 11
SEQ = 4096
DEPTH = 2
TT = 512
EPS = 1e-6
NEG = -30000.0

C_N1, C_N2, C_N3 = 0, 8, 16
C_ADAB = 24
C_CONVW = 96
C_CONVB = 144
C_DTB, C_ALOG, C_DSK = 156, 172, 188
C_SUBLN = 204
C_LQ1, C_LK1, C_LQ2, C_LK2 = 205, 269, 333, 397
NS = 461
G_FN, G_RB, G_B0, G_B1 = 0, 8, 136, 264
NG = 392


class Res:
    __slots__ = ("name", "w", "r")

    def __init__(self, name):
        self.name = name
        self.w = None
        self.r = {}


class Sched:
    ENG = ("pe", "act", "dve", "pool", "sp")

    def __init__(self, nc):
        self.nc = nc
        self.prog = {k: [] for k in self.ENG}
        self.cnt = {}
        self.seen = {k: {} for k in self.ENG}
        self.semkeys = []
        self.owner = {}

    def _sem(self, key):
        if key not in self.cnt:
            self.cnt[key] = 0
            self.semkeys.append(key)

    def _need(self, eng, dep, kind):
        if dep is None:
            return
        key, val = dep
        if key == "c_" + eng:
            if eng == "pe":
                return
        if self.owner.get(key) == eng:
            val = self.cnt[key]
        if self.seen[eng].get(key, 0) >= val:
            return
        self.seen[eng][key] = val
        self.prog[eng].append(("wait", key, val))

    def op(self, eng, fn, reads=(), writes=(), dma=None):
        for r in reads:
            self._need(eng, r.w, "raw")
        for w in writes:
            self._need(eng, w.w, "waw")
            for k, v in list(w.r.items()):
                self._need(eng, (k, v), "war")
        if dma is None:
            key, inc = "c_" + eng, 1
        else:
            key, inc = dma, 16
        self._sem(key)
        if dma is not None:
            self.owner[key] = eng
        self.cnt[key] += inc
        me = (key, self.cnt[key])
        self.prog[eng].append(("op", fn, key, inc))
        for r in reads:
            if r.r.get(key, 0) < me[1]:
                r.r[key] = me[1]
        for w in writes:
            w.w = me
            w.r = {}
        return me

    def wait_all(self, eng, keys):
        for k in keys:
            if k in self.cnt:
                self._need(eng, (k, self.cnt[k]), "raw")

    def barrier(self, engs=("pe", "act", "dve", "sp")):
        for e in engs:
            for k in list(self.cnt.keys()):
                if k == "c_" + e or k.startswith("d_w"):
                    continue
                v = self.cnt[k]
                if self.seen[e].get(k, 0) < v:
                    self.seen[e][k] = v
                    self.prog[e].append(("wait", k, v))

    def emit(self):
        nc = self.nc
        sems = {k: nc.alloc_semaphore(name="s_" + k) for k in self.semkeys}
        prog = self.prog

        def replay(ek, e):
            for it in prog[ek]:
                if it[0] == "wait":
                    e.wait_ge(sems[it[1]], it[2])
                else:
                    it[1](e).then_inc(sems[it[2]], it[3])

        with nc.Block() as block:
            @block.tensor
            def _(e):
                replay("pe", e)

            @block.scalar
            def _(e):
                replay("act", e)

            @block.vector
            def _(e):
                replay("dve", e)

            @block.gpsimd
            def _(e):
                replay("pool", e)

            @block.sync
            def _(e):
                replay("sp", e)


class Buf:
    __slots__ = ("t", "r")

    def __init__(self, t, r):
        self.t = t
        self.r = r


FPARTS = (6, 6, 5, 5)
FOFF = (0, 6, 12, 17)


class Builder:
    def __init__(self, NT, NPASS, layers=DEPTH, do_ffn1=True, do_ssd=True, do_att=True,
                 do_ffn2=True, final_norm=True):
        self.NT, self.NPASS = NT, NPASS
        self.NTT = NT // TT
        self.NTOK = NT * NPASS
        self.layers = layers
        self.flags = dict(ffn1=do_ffn1, ssd=do_ssd, att=do_att, ffn2=do_ffn2, fn=final_norm)
        self.nc = bass.Bass("TRN2", target_bir_lowering=False)
        self.S = Sched(self.nc)
        self.uid = 0

    def sb(self, st, name, shape, dt=F32):
        self.uid += 1
        nm = "%s_%d" % (name, self.uid)
        t = st.enter_context(self.nc.sbuf_tensor(nm, list(shape), dt))
        return Buf(t, Res(nm))

    def dram_in(self, name, shape, dt=F32):
        return self.nc.dram_tensor(name, list(shape), dt, kind="ExternalInput").ap()

    def dram_scr(self, name, shape, dt):
        return self.nc.dram_tensor(name, list(shape), dt, kind="Internal").ap()

    def mm(self, out, lhsT, rhs, start, stop, reads, writes, skip=False):
        kw = dict(skip_group_check=True) if skip else {}
        self.S.op("pe", lambda e: e.matmul(out, lhsT=lhsT, rhs=rhs, start=start, stop=stop, **kw),
                  reads, writes)

    def tr(self, out, in_, ident, reads, writes):
        self.S.op("pe", lambda e: e.transpose(out=out, in_=in_, identity=ident), reads, writes)

    def act(self, out, in_, func, reads, writes, bias=None, scale=None, accum_out=None):
        kw = {}
        if bias is not None:
            kw["bias"] = bias
        if scale is not None:
            kw["scale"] = scale
        if accum_out is not None:
            kw["accum_out"] = accum_out
        self.S.op("act", lambda e: e.activation(out=out, in_=in_, func=func, **kw), reads, writes)

    def tt_(self, out, in0, in1, op, reads, writes, eng="dve"):
        self.S.op(eng, lambda e: e.tensor_tensor(out=out, in0=in0, in1=in1, op=op), reads, writes)

    def ts(self, out, in0, s1, op0, reads, writes, s2=None, op1=None, eng="dve"):
        if op1 is None:
            self.S.op(eng, lambda e: e.tensor_scalar(out=out, in0=in0, scalar1=s1, scalar2=None, op0=op0),
                      reads, writes)
        else:
            self.S.op(eng, lambda e: e.tensor_scalar(out=out, in0=in0, scalar1=s1, scalar2=s2, op0=op0, op1=op1),
                      reads, writes)

    def stt(self, out, in0, scalar, in1, op0, op1, reads, writes):
        self.S.op("dve", lambda e: e.scalar_tensor_tensor(out=out, in0=in0, scalar=scalar, in1=in1,
                                                          op0=op0, op1=op1), reads, writes)

    def cp(self, out, in_, reads, writes, eng="dve"):
        if eng == "act":
            self.act(out, in_, AF.Identity, reads, writes)
        else:
            self.S.op(eng, lambda e: e.tensor_copy(out=out, in_=in_), reads, writes)

    def memset(self, ap, val, writes, eng="dve"):
        self.S.op(eng, lambda e: e.memset(ap, val), (), writes)

    def dma(self, q, out, in_, sem, reads, writes):
        self.S.op(q, lambda e: e.dma_start(out=out, in_=in_), reads, writes, dma=sem)

    def wload(self, pool, src, shape):
        i = pool["idx"] % len(pool["slots"])
        pool["idx"] += 1
        b = pool["slots"][i]
        n = int(np.prod(shape[1:]))
        flat = b.t[:, 0:n]
        view = flat.rearrange("p (a b) -> p a b", a=shape[1])
        self.dma("pool", view, src, "d_w%s%d" % (pool["name"], i), (), [b.r])
        return view, b.r

    def build(self):
        nc, S = self.nc, self.S
        NT, NPASS, NTT, NTOK = self.NT, self.NPASS, self.NTT, self.NTOK
        L = DEPTH
        FL = self.flags
        xin = self.dram_in("xin", [NTOK, D])
        cT_d = self.dram_in("cT", [128, 8])
        adaw = self.dram_in("adaw", [L, 18, 128, 8, 512])
        w13 = [self.dram_in("w13_%d" % f, [L, NJ, 128, 8, 256]) for f in (1, 2)]
        w2 = [self.dram_in("w2_%d" % f, [L, 4, 8, 128, 6, 128]) for f in (1, 2)]
        wz = self.dram_in("wz", [L, 2, 128, 8, 512])
        wx = self.dram_in("wx", [L, 12, 128, 8, 128])
        wdt = self.dram_in("wdt", [L, 128, 8, 16])
        wqk = self.dram_in("wqk", [L, 16, 128, 8, 128])
        wv = self.dram_in("wv", [L, 2, 128, 8, 512])
        wo = self.dram_in("wo", [L, 2, 2, 128, 8, 512])
        small_d = self.dram_in("small", [L, 128, NS])
        ssmn_d = self.dram_in("ssmn", [L, 128, 1024])
        gsmall_d = self.dram_in("gsmall", [128, NG])
        out_d = nc.dram_tensor("out", [NTOK, D], F32, kind="ExternalOutput").ap()
        ksb = self.dram_scr("ksb", [L, 4, 128, NTOK], BF16)
        kdf = self.dram_scr("kdf", [L, 4, 128, NTOK], BF16)
        vsb = self.dram_scr("vsb", [L, NTOK, 1024], BF16)
        vdf = self.dram_scr("vdf", [L, NTOK, 512], BF16)
        qsb = self.dram_scr("qsb", [4, 128, NT], BF16)
        qdf = self.dram_scr("qdf", [4, 128, NT], BF16)
        sst = self.dram_scr("sst", [L, 128, 1024], F32)
        ctl = self.dram_scr("ctl", [L, 128, 36], F32)
        R_kv = [Res("kv%d" % l) for l in range(L)]
        R_q = Res("qscr")
        R_st = [Res("sst%d" % l) for l in range(L)]
        R_out = Res("out")

        with ExitStack() as top:
            PS = top.enter_context(nc.psum_tensor("ps", [128, 8, 512], F32))
            RB = [Res("bank%d" % i) for i in range(8)]

            def bank(i):
                return PS[:, i, :]

            def bank2(i):
                return PS[:, i:i + 2, :].rearrange("p a b -> p (a b)")

            def bankb(i):
                return PS[:, i, :].bitcast(BF16)

            xT = self.sb(top, "xT", [128, KD, NT])
            RX = [[Res("xT_%d_%d" % (c, t)) for t in range(NTT)] for c in range(KD)]
            identb = self.sb(top, "identb", [128, 128], BF16)
            identf = self.sb(top, "identf", [128, 128])
            nidentb = self.sb(top, "nidentb", [128, 128], BF16)
            negm = self.sb(top, "negm", [128, 128], BF16)
            triN = self.sb(top, "triN", [128, 128], BF16)
            negones = self.sb(top, "negones", [128, 128], BF16)
            onesb = self.sb(top, "onesb", [128, 128], BF16)
            ones_d = self.sb(top, "ones_d", [128, 128], BF16)
            ones_v = self.sb(top, "ones_v", [128, 128], BF16)
            U32 = self.sb(top, "U32", [128, 128])
            onesf = self.sb(top, "onesf", [128, 128])
            m_lt = self.sb(top, "m_lt", [128, 128])
            D0 = self.sb(top, "D0", [128, 4, 128])
            D1 = self.sb(top, "D1", [128, 4, 128])
            small = [self.sb(top, "small%d" % l, [128, NS]) for l in range(L)]
            gsm = self.sb(top, "gsm", [128, NG])
            mv = [self.sb(top, "mv%d" % l, [128, 9, 8]) for l in range(L)]
            abc = [self.sb(top, "abc%d" % l, [128, 16]) for l in range(L)]
            neglam = [self.sb(top, "neglam%d" % l, [128, 1]) for l in range(L)]
            subg = [self.sb(top, "subg%d" % l, [128, 1]) for l in range(L)]
            epsc = self.sb(top, "epsc", [128, 1])
            WA = dict(name="a", idx=0, slots=[self.sb(top, "wa%d" % i, [128, 4096], BF16) for i in range(3)])
            WB = dict(name="b", idx=0, slots=[self.sb(top, "wb%d" % i, [128, 6 * 128], BF16) for i in range(3)])

            def sel(h):
                return identb.t[0:16, h:h + 1].to_broadcast([16, 128])

            def nsel(h):
                return nidentb.t[0:16, h:h + 1].to_broadcast([16, 128])

            def aff(buf, init, pattern, cmp_op, fill, base, cm):
                ap = buf.t[:]
                self.memset(ap, init, [buf.r], eng="pool")
                S.op("pool", lambda e: e.affine_select(out=ap, in_=ap, pattern=pattern, compare_op=cmp_op,
                                                       fill=fill, base=base, channel_multiplier=cm),
                     [buf.r], [buf.r])

            with ExitStack() as pst:
                aff(identf, 0.0, [[-1, 128]], ALU.not_equal, 1.0, 0, 1)
                self.cp(identb.t[:], identf.t[:], [identf.r], [identb.r])
                self.ts(nidentb.t[:], identf.t[:], -1.0, ALU.mult, [identf.r], [nidentb.r])
                tmpf = self.sb(pst, "tmpf", [128, 128])
                aff(tmpf, -1.0, [[-1, 128]], ALU.is_ge, 0.0, 0, 1)
                self.cp(triN.t[:], tmpf.t[:], [tmpf.r], [triN.r])
                tmpn = self.sb(pst, "tmpn", [128, 128])
                aff(tmpn, 0.0, [[1, 128]], ALU.is_ge, NEG, 0, -1)
                self.cp(negm.t[:], tmpn.t[:], [tmpn.r], [negm.r])
                self.memset(negones.t[:], -1.0, [negones.r])
                self.memset(onesb.t[:], 1.0, [onesb.r])
                self.memset(ones_d.t[:], 1.0 / 1024.0, [ones_d.r])
                self.memset(ones_v.t[:], 1.0 / 128.0, [ones_v.r])
                self.memset(onesf.t[:], 1.0, [onesf.r])
                self.memset(epsc.t[:], EPS, [epsc.r])
                aff(U32, 1.0, [[1, 128]], ALU.is_ge, 0.0, 0, -1)
                aff(m_lt, 1.0, [[1, 128]], ALU.is_ge, 0.0, -1, -1)

                for l in range(L):
                    self.dma("sp", small[l].t[:], small_d[l], "d_c%d" % l, (), [small[l].r])
                self.dma("sp", gsm.t[:], gsmall_d, "d_cg", (), [gsm.r])
                cT = self.sb(pst, "cT", [128, 8])
                self.dma("sp", cT.t[:], cT_d, "d_cc", (), [cT.r])
                scT = self.sb(pst, "scT", [128, 8], BF16)
                self.act(scT.t[:], cT.t[:], AF.Silu, [cT.r], [scT.r])

                rbd = self.sb(pst, "rbd", [128, 32, 4])
                rbv = gsm.t[:, G_RB:G_RB + 128].rearrange("p (b h) -> p b h", h=4)
                self.tt_(rbd.t[:], rbv, gsm.t[:, G_RB + 124:G_RB + 128].unsqueeze(1).to_broadcast([128, 32, 4]),
                         ALU.subtract, [gsm.r], [rbd.r])
                tmpd = self.sb(pst, "tmpd", [128, 128])
                for (Dt, gcol) in ((D0, G_B0), (D1, G_B1)):
                    bidx = gsm.t[:, gcol:gcol + 128]
                    for h in range(4):
                        dst = Dt.t[:, h, :]
                        self.ts(dst, bidx, 32.0, ALU.is_equal, [gsm.r], [Dt.r], s2=NEG, op1=ALU.mult)
                        for b in range(31):
                            self.ts(tmpd.t[:], bidx, float(b), ALU.is_equal, [gsm.r, rbd.r], [tmpd.r],
                                    s2=rbd.t[:, b, h:h + 1], op1=ALU.mult)
                            self.tt_(dst, dst, tmpd.t[:], ALU.add, [tmpd.r, Dt.r], [Dt.r])

                for l in range(L):
                    modps = bank(0)
                    for blk in range(18):
                        wv_, wr = self.wload(WA, adaw[l, blk], [128, 8, 512])
                        for oc in range(4):
                            j = blk * 4 + oc
                            for kc in range(KD):
                                self.mm(modps[:, j:j + 1], wv_[:, kc, oc * 128:(oc + 1) * 128], scT.t[:, kc:kc + 1],
                                        kc == 0, kc == KD - 1, [wr, scT.r], [RB[0]])
                    modT = self.sb(pst, "modT%d" % l, [128, 9, 8])
                    self.tt_(modT.t[:].rearrange("p a b -> p (a b)"), modps[:, 0:72],
                             small[l].t[:, C_ADAB:C_ADAB + 72], ALU.add, [RB[0], small[l].r], [modT.r])
                    for s, cn in enumerate((C_N1, C_N2, C_N3)):
                        self.stt(mv[l].t[:, 3 * s, :], modT.t[:, 3 * s + 1, :], 1.0, small[l].t[:, cn:cn + 8],
                                 ALU.add, ALU.mult, [modT.r, small[l].r], [mv[l].r])
                        self.cp(mv[l].t[:, 3 * s + 1, :], modT.t[:, 3 * s, :], [modT.r], [mv[l].r])
                        self.ts(mv[l].t[:, 3 * s + 2, :], modT.t[:, 3 * s + 2, :], 1.0 if s == 1 else 0.5, ALU.mult,
                                [modT.r], [mv[l].r])
                    self.act(abc[l].t[:], small[l].t[:, C_ALOG:C_ALOG + 16], AF.Exp, [small[l].r], [abc[l].r])
                    self.ts(abc[l].t[:], abc[l].t[:], -1.0, ALU.mult, [abc[l].r], [abc[l].r])
                    lam_init = 0.8 - 0.6 * math.exp(-0.3 * l)
                    lt = self.sb(pst, "lt%d" % l, [128, 64])
                    l2 = self.sb(pst, "l2_%d" % l, [128, 2])
                    for i, (ca, cb) in enumerate(((C_LQ1, C_LK1), (C_LQ2, C_LK2))):
                        self.tt_(lt.t[:], small[l].t[:, ca:ca + 64], small[l].t[:, cb:cb + 64], ALU.mult,
                                 [small[l].r], [lt.r])
                        S.op("dve", (lambda o, i_: (lambda e: e.tensor_reduce(out=o, in_=i_, axis=AX.X, op=ALU.add)))(
                            l2.t[:, i:i + 1], lt.t[:]), [lt.r], [l2.r])
                    self.act(l2.t[:], l2.t[:], AF.Exp, [l2.r], [l2.r])
                    self.tt_(neglam[l].t[:], l2.t[:, 1:2], l2.t[:, 0:1], ALU.subtract, [l2.r], [neglam[l].r])
                    self.ts(neglam[l].t[:], neglam[l].t[:], -lam_init, ALU.add, [neglam[l].r], [neglam[l].r])
                    self.ts(subg[l].t[:], small[l].t[:, C_SUBLN:C_SUBLN + 1], 1.0 - lam_init, ALU.mult,
                            [small[l].r], [subg[l].r])
            S.barrier()

            def rms_tile(scr, tt, gain_fn, shift_fn, dst_fn, dst_res, pbank):
                sl = slice(tt * TT, (tt + 1) * TT)
                sq = scr[0:2]
                rstd_ap, rstd_r = scr[2]
                tmp = scr[3:5]
                for c in range(KD):
                    q_ap, q_r = sq[c % 2]
                    self.tt_(q_ap, xT.t[:, c, sl], xT.t[:, c, sl], ALU.mult, [RX[c][tt]], [q_r])
                    self.mm(bank(pbank), ones_d.t[:], q_ap, c == 0, c == KD - 1, [ones_d.r, q_r], [RB[pbank]])
                self.act(rstd_ap, bank(pbank), AF.Ln, [RB[pbank], epsc.r], [rstd_r], bias=epsc.t[:], scale=1.0)
                self.act(rstd_ap, rstd_ap, AF.Exp, [rstd_r], [rstd_r], scale=-0.5)
                for c in range(KD):
                    t_ap, t_r = tmp[c % 2]
                    self.tt_(t_ap, xT.t[:, c, sl], rstd_ap, ALU.mult, [RX[c][tt], rstd_r], [t_r])
                    self.act(dst_fn(c), t_ap, AF.Identity, [t_r], [dst_res], bias=shift_fn(c), scale=gain_fn(c))

            def rms_all(sq, rstds, tmp, gain_fn, shift_fn, dst_fn, dst_res_fn):
                for tt in range(NTT):
                    sl = slice(tt * TT, (tt + 1) * TT)
                    pb = 4 + (tt % 4)
                    for c in range(KD):
                        q = sq[c % 2]
                        self.tt_(q.t[:], xT.t[:, c, sl], xT.t[:, c, sl], ALU.mult, [RX[c][tt]], [q.r])
                        self.mm(bank(pb), ones_d.t[:], q.t[:], c == 0, c == KD - 1, [ones_d.r, q.r], [RB[pb]])
                for tt in range(NTT):
                    pb = 4 + (tt % 4)
                    r_ = rstds[tt]
                    self.act(r_.t[:], bank(pb), AF.Ln, [RB[pb], epsc.r], [r_.r], bias=epsc.t[:], scale=1.0)
                    self.act(r_.t[:], r_.t[:], AF.Exp, [r_.r], [r_.r], scale=-0.5)
                k = 0
                for tt in range(NTT):
                    sl = slice(tt * TT, (tt + 1) * TT)
                    r_ = rstds[tt]
                    for c in range(KD):
                        t_ = tmp[k % len(tmp)]
                        k += 1
                        self.tt_(t_.t[:], xT.t[:, c, sl], r_.t[:], ALU.mult, [RX[c][tt], r_.r], [t_.r])
                        self.act(dst_fn(tt, c), t_.t[:], AF.Identity, [t_.r], [dst_res_fn(tt, c)], bias=shift_fn(c), scale=gain_fn(c))

            def resid_update(tt, m, ps_ap, ps_res, gate_ap, extra_reads=()):
                sl = slice(tt * TT, (tt + 1) * TT)
                self.stt(xT.t[:, m, sl], ps_ap, gate_ap, xT.t[:, m, sl], ALU.mult, ALU.add,
                         [ps_res, RX[m][tt]] + list(extra_reads), [RX[m][tt]])

            def ffn(l, f):
                s = 0 if f == 0 else 2
                with ExitStack() as st:
                    hT = self.sb(st, "hT", [128, KD, NT], BF16)
                    Rh = [Res("hT%d" % t) for t in range(NTT)]
                    gT = self.sb(st, "gT", [128, 6, NT], BF16)
                    Rg = [[Res("gT%d_%d" % (j, t)) for t in range(NTT)] for j in range(6)]
                    sil = [self.sb(st, "sil%d" % i, [128, TT]) for i in range(2)]
                    sqb = [self.sb(st, "sqb%d" % i, [128, TT], BF16) for i in range(2)]
                    rstds = [self.sb(st, "rstd%d" % i, [128, TT]) for i in range(NTT)]
                    ntm = [self.sb(st, "ntm%d" % i, [128, TT]) for i in range(3)]
                    rms_all(sqb, rstds, ntm, lambda c: mv[l].t[:, 3 * s, c:c + 1], lambda c: mv[l].t[:, 3 * s + 1, c:c + 1],
                            lambda tt, c: hT.t[:, c, tt * TT:(tt + 1) * TT], lambda tt, c: Rh[tt])
                    k = 0
                    for H in range(4):
                        nj = FPARTS[H]
                        for jj in range(nj):
                            j = FOFF[H] + jj
                            wv_, wr = self.wload(WA, w13[f][l, j], [128, 8, 256])
                            for tt in range(NTT):
                                sl = slice(tt * TT, (tt + 1) * TT)
                                ba, bu = (k % 2) * 2, (k % 2) * 2 + 1
                                k += 1
                                for kc in range(KD):
                                    self.mm(bank(ba), wv_[:, kc, 0:128], hT.t[:, kc, sl], kc == 0, kc == KD - 1,
                                            [wr, Rh[tt]], [RB[ba]])
                                for kc in range(KD):
                                    self.mm(bank(bu), wv_[:, kc, 128:256], hT.t[:, kc, sl], kc == 0, kc == KD - 1,
                                            [wr, Rh[tt]], [RB[bu]])
                                sb_ = sil[k % 2]
                                self.act(sb_.t[:], bank(ba), AF.Silu, [RB[ba]], [sb_.r])
                                self.tt_(gT.t[:, jj, sl], sb_.t[:], bank(bu), ALU.mult, [sb_.r, RB[bu]], [Rg[jj][tt]])
                        for m in range(KD):
                            wv_, wr = self.wload(WB, w2[f][l, H, m, :, 0:nj, :], [128, nj, 128])
                            for tt in range(NTT):
                                sl = slice(tt * TT, (tt + 1) * TT)
                                bo = 4 + (k % 4)
                                k += 1
                                for jj in range(nj):
                                    self.mm(bank(bo), wv_[:, jj, :], gT.t[:, jj, sl], jj == 0, jj == nj - 1,
                                            [wr, Rg[jj][tt]], [RB[bo]])
                                resid_update(tt, m, bank(bo), RB[bo], mv[l].t[:, 3 * s + 2, m:m + 1], [mv[l].r])
                S.barrier()

            def mixer(l, p):
                g0 = p * NT
                sm = small[l]
                with ExitStack() as st:
                    hbs = [self.sb(st, "hTt%d" % i, [128, KD, TT], BF16) for i in range(2 if NTT % 2 == 0 else 1)]
                    stage = [self.sb(st, "stage%d" % i, [128, 3 + TT]) for i in range(2)]
                    stage_c = [Res("stage_c0"), Res("stage_c1")]
                    carry = self.sb(st, "carry", [128, 12, 3])
                    xact = [self.sb(st, "xact%d" % i, [128, 12, TT], BF16) for i in range(2 if NTT % 2 == 0 else 1)]
                    RxaL = [[Res("xact%d_%d" % (i, c)) for c in range(12)] for i in range(2)]
                    qst = [self.sb(st, "qst%d" % i, [128, TT], BF16) for i in range(4)]
                    vpad = [self.sb(st, "vpad%d" % i, [128, 1024], BF16) for i in range(1)]
                    vst = [self.sb(st, "vst%d" % i, [128, 512], BF16) for i in range(1)]
                    szc4 = self.sb(st, "szc4", [128, 4, 1024], BF16)
                    x_tm = self.sb(st, "x_tm", [128, 1024], BF16)
                    B_tm = self.sb(st, "B_tm", [128, 256], BF16)
                    dtb = self.sb(st, "dtb", [128, 8, 16])
                    dt_ = self.sb(st, "dt", [128, 8, 16])
                    da = self.sb(st, "da", [128, 8, 16])
                    acum = self.sb(st, "acum", [128, 16])
                    ea = self.sb(st, "ea", [128, 16])
                    cd = self.sb(st, "cd", [128, 16])
                    dte = self.sb(st, "dte", [128, 16])
                    w2s = self.sb(st, "w2s", [128, 16])
                    afh = self.sb(st, "afh", [16, 128], BF16)
                    afl = self.sb(st, "afl", [16, 128], BF16)
                    dec = [self.sb(st, "dec%d" % i, [128, 4, 128], BF16) for i in range(2)]
                    CBm = self.sb(st, "CBm", [128, 2, 128])
                    MT = [self.sb(st, "MT%d" % i, [128, 4, 128], BF16) for i in range(2)]
                    xd = self.sb(st, "xd", [128, 1024], BF16)
                    xds = self.sb(st, "xds", [128, 1024], BF16)
                    t1 = self.sb(st, "t1", [128, 1024])
                    t1h = [Res("t1a"), Res("t1b")]
                    t2 = self.sb(st, "t2", [128, 1024])
                    ss = self.sb(st, "ss", [128, 2])
                    rs = self.sb(st, "rs", [128, 2])
                    yn = self.sb(st, "yn", [128, 1024], BF16)
                    yTs = self.sb(st, "yTs", [128, 8, TT], BF16)
                    prevf = self.sb(st, "prevf", [128, 1024])
                    prevb = self.sb(st, "prevb", [128, 1024], BF16)
                    ssmn = self.sb(st, "ssmn", [128, 1024])
                    scr = [(yn.t[:, 0:512], yn.r), (yn.t[:, 512:1024], yn.r), (t2.t[:, 0:512], t2.r),
                           (t1.t[:, 0:512], t1h[0]), (t1.t[:, 512:1024], t1h[1])]

                    if FL["ssd"]:
                        self.dma("sp", ssmn.t[:], ssmn_d[l], "d_sn", (), [ssmn.r])
                        if p == 0:
                            self.memset(prevf.t[:], 0.0, [prevf.r])
                            self.memset(carry.t[:], 0.0, [carry.r])
                        else:
                            self.dma("sp", prevf.t[:], sst[l], "d_st0", [R_st[l]], [prevf.r])
                            self.dma("sp", carry.t[:].rearrange("p a b -> p (a b)"), ctl[l], "d_st1", [R_st[l]], [carry.r])
                        self.cp(prevb.t[:], prevf.t[:], [prevf.r], [prevb.r], eng="act")
                    self.memset(vpad[0].t[:], 0.0, [vpad[0].r])

                    TP = 2 if NTT % 2 == 0 else 1
                    for gp in range(NTT // TP):
                        tiles = [gp * TP + ti for ti in range(TP)]
                        for ti, tt in enumerate(tiles):
                            sl = slice(tt * TT, (tt + 1) * TT)
                            for c in range(KD):
                                q_ap = yn.t[:, (c % 2) * 512:(c % 2 + 1) * 512]
                                self.tt_(q_ap, xT.t[:, c, sl], xT.t[:, c, sl], ALU.mult, [RX[c][tt]], [yn.r])
                                self.mm(bank(6 + ti), ones_d.t[:], q_ap, c == 0, c == KD - 1, [ones_d.r, yn.r], [RB[6 + ti]])
                        for ti, tt in enumerate(tiles):
                            r_ap = t2.t[:, ti * 512:(ti + 1) * 512]
                            self.act(r_ap, bank(6 + ti), AF.Ln, [RB[6 + ti], epsc.r], [t2.r], bias=epsc.t[:], scale=1.0)
                            self.act(r_ap, r_ap, AF.Exp, [t2.r], [t2.r], scale=-0.5)
                        for ti, tt in enumerate(tiles):
                            sl = slice(tt * TT, (tt + 1) * TT)
                            r_ap = t2.t[:, ti * 512:(ti + 1) * 512]
                            for c in range(KD):
                                t_ap = t1.t[:, (c % 2) * 512:(c % 2 + 1) * 512]
                                self.tt_(t_ap, xT.t[:, c, sl], r_ap, ALU.mult, [RX[c][tt], t2.r], [t1h[c % 2]])
                                self.act(hbs[ti].t[:, c, :], t_ap, AF.Identity, [t1h[c % 2]], [hbs[ti].r],
                                         bias=mv[l].t[:, 4, c:c + 1], scale=mv[l].t[:, 3, c:c + 1])
                        kb = 0
                        vpad2 = [(vpad[0].t[:], [vpad[0].r]), (stage[0].t[:, 0:512].bitcast(BF16), [stage[0].r, stage_c[0]])]
                        vst2 = [(vst[0].t[:], [vst[0].r]), (stage[1].t[:, 0:256].bitcast(BF16), [stage[1].r, stage_c[1]])]
                        if FL["att"]:
                            self.memset(vpad2[1][0], 0.0, vpad2[1][1])
                            for cc in range(16):
                                wv_, wr = self.wload(WA, wqk[l, cc], [128, 8, 128])
                                for ti, tt in enumerate(tiles):
                                    sl = slice(tt * TT, (tt + 1) * TT)
                                    gt0 = g0 + tt * TT
                                    hb = hbs[ti]
                                    b = kb % 4
                                    kb += 1
                                    for kc in range(KD):
                                        self.mm(bank(b), wv_[:, kc, :], hb.t[:, kc, :], kc == 0, kc == KD - 1,
                                                [wr, hb.r], [RB[b]])
                                    qs_ = qst[kb % 4]
                                    isq = cc < 4 or 8 <= cc < 12
                                    if kb % 2 == 0:
                                        self.act(qs_.t[:], bank(b), AF.Identity, [RB[b]], [qs_.r], scale=0.125 if isq else 1.0)
                                    else:
                                        self.ts(qs_.t[:], bank(b), 0.125 if isq else 1.0, ALU.mult, [RB[b]], [qs_.r])
                                    if cc < 4:
                                        self.dma("sp", qsb[cc, :, sl], qs_.t[:], "d_sq%d" % (kb % 4), [qs_.r], [R_q])
                                    elif cc < 8:
                                        self.dma("sp", ksb[l, cc - 4, :, gt0:gt0 + TT], qs_.t[:], "d_sq%d" % (kb % 4), [qs_.r], [R_kv[l]])
                                    elif cc < 12:
                                        self.dma("sp", qdf[cc - 8, :, sl], qs_.t[:], "d_sq%d" % (kb % 4), [qs_.r], [R_q])
                                    else:
                                        self.dma("sp", kdf[l, cc - 12, :, gt0:gt0 + TT], qs_.t[:], "d_sq%d" % (kb % 4), [qs_.r], [R_kv[l]])
                            for wi in range(2):
                                wv_, wr = self.wload(WA, wv[l, wi], [128, 8, 512])
                                for ti, tt in enumerate(tiles):
                                    gt0 = g0 + tt * TT
                                    hb = hbs[ti]
                                    for t4 in range(4):
                                        b = kb % 4
                                        kb += 1
                                        for kc in range(KD):
                                            self.mm(bank(b), hb.t[:, kc, t4 * 128:(t4 + 1) * 128], wv_[:, kc, :],
                                                    kc == 0, kc == KD - 1, [wr, hb.r], [RB[b]])
                                        r0 = gt0 + t4 * 128
                                        if wi == 0:
                                            vp_ap, vp_r = vpad2[t4 % 2]
                                            src = bank(b).rearrange("p (h e d) -> p h e d", h=4, e=2)
                                            dst = vp_ap.rearrange("p (h e x d) -> p h e x d", h=4, e=2, x=2)
                                            self.cp(dst[:, :, 0, 0, :], src[:, :, 0, :], [RB[b]], vp_r, eng="act")
                                            self.cp(dst[:, :, 1, 1, :], src[:, :, 1, :], [RB[b]], vp_r, eng="dve")
                                            self.dma("sp", vsb[l, r0:r0 + 128, :], vp_ap, "d_svp%d" % (t4 % 2), vp_r, [R_kv[l]])
                                        else:
                                            vs_ap, vs_r = vst2[t4 % 2]
                                            self.cp(vs_ap, bank(b), [RB[b]], vs_r, eng="act")
                                            self.dma("sp", vdf[l, r0:r0 + 128, :], vs_ap, "d_svs%d" % (t4 % 2), vs_r, [R_kv[l]])
                        if not FL["ssd"]:
                            continue
                        xun = [(c, ti) for c in range(12) for ti in range(TP)]
                        xb_bank = {}
                        xw_ = {}

                        def xP(u):
                            c, ti = xun[u]
                            if ti == 0:
                                xw_[c] = self.wload(WA, wx[l, c], [128, 8, 128])
                            wv_, wr = xw_[c]
                            hb = hbs[ti]
                            b = (kb + u) % 4
                            xb_bank[u] = b
                            for kc in range(KD):
                                self.mm(bank(b), wv_[:, kc, :], hb.t[:, kc, :], kc == 0, kc == KD - 1, [wr, hb.r], [RB[b]])

                        def xE(u):
                            c, ti = xun[u]
                            b = xb_bank[u]
                            sg = stage[u % 2]
                            self.cp(sg.t[:, 3:3 + TT], bank(b), [RB[b]], [sg.r], eng="act")
                            self.cp(sg.t[:, 0:3], carry.t[:, c, :], [carry.r], [stage_c[u % 2]])
                            self.cp(carry.t[:, c, :], sg.t[:, TT:TT + 3], [sg.r], [carry.r])

                        def xV(u):
                            c, ti = xun[u]
                            sg = stage[u % 2]
                            ct_ap = t1.t[:, (u % 2) * 512:(u % 2 + 1) * 512]
                            cw = C_CONVW + c * 4
                            self.ts(ct_ap, sg.t[:, 0:TT], sm.t[:, cw:cw + 1], ALU.mult, [sg.r, stage_c[u % 2], sm.r], [t1h[u % 2]])
                            for kk in range(1, 4):
                                self.stt(ct_ap, sg.t[:, kk:kk + TT], sm.t[:, cw + kk:cw + kk + 1], ct_ap,
                                         ALU.mult, ALU.add, [sg.r, stage_c[u % 2], t1h[u % 2], sm.r], [t1h[u % 2]])

                        def xS(u):
                            c, ti = xun[u]
                            ct_ap = t1.t[:, (u % 2) * 512:(u % 2 + 1) * 512]
                            self.act(xact[ti].t[:, c, :], ct_ap, AF.Silu, [t1h[u % 2], sm.r], [RxaL[ti][c]],
                                     bias=sm.t[:, C_CONVB + c:C_CONVB + c + 1], scale=1.0)

                        for k_ in range(len(xun) + 2):
                            for stg, sk in ((xP, 0), (xE, 0), (xV, 1), (xS, 2)):
                                u_ = k_ - sk
                                if 0 <= u_ < len(xun):
                                    stg(u_)
                        kb += len(xun)
                        wv_, wr = self.wload(WA, wdt[l], [128, 8, 16])
                        nch = 4 * TP
                        for ti in range(TP):
                            hb = hbs[ti]
                            for t4 in range(4):
                                t4g = ti * 4 + t4
                                for kc in range(KD):
                                    self.mm(bank(0)[:, t4g * 16:(t4g + 1) * 16], hb.t[:, kc, t4 * 128:(t4 + 1) * 128],
                                            wv_[:, kc, :], kc == 0, kc == KD - 1, [wr, hb.r], [RB[0]])
                        self.tt_(dtb.t[:, 0:nch, :], bank(0)[:, 0:16 * nch].rearrange("p (a b) -> p a b", a=nch),
                                 sm.t[:, C_DTB:C_DTB + 16].unsqueeze(1).to_broadcast([128, nch, 16]), ALU.add,
                                 [RB[0], sm.r], [dtb.r])
                        self.act(dtb.t[:, 0:nch, :], dtb.t[:, 0:nch, :], AF.Exp, [dtb.r], [dtb.r])
                        self.act(dt_.t[:, 0:nch, :], dtb.t[:, 0:nch, :], AF.Ln, [dtb.r], [dt_.r], bias=1.0, scale=1.0)
                        self.tt_(da.t[:, 0:nch, :], dt_.t[:, 0:nch, :], abc[l].t[:].unsqueeze(1).to_broadcast([128, nch, 16]), ALU.mult,
                                 [dt_.r, abc[l].r], [da.r])
                        for ti, tt in enumerate(tiles):
                            sl = slice(tt * TT, (tt + 1) * TT)
                            hb = hbs[ti]
                            xa_ = xact[ti]
                            Rxa_ = RxaL[ti]
                            wz_ = [self.wload(WA, wz[l, hf], [128, 8, 512]) for hf in range(2)]
                            for t4 in range(4):
                                for hf in range(2):
                                    wzv, wzr = wz_[hf]
                                    bz_ = 4 + ((t4 * 2 + hf) % 4)
                                    for kc in range(KD):
                                        self.mm(bank(bz_), hb.t[:, kc, t4 * 128:(t4 + 1) * 128], wzv[:, kc, :], kc == 0, kc == KD - 1,
                                                [wzr, hb.r], [RB[bz_]])
                                    self.act(szc4.t[:, t4, hf * 512:(hf + 1) * 512], bank(bz_), AF.Silu, [RB[bz_]], [szc4.r])
                            for t4 in range(4):
                                tsl = slice(t4 * 128, (t4 + 1) * 128)
                                t4g = ti * 4 + t4
                                pb = bankb(3)
                                for c in range(8):
                                    self.tr(pb[:, c * 128:(c + 1) * 128], xa_.t[:, c, tsl], identb.t[:], [Rxa_[c], identb.r], [RB[3]])
                                self.cp(x_tm.t[:, 0:512], pb[:, 0:512], [RB[3]], [x_tm.r], eng="act")
                                self.cp(x_tm.t[:, 512:1024], pb[:, 512:1024], [RB[3]], [x_tm.r], eng="dve")
                                for c in range(2):
                                    self.tr(pb[:, c * 128:(c + 1) * 128], xa_.t[:, 8 + c, tsl], identb.t[:], [Rxa_[8 + c], identb.r], [RB[3]])
                                self.cp(B_tm.t[:], pb[:, 0:256], [RB[3]], [B_tm.r], eng="act")
                                pA = bank(0)
                                for g in range(2):
                                    self.mm(pA[:, 256 + g * 128:256 + (g + 1) * 128], xa_.t[:, 8 + g, tsl], xa_.t[:, 10 + g, tsl],
                                            True, True, [Rxa_[8 + g], Rxa_[10 + g]], [RB[0]])
                                self.tt_(CBm.t[:], pA[:, 256:512].rearrange("p (g l) -> p g l", g=2),
                                         U32.t[:].unsqueeze(1).to_broadcast([128, 2, 128]), ALU.mult,
                                         [RB[0], U32.r], [CBm.r])
                                pA = bank(0)
                                self.mm(pA[:, 64:80], U32.t[:], da.t[:, t4g, :], True, True, [U32.r, da.r], [RB[0]])
                                self.mm(pA[:, 80:96], onesf.t[:], da.t[:, t4g, :], True, True, [onesf.r, da.r], [RB[0]])
                                self.mm(pA[0:16, 128:256], da.t[:, t4g, :], U32.t[:], True, True, [U32.r, da.r], [RB[0]])
                                self.cp(acum.t[:], pA[:, 64:80], [RB[0]], [acum.r])
                                self.act(ea.t[:], pA[:, 64:80], AF.Exp, [RB[0]], [ea.r])
                                self.act(cd.t[:], pA[:, 80:96], AF.Exp, [RB[0]], [cd.r])
                                self.tt_(dte.t[:], pA[:, 80:96], acum.t[:], ALU.subtract, [RB[0], acum.r], [dte.r])
                                self.act(dte.t[:], dte.t[:], AF.Exp, [dte.r], [dte.r])
                                self.cp(afh.t[:], pA[0:16, 128:256], [RB[0]], [afh.r])
                                self.tt_(afl.t[:], pA[0:16, 128:256], afh.t[:], ALU.subtract, [RB[0], afh.r], [afl.r])
                                xv = x_tm.t[:].rearrange("p (h d) -> p h d", h=16)
                                self.tt_(xd.t[:].rearrange("p (h d) -> p h d", h=16), xv,
                                         dt_.t[:, t4g, :].unsqueeze(2).to_broadcast([128, 16, 64]), ALU.mult,
                                         [x_tm.r, dt_.r], [xd.r])
                                self.tt_(w2s.t[:], dt_.t[:, t4g, :], dte.t[:], ALU.mult, [dt_.r, dte.r], [w2s.r])
                                self.tt_(xds.t[:].rearrange("p (h d) -> p h d", h=16), xv,
                                         w2s.t[:].unsqueeze(2).to_broadcast([128, 16, 64]), ALU.mult,
                                         [x_tm.r, w2s.r], [xds.r])
                                pY = bank2(4)

                                def r1_(rd):
                                    b = 1 + (rd % 2)
                                    pS = bank(b)
                                    for hh in range(4):
                                        h = rd * 4 + hh
                                        o_ = pS[:, hh * 128:(hh + 1) * 128]
                                        self.mm(o_, sel(h), afh.t[:], True, False, [afh.r, identb.r], [RB[b]])
                                        self.mm(o_, sel(h), afl.t[:], False, False, [afl.r, identb.r], [RB[b]])
                                        self.mm(o_, afh.t[:], nsel(h), False, False, [afh.r, nidentb.r], [RB[b]])
                                        self.mm(o_, afl.t[:], nsel(h), False, False, [afl.r, nidentb.r], [RB[b]])
                                        self.mm(o_, identb.t[:], negm.t[:], False, True, [identb.r, negm.r], [RB[b]])

                                def r2_(rd):
                                    b = 1 + (rd % 2)
                                    dc_, mt_ = dec[rd % 2], MT[rd % 2]
                                    self.act(dc_.t[:].rearrange("p a b -> p (a b)"), bank(b), AF.Exp, [RB[b]], [dc_.r])
                                    g = rd // 2
                                    self.stt(mt_.t[:], dc_.t[:], 1.0, CBm.t[:, g, :].unsqueeze(1).to_broadcast([128, 4, 128]),
                                             ALU.min, ALU.mult, [dc_.r, CBm.r], [mt_.r])

                                def r3_(rd):
                                    mt_ = MT[rd % 2]
                                    for hh in range(4):
                                        h = rd * 4 + hh
                                        self.mm(pY[:, h * 64:(h + 1) * 64], mt_.t[:, hh, :], xd.t[:, h * 64:(h + 1) * 64],
                                                True, True, [mt_.r, xd.r], [RB[4 + h // 8]])

                                for k_ in range(4 + 2):
                                    for stg, sk in ((r1_, 0), (r2_, 1), (r3_, 2)):
                                        u_ = k_ - sk
                                        if 0 <= u_ < 4:
                                            stg(u_)
                                pO = bank2(6)
                                for g in range(2):
                                    self.mm(pO[:, g * 512:(g + 1) * 512], xa_.t[:, 10 + g, tsl], prevb.t[:, g * 512:(g + 1) * 512],
                                            True, True, [Rxa_[10 + g], prevb.r], [RB[6 + g]])
                                pSt = bank2(1)
                                for g in range(2):
                                    self.mm(pSt[:, g * 512:(g + 1) * 512], B_tm.t[:, g * 128:(g + 1) * 128],
                                            xds.t[:, g * 512:(g + 1) * 512], True, True, [B_tm.r, xds.r], [RB[1 + g]])
                                self.tt_(t1.t[:].rearrange("p (h d) -> p h d", h=16), pO.rearrange("p (h d) -> p h d", h=16),
                                         ea.t[:].unsqueeze(2).to_broadcast([128, 16, 64]), ALU.mult,
                                         [RB[6], RB[7], ea.r], [t1h[0], t1h[1]])
                                self.tt_(t1.t[:], t1.t[:], pY, ALU.add, [t1h[0], t1h[1], RB[4], RB[5]], [t1h[0], t1h[1]])
                                self.tt_(t2.t[:].rearrange("p (h d) -> p h d", h=16), xv,
                                         sm.t[:, C_DSK:C_DSK + 16].unsqueeze(2).to_broadcast([128, 16, 64]), ALU.mult,
                                         [x_tm.r, sm.r], [t2.r])
                                self.tt_(t1.t[:], t1.t[:], t2.t[:], ALU.add, [t1h[0], t1h[1], t2.r], [t1h[0], t1h[1]])
                                self.tt_(t2.t[:], t1.t[:], szc4.t[:, t4, :], ALU.mult, [t1h[0], t1h[1], szc4.r], [t2.r])
                                for g in range(2):
                                    self.act(yn.t[:, g * 512:(g + 1) * 512], t2.t[:, g * 512:(g + 1) * 512], AF.Square, [t2.r], [yn.r, ss.r],
                                             accum_out=ss.t[:, g:g + 1])
                                self.act(rs.t[:], ss.t[:], AF.Ln, [ss.r, epsc.r], [rs.r], bias=epsc.t[:], scale=1.0 / 512.0)
                                self.act(rs.t[:], rs.t[:], AF.Exp, [rs.r], [rs.r], scale=-0.5)
                                for g in range(2):
                                    gs = slice(g * 512, (g + 1) * 512)
                                    self.stt(yn.t[:, gs], t2.t[:, gs], rs.t[:, g:g + 1], ssmn.t[:, gs],
                                             ALU.mult, ALU.mult, [t2.r, rs.r, ssmn.r], [yn.r])
                                pv = prevf.t[:].rearrange("p (h d) -> p h d", h=16)
                                self.tt_(pv, pv, cd.t[:].unsqueeze(2).to_broadcast([128, 16, 64]), ALU.mult, [prevf.r, cd.r], [prevf.r])
                                self.tt_(prevf.t[:], prevf.t[:], pSt, ALU.add, [prevf.r, RB[1], RB[2]], [prevf.r])
                                self.cp(prevb.t[:], prevf.t[:], [prevf.r], [prevb.r], eng="act")
                                for c8 in range(8):
                                    self.tr(pb[:, c8 * 128:(c8 + 1) * 128], yn.t[:, c8 * 128:(c8 + 1) * 128], identb.t[:],
                                            [yn.r, identb.r], [RB[3]])
                                self.cp(yTs.t[:, :, tsl], pb.rearrange("p (c t) -> p c t", c=8), [RB[3]], [yTs.r], eng="act")
                            for mh in range(2):
                                wov, wor = self.wload(WA, wo[l, 0, mh], [128, 8, 512])
                                for mm_ in range(4):
                                    m = mh * 4 + mm_
                                    b = 4 + (m % 4)
                                    for fc in range(8):
                                        self.mm(bank(b), wov[:, fc, mm_ * 128:(mm_ + 1) * 128], yTs.t[:, fc, :], fc == 0, fc == 7,
                                                [wor, yTs.r], [RB[b]])
                                    resid_update(tt, m, bank(b), RB[b], mv[l].t[:, 5, m:m + 1], [mv[l].r])
                    if FL["ssd"] and p < NPASS - 1:
                        self.dma("sp", sst[l], prevf.t[:], "d_ss0", [prevf.r], [R_st[l]])
                        self.dma("sp", ctl[l], carry.t[:].rearrange("p a b -> p (a b)"), "d_ss1", [carry.r], [R_st[l]])
                S.barrier()
                if not FL["att"]:
                    return
                S.barrier(engs=("pool",))
                with ExitStack() as st:
                    nkmax = g0 + NT
                    qq = self.sb(st, "qq", [128, 4, TT], BF16)
                    ksl = [self.sb(st, "ksl%d" % i, [128, nkmax], BF16) for i in range(2)]
                    vsl = [self.sb(st, "vsl%d" % i, [128, nkmax // 128, 256], BF16) for i in range(2)]
                    SS = self.sb(st, "SS", [128, 2, TT])
                    SSb2 = [self.sb(st, "SSb%d" % i, [128, 2, TT], BF16) for i in range(2)]
                    SSb = SSb2[0]
                    etmp = [self.sb(st, "etmp%d" % i, [128, 2, TT]) for i in range(2)]
                    spb = [self.sb(st, "spb%d" % i, [128, 2, TT], BF16) for i in range(3)]
                    Wt = [self.sb(st, "Wt%d" % i, [128, 2, TT], BF16) for i in range(3)]
                    Esum = self.sb(st, "Esum", [128, 2, TT])
                    REs = [Res("Esum0"), Res("Esum1")]
                    yTa = self.sb(st, "yTa", [128, 8, TT], BF16)
                    RyT = [Res("yTa%d" % i) for i in range(8)]
                    D0b = self.sb(st, "D0b", [128, 4, 128], BF16)
                    D1b = self.sb(st, "D1b", [128, 4, 128], BF16)
                    self.cp(D0b.t[:], D0.t[:], [D0.r], [D0b.r])
                    self.cp(D1b.t[:], D1.t[:], [D1.r], [D1b.r])
                    Eb = Wt
                    kvi = 0
                    ZP = (0, 4, 6)

                    def pair(b, cs):
                        return PS[:, b:b + 2, cs]

                    def pipeline(n, stages, skews):
                        for k in range(n + max(skews)):
                            for stg, sk in zip(stages, skews):
                                u = k - sk
                                if 0 <= u < n:
                                    stg(u)

                    mlt2 = m_lt.t[:].unsqueeze(1).to_broadcast([128, 2, 128])
                    for tt in range(NTT):
                        sl = slice(tt * TT, (tt + 1) * TT)
                        gt0 = g0 + tt * TT
                        i0 = gt0 // 128
                        Jmax = i0 + 3
                        nk = (Jmax + 1) * 128
                        nJ = Jmax + 1

                        def load_kv(kind, idx, slot):
                            ks_, vs_ = ksl[slot], vsl[slot]
                            if kind == 0:
                                self.dma("sp", ks_.t[:, 0:nk], ksb[l, idx, :, 0:nk], "d_k%d" % slot, [R_kv[l]], [ks_.r])
                                self.dma("sp", vs_.t[:, 0:nJ, :],
                                         vsb[l, 0:nk, idx * 256:(idx + 1) * 256].rearrange("(j p) f -> p j f", p=128),
                                         "d_v%d" % slot, [R_kv[l]], [vs_.r])
                            else:
                                self.dma("sp", ks_.t[:, 0:nk], kdf[l, idx, :, 0:nk], "d_k%d" % slot, [R_kv[l]], [ks_.r])
                                self.dma("sp", vs_.t[:, 0:nJ, 0:128],
                                         vdf[l, 0:nk, idx * 128:(idx + 1) * 128].rearrange("(j p) f -> p j f", p=128),
                                         "d_v%d" % slot, [R_kv[l]], [vs_.r])

                        self.dma("sp", qq.t[:], qsb[:, :, sl].rearrange("c p t -> p c t"), "d_qs", [R_q], [qq.r])
                        units = [(c, J) for c in range(4) for J in range(Jmax, -1, -1)]
                        slot_of = {c: (kvi + c) % 2 for c in range(4)}
                        load_kv(0, 0, slot_of[0])

                        def geom(u):
                            c, J = units[u]
                            d = J - i0
                            c0 = max(d, 0) * 128
                            return c, J, d, c0, slice(c0, TT)

                        def sb1(u):
                            c, J, d, c0, cs = geom(u)
                            ks_ = ksl[slot_of[c]]
                            b = ZP[u % 3]
                            for e_ in range(2):
                                ps_ = slice(e_ * 64, (e_ + 1) * 64)
                                self.mm(bank(b + e_)[:, cs], ks_.t[ps_, J * 128:(J + 1) * 128], qq.t[ps_, c, cs], True, True,
                                        [ks_.r, qq.r], [RB[b + e_]])

                        def sb2(u):
                            c, J, d, c0, cs = geom(u)
                            b = ZP[u % 3]
                            et, sp_ = etmp[u % 2], spb[u % 3]
                            self.act(et.t[:, :, cs], pair(b, cs), AF.Exp, [RB[b], RB[b + 1]], [et.r])
                            self.act(sp_.t[:, :, cs], et.t[:, :, cs], AF.Ln, [et.r], [sp_.r], bias=1.0, scale=1.0)
                            if d >= 0:
                                dsl = slice(c0, c0 + 128)
                                self.tt_(sp_.t[:, :, dsl], sp_.t[:, :, dsl], mlt2, ALU.mult, [sp_.r, m_lt.r], [sp_.r])

                        def sb3(u):
                            c, J, d, c0, cs = geom(u)
                            b = ZP[u % 3]
                            sp_ = spb[u % 3]
                            for e_ in range(2):
                                self.mm(bank(b + e_)[:, cs], triN.t[:], sp_.t[:, e_, cs], False, True, [triN.r, sp_.r], [RB[b + e_]], skip=True)
                                if J < Jmax:
                                    ssb_ = SSb2[u % 2]
                                    self.mm(bank(b + e_)[:, cs], negones.t[:], ssb_.t[:, e_, cs], False, True, [negones.r, ssb_.r],
                                            [RB[b + e_]], skip=True)

                        def sbU(u):
                            c, J, d, c0, cs = geom(u)
                            sp_ = spb[u % 3]
                            if J == 0:
                                return
                            nxt = SSb2[(u + 1) % 2]
                            if J == Jmax:
                                self.memset(SS.t[:], 0.0, [SS.r])
                            self.tt_(SS.t[:, :, cs], SS.t[:, :, cs], sp_.t[:, :, cs], ALU.add, [SS.r, sp_.r], [SS.r])
                            self.cp(nxt.t[:], SS.t[:], [SS.r], [nxt.r])

                        def sb4(u):
                            c, J, d, c0, cs = geom(u)
                            b = ZP[u % 3]
                            wt_ = Wt[u % 3]
                            self.act(wt_.t[:, :, cs], pair(b, cs), AF.Exp, [RB[b], RB[b + 1]], [wt_.r])
                            if d >= 0:
                                dsl = slice(c0, c0 + 128)
                                self.tt_(wt_.t[:, :, dsl], wt_.t[:, :, dsl], mlt2, ALU.mult, [wt_.r, m_lt.r], [wt_.r])

                        def sb5(u):
                            c, J, d, c0, cs = geom(u)
                            wt_, sp_ = Wt[u % 3], spb[u % 3]
                            vs_ = vsl[slot_of[c]]
                            bo = 2 + (c % 2)
                            for e_ in range(2):
                                first = (e_ == 0 and J == Jmax)
                                self.mm(bank(bo)[:, cs], vs_.t[:, J, e_ * 128:(e_ + 1) * 128], wt_.t[:, e_, cs], first, True,
                                        [vs_.r, wt_.r], [RB[bo]], skip=not first)
                            if J == Jmax and c + 1 < 4:
                                load_kv(0, c + 1, slot_of[c + 1])
                            if J == 0:
                                self.cp(yTa.t[:, c, :], bank(bo), [RB[bo]], [RyT[c]], eng="dve")

                        pipeline(len(units), (sb1, sb2, sb3, sbU, sb4, sb5), (0, 0, 1, 1, 1, 2))
                        kvi += 4

                        self.dma("sp", qq.t[:], qdf[:, :, sl].rearrange("c p t -> p c t"), "d_qs", [R_q], [qq.r])
                        dunits = [(h, J) for h in range(4) for J in range(0, Jmax + 1)]
                        dslot = {h: (kvi + h) % 2 for h in range(4)}
                        load_kv(1, 0, dslot[0])
                        tS = SS.t[:].rearrange("p a (b c) -> p (a b) c", b=2)[:, 0:2, :]
                        r1, r2 = etmp[0].t[:, 0, :], etmp[0].t[:, 1, :]
                        ot, ou = etmp[1].t[:, 0, :], etmp[1].t[:, 1, :]
                        rsd = SS.t[:, 1, :]
                        sqd = SSb.t[:, 0, :]

                        def dgeom(u):
                            h, J = dunits[u]
                            d = J - i0
                            c0 = max(d, 0) * 128
                            return h, J, d, c0, slice(c0, TT)

                        DP = (0, 4)

                        def df1(u):
                            h, J, d, c0, cs = dgeom(u)
                            ks_ = ksl[dslot[h]]
                            b = DP[u % 2]
                            for m_ in range(2):
                                ps_ = slice(m_ * 64, (m_ + 1) * 64)
                                self.mm(bank(b + m_)[:, cs], ks_.t[ps_, J * 128:(J + 1) * 128], qq.t[ps_, h, cs], True, True,
                                        [ks_.r, qq.r], [RB[b + m_]])
                                if d >= 0:
                                    self.mm(bank(b + m_)[:, c0:c0 + 128], identb.t[:], D0b.t[:, h, :], False, True,
                                            [identb.r, D0b.r], [RB[b + m_]], skip=True)
                                    if c0 + 256 <= TT:
                                        self.mm(bank(b + m_)[:, c0 + 128:c0 + 256], identb.t[:], D1b.t[:, h, :], False, True,
                                                [identb.r, D1b.r], [RB[b + m_]], skip=True)
                                elif d == -1:
                                    self.mm(bank(b + m_)[:, 0:128], identb.t[:], D1b.t[:, h, :], False, True,
                                            [identb.r, D1b.r], [RB[b + m_]], skip=True)

                        def df2(u):
                            h, J, d, c0, cs = dgeom(u)
                            b = DP[u % 2]
                            eb = Eb[u % 3]
                            self.act(eb.t[:, :, cs], pair(b, cs), AF.Exp, [RB[b], RB[b + 1]], [eb.r])

                        def df3(u):
                            h, J, d, c0, cs = dgeom(u)
                            eb = Eb[u % 3]
                            vs_ = vsl[dslot[h]]
                            for m_ in range(2):
                                self.mm(bank(2 + m_)[:, cs], vs_.t[:, J, 0:128], eb.t[:, m_, cs], J == 0, True,
                                        [vs_.r, eb.r], [RB[2 + m_]], skip=J > 0)
                            if J == 0:
                                self.cp(Esum.t[:, 0, :], eb.t[:, 0, :], [eb.r], [REs[0]])
                            else:
                                self.tt_(Esum.t[:, 0, cs], Esum.t[:, 0, cs], eb.t[:, 0, cs], ALU.add, [eb.r, REs[0]], [REs[0]])
                            self.mm(bank(7)[:, cs], onesb.t[:], eb.t[:, 1, cs], J == 0, True, [onesb.r, eb.r], [RB[7]], skip=J > 0)
                            if J == 0 and h + 1 < 4:
                                load_kv(1, h + 1, dslot[h + 1])
                            if J == Jmax:
                                self.mm(bank(6), onesf.t[:], Esum.t[:, 0, :], True, True, [onesf.r, REs[0]], [RB[6]])

                        def df4(u):
                            h, J, d, c0, cs = dgeom(u)
                            if J != Jmax:
                                return
                            bs = 6
                            self.act(r1, bank(6), AF.Ln, [RB[6]], [etmp[0].r])
                            self.act(r1, r1, AF.Exp, [etmp[0].r], [etmp[0].r], scale=-1.0)
                            self.act(r2, bank(7), AF.Ln, [RB[7]], [etmp[0].r])
                            self.act(r2, r2, AF.Exp, [etmp[0].r], [etmp[0].r], scale=-1.0)
                            self.tt_(ot, bank(2), r1, ALU.mult, [RB[2], etmp[0].r], [etmp[1].r])
                            self.tt_(ou, bank(3), r2, ALU.mult, [RB[3], etmp[0].r], [etmp[1].r])
                            self.stt(ot, ou, neglam[l].t[:, 0:1], ot, ALU.mult, ALU.add, [etmp[1].r, neglam[l].r], [etmp[1].r])
                            self.tt_(sqd, ot, ot, ALU.mult, [etmp[1].r], [SSb.r])
                            self.mm(bank(bs), ones_v.t[:], sqd, True, True, [ones_v.r, SSb.r], [RB[bs]])
                            self.act(rsd, bank(bs), AF.Ln, [RB[bs], epsc.r], [SS.r], bias=epsc.t[:], scale=1.0)
                            self.act(rsd, rsd, AF.Exp, [SS.r], [SS.r], scale=-0.5)
                            self.stt(yTa.t[:, 4 + h, :], ot, subg[l].t[:, 0:1], rsd, ALU.mult, ALU.mult,
                                     [etmp[1].r, SS.r, subg[l].r], [RyT[4 + h]])

                        pipeline(len(dunits), (df1, df2, df3, df4), (0, 0, 2, 2))
                        kvi += 4
                        for mh in range(2):
                            wov, wor = self.wload(WA, wo[l, 1, mh], [128, 8, 512])
                            for mm_ in range(4):
                                m = mh * 4 + mm_
                                b = 6 + (m % 2)
                                for fc in range(8):
                                    self.mm(bank(b), wov[:, fc, mm_ * 128:(mm_ + 1) * 128], yTa.t[:, fc, :], fc == 0, fc == 7,
                                            [wor, RyT[fc]], [RB[b]])
                                resid_update(tt, m, bank(b), RB[b], mv[l].t[:, 5, m:m + 1], [mv[l].r])
                S.barrier()

            for p in range(NPASS):
                with ExitStack() as st:
                    xl = [self.sb(st, "xl%d" % i, [128, D]) for i in range(2)]
                    for t16 in range(NT // 128):
                        xb_ = xl[t16 % 2]
                        r0 = p * NT + t16 * 128
                        self.dma("sp", xb_.t[:], xin[r0:r0 + 128, :], "d_x%d" % (t16 % 2), (), [xb_.r])
                        tt = t16 // 4
                        for half in range(2):
                            b = (t16 * 2 + half) % 4
                            for i in range(4):
                                c = half * 4 + i
                                self.tr(bank(b)[:, i * 128:(i + 1) * 128], xb_.t[:, c * 128:(c + 1) * 128], identf.t[:],
                                        [xb_.r, identf.r], [RB[b]])
                            wr_ = [RX[half * 4 + i][tt] for i in range(4)]
                            self.cp(xT.t[:, half * 4:half * 4 + 4, t16 * 128:(t16 + 1) * 128],
                                    bank(b).rearrange("p (c t) -> p c t", c=4), [RB[b]], wr_,
                                    eng="act" if half == 0 else "dve")
                S.barrier()
                for l in range(self.layers):
                    if FL["ffn1"]:
                        ffn(l, 0)
                    if FL["ssd"] or FL["att"]:
                        mixer(l, p)
                    if FL["ffn2"]:
                        ffn(l, 1)
                with ExitStack() as st:
                    osb = [self.sb(st, "osb%d" % i, [128, D]) for i in range(2)]
                    sqb = [self.sb(st, "fsq%d" % i, [128, TT], BF16) for i in range(2)]
                    rstds = [self.sb(st, "frstd%d" % i, [128, TT]) for i in range(NTT)]
                    ntm = [self.sb(st, "fnt%d" % i, [128, TT]) for i in range(3)]
                    if FL["fn"]:
                        rms_all(sqb, rstds, ntm, lambda c: gsm.t[:, G_FN + c:G_FN + c + 1], lambda c: 0.0,
                                lambda tt, c: xT.t[:, c, tt * TT:(tt + 1) * TT], lambda tt, c: RX[c][tt])
                    for tt in range(NTT):
                        for t4 in range(4):
                            ob = osb[t4 % 2]
                            c0_ = tt * TT + t4 * 128
                            for half in range(2):
                                b = (t4 * 2 + half) % 4
                                for i in range(4):
                                    c = half * 4 + i
                                    self.tr(bank(b)[:, i * 128:(i + 1) * 128], xT.t[:, c, c0_:c0_ + 128], identf.t[:],
                                            [RX[c][tt], identf.r], [RB[b]])
                                self.cp(ob.t[:, half * 512:(half + 1) * 512], bank(b), [RB[b]], [ob.r],
                                        eng="act" if half == 0 else "dve")
                            r0 = p * NT + tt * TT + t4 * 128
                            self.dma("sp", out_d[r0:r0 + 128, :], ob.t[:], "d_out%d" % (t4 % 2), [ob.r], [R_out])
                S.barrier()
            S.wait_all("sp", [k for k in S.cnt if k.startswith("d_out") or k.startswith("d_s")])
            S.emit()
        return nc


def _t5_bucket_np(dist):
    max_exact = 16
    d = np.maximum(dist.astype(np.float32), np.float32(max_exact))
    large = max_exact + (np.log(d / np.float32(max_exact)) / np.float32(math.log(128 / max_exact))
                         * np.float32(32 - max_exact)).astype(np.int32)
    large = np.minimum(large, 31)
    return np.where(dist < max_exact, dist, large)


def _bucket_tiles():
    k = np.arange(128)[:, None]
    q = np.arange(128)[None, :]
    d0 = q - k
    b0 = np.where(d0 >= 0, _t5_bucket_np(np.maximum(d0, 0)), 32).astype(np.float32)
    d1 = q - k + 128
    b1 = _t5_bucket_np(d1).astype(np.float32)
    return b0, b1


def pack_weights(inp):
    f = np.float32
    L = DEPTH
    A = lambda a: np.ascontiguousarray(a, dtype=f)

    def kblk(w):
        return w.reshape(8, 128, -1).transpose(1, 0, 2)

    out = {}
    aw = inp["ada_w"]
    out["adaw"] = A(np.stack([np.stack([kblk(aw[l][:, b * 512:(b + 1) * 512]) for b in range(18)]) for l in range(L)]))
    for fi, (n13, n2) in enumerate((("ffn1_w13", "ffn1_w2"), ("ffn2_w13", "ffn2_w2")), start=1):
        w13 = inp[n13]
        blocks = []
        for l in range(L):
            bl = []
            for j in range(NJ):
                a = kblk(w13[l][:, j * 128:(j + 1) * 128])
                u = kblk(w13[l][:, DFF + j * 128:DFF + (j + 1) * 128])
                bl.append(np.concatenate([a, u], axis=2))
            blocks.append(np.stack(bl))
        out["w13_%d" % fi] = A(np.stack(blocks))
        w2 = inp[n2]
        t = np.zeros((L, 4, 8, 128, 6, 128), f)
        for l in range(L):
            for H in range(4):
                for jj in range(FPARTS[H]):
                    j = FOFF[H] + jj
                    t[l, H, :, :, jj, :] = w2[l][j * 128:(j + 1) * 128, :].reshape(128, 8, 128).transpose(1, 0, 2)
        out["w2_%d" % fi] = t
    wi = inp["w_in"]
    out["wz"] = A(np.stack([np.stack([kblk(wi[l][:, h * 512:(h + 1) * 512]) for h in range(2)]) for l in range(L)]))
    out["wx"] = A(np.stack([np.stack([kblk(wi[l][:, 1024 + c * 128:1024 + (c + 1) * 128]) for c in range(12)]) for l in range(L)]))
    out["wdt"] = A(np.stack([kblk(wi[l][:, 2560:2576]) for l in range(L)]))
    qk_off = [2576 + i * 128 for i in range(4)] + [3088 + i * 128 for i in range(4)] + \
             [4112 + i * 128 for i in range(4)] + [4624 + i * 128 for i in range(4)]
    out["wqk"] = A(np.stack([np.stack([kblk(wi[l][:, o:o + 128]) for o in qk_off]) for l in range(L)]))
    out["wv"] = A(np.stack([np.stack([kblk(wi[l][:, 3600:4112]), kblk(wi[l][:, 5136:5648])]) for l in range(L)]))
    wo = inp["w_out"]
    out["wo"] = A(np.stack([np.stack([np.stack([kblk(wo[l][pt * 1024:(pt + 1) * 1024, mh * 512:(mh + 1) * 512])
                                                for mh in range(2)]) for pt in range(2)]) for l in range(L)]))
    small = np.zeros((L, 128, NS), f)
    fm = lambda v: v.reshape(-1, 128).T
    for l in range(L):
        s = small[l]
        s[:, C_N1:C_N1 + 8] = fm(inp["ffn1_norm"][l])
        s[:, C_N2:C_N2 + 8] = fm(inp["mix_norm"][l])
        s[:, C_N3:C_N3 + 8] = fm(inp["ffn2_norm"][l])
        s[:, C_ADAB:C_ADAB + 72] = fm(inp["ada_b"][l])
        cw = inp["ssm_conv_w"][l].reshape(12, 128, 4).transpose(1, 0, 2).reshape(128, 48)
        s[:, C_CONVW:C_CONVW + 48] = cw
        s[:, C_CONVB:C_CONVB + 12] = fm(inp["ssm_conv_b"][l])
        s[:, C_DTB:C_DTB + 16] = inp["ssm_dt_bias"][l][None, :]
        s[:, C_ALOG:C_ALOG + 16] = inp["ssm_a_log"][l][None, :]
        s[:, C_DSK:C_DSK + 16] = inp["ssm_d"][l][None, :]
        s[:, C_SUBLN] = inp["diff_subln"][l]
        s[:, C_LQ1:C_LQ1 + 64] = inp["diff_lambda_q1"][l][None, :]
        s[:, C_LK1:C_LK1 + 64] = inp["diff_lambda_k1"][l][None, :]
        s[:, C_LQ2:C_LQ2 + 64] = inp["diff_lambda_q2"][l][None, :]
        s[:, C_LK2:C_LK2 + 64] = inp["diff_lambda_k2"][l][None, :]
    out["small"] = small
    out["ssmn"] = A(np.broadcast_to(inp["ssm_norm"][:, None, :], (L, 128, 1024)))
    gs = np.zeros((128, NG), f)
    gs[:, G_FN:G_FN + 8] = fm(inp["final_norm"])
    gs[:, G_RB:G_RB + 128] = inp["rel_bias"].reshape(1, 128)
    b0, b1 = _bucket_tiles()
    gs[:, G_B0:G_B0 + 128] = b0
    gs[:, G_B1:G_B1 + 128] = b1
    out["gsmall"] = gs
    return out


_PROG_CACHE = {}


def run(inputs, NT=2048, NPASS=2, n_batch=4, **flags):
    inp = {k: np.asarray(v, dtype=np.float32) for k, v in inputs.items()}
    key = (NT, NPASS, tuple(sorted(flags.items())))
    if key not in _PROG_CACHE:
        _PROG_CACHE[key] = Builder(NT, NPASS, **flags).build()
    nc = _PROG_CACHE[key]
    shared = pack_weights(inp)
    ntok = NT * NPASS
    in_maps = []
    for core in range(8):
        b = core % n_batch
        m = dict(shared)
        m["xin"] = np.ascontiguousarray(inp["x"][b, :ntok, :])
        m["cT"] = np.ascontiguousarray(inp["c"][b].reshape(8, 128).T)
        in_maps.append(m)
    res = run_bass_kernel_spmd(nc, in_maps, core_ids=list(range(8)))
    return np.stack([res.results[b]["out"] for b in range(n_batch)], axis=0)


def kernel(**inputs):
    out = run(inputs, NT=2048, NPASS=2)
    return out.astype(np.float32)
```

```python
import math
from contextlib import ExitStack

import numpy as np
import concourse.bass as bass
import concourse.mybir as mybir
from concourse.bass_utils import run_bass_kernel_spmd

F32 = mybir.dt.float32
BF16 = mybir.dt.bfloat16
AF = mybir.ActivationFunctionType
ALU = mybir.AluOpType
AX = mybir.AxisListType

D = 1024
KD = 8
DFF = 2816
NJ = 22
NJH = 11
SEQ = 4096
DEPTH = 2
TT = 512
EPS = 1e-6
NEG = -30000.0

C_N1, C_N2, C_N3 = 0, 8, 16
C_ADAB = 24
C_CONVW = 96
C_CONVB = 144
C_DTB, C_ALOG, C_DSK = 156, 172, 188
C_SUBLN = 204
C_LQ1, C_LK1, C_LQ2, C_LK2 = 205, 269, 333, 397
NS = 461
G_FN, G_RB, G_B0, G_B1 = 0, 8, 136, 264
NG = 392


class Res:
    __slots__ = ("name", "w", "r")

    def __init__(self, name):
        self.name = name
        self.w = None
        self.r = {}


class Sched:
    ENG = ("pe", "act", "dve", "pool", "sp")

    def __init__(self, nc):
        self.nc = nc
        self.prog = {k: [] for k in self.ENG}
        self.cnt = {}
        self.seen = {k: {} for k in self.ENG}
        self.semkeys = []
        self.owner = {}

    def _sem(self, key):
        if key not in self.cnt:
            self.cnt[key] = 0
            self.semkeys.append(key)

    def _need(self, eng, dep, kind):
        if dep is None:
            return
        key, val = dep
        if key == "c_" + eng:
            if eng == "pe":
                return
        if self.owner.get(key) == eng:
            val = self.cnt[key]
        if self.seen[eng].get(key, 0) >= val:
            return
        self.seen[eng][key] = val
        self.prog[eng].append(("wait", key, val))

    def op(self, eng, fn, reads=(), writes=(), dma=None):
        for r in reads:
            self._need(eng, r.w, "raw")
        for w in writes:
            self._need(eng, w.w, "waw")
            for k, v in list(w.r.items()):
                self._need(eng, (k, v), "war")
        if dma is None:
            key, inc = "c_" + eng, 1
        else:
            key, inc = dma, 16
        self._sem(key)
        if dma is not None:
            self.owner[key] = eng
        self.cnt[key] += inc
        me = (key, self.cnt[key])
        self.prog[eng].append(("op", fn, key, inc))
        for r in reads:
            if r.r.get(key, 0) < me[1]:
                r.r[key] = me[1]
        for w in writes:
            w.w = me
            w.r = {}
        return me

    def wait_all(self, eng, keys):
        for k in keys:
            if k in self.cnt:
                self._need(eng, (k, self.cnt[k]), "raw")

    def barrier(self, engs=("pe", "act", "dve", "sp")):
        for e in engs:
            for k in list(self.cnt.keys()):
                if k == "c_" + e or k.startswith("d_w"):
                    continue
                v = self.cnt[k]
                if self.seen[e].get(k, 0) < v:
                    self.seen[e][k] = v
                    self.prog[e].append(("wait", k, v))

    def emit(self):
        nc = self.nc
        sems = {k: nc.alloc_semaphore(name="s_" + k) for k in self.semkeys}
        prog = self.prog

        def replay(ek, e):
            for it in prog[ek]:
                if it[0] == "wait":
                    e.wait_ge(sems[it[1]], it[2])
                else:
                    it[1](e).then_inc(sems[it[2]], it[3])

        with nc.Block() as block:
            @block.tensor
            def _(e):
                replay("pe", e)

            @block.scalar
            def _(e):
                replay("act", e)

            @block.vector
            def _(e):
                replay("dve", e)

            @block.gpsimd
            def _(e):
                replay("pool", e)

            @block.sync
            def _(e):
                replay("sp", e)


class Buf:
    __slots__ = ("t", "r")

    def __init__(self, t, r):
        self.t = t
        self.r = r


FPARTS = (6, 6, 5, 5)
FOFF = (0, 6, 12, 17)


class Builder:
    def __init__(self, NT, NPASS, layers=DEPTH, do_ffn1=True, do_ssd=True, do_att=True,
                 do_ffn2=True, final_norm=True):
        self.NT, self.NPASS = NT, NPASS
        self.NTT = NT // TT
        self.NTOK = NT * NPASS
        self.layers = layers
        self.flags = dict(ffn1=do_ffn1, ssd=do_ssd, att=do_att, ffn2=do_ffn2, fn=final_norm)
        self.nc = bass.Bass("TRN2", target_bir_lowering=False)
        self.S = Sched(self.nc)
        self.uid = 0

    def sb(self, st, name, shape, dt=F32):
        self.uid += 1
        nm = "%s_%d" % (name, self.uid)
        t = st.enter_context(self.nc.sbuf_tensor(nm, list(shape), dt))
        return Buf(t, Res(nm))

    def dram_in(self, name, shape, dt=F32):
        return self.nc.dram_tensor(name, list(shape), dt, kind="ExternalInput").ap()

    def dram_scr(self, name, shape, dt):
        return self.nc.dram_tensor(name, list(shape), dt, kind="Internal").ap()

    def mm(self, out, lhsT, rhs, start, stop, reads, writes, skip=False):
        kw = dict(skip_group_check=True) if skip else {}
        self.S.op("pe", lambda e: e.matmul(out, lhsT=lhsT, rhs=rhs, start=start, stop=stop, **kw),
                  reads, writes)

    def tr(self, out, in_, ident, reads, writes):
        self.S.op("pe", lambda e: e.transpose(out=out, in_=in_, identity=ident), reads, writes)

    def act(self, out, in_, func, reads, writes, bias=None, scale=None, accum_out=None):
        kw = {}
        if bias is not None:
            kw["bias"] = bias
        if scale is not None:
            kw["scale"] = scale
        if accum_out is not None:
            kw["accum_out"] = accum_out
        self.S.op("act", lambda e: e.activation(out=out, in_=in_, func=func, **kw), reads, writes)

    def tt_(self, out, in0, in1, op, reads, writes, eng="dve"):
        self.S.op(eng, lambda e: e.tensor_tensor(out=out, in0=in0, in1=in1, op=op), reads, writes)

    def ts(self, out, in0, s1, op0, reads, writes, s2=None, op1=None, eng="dve"):
        if op1 is None:
            self.S.op(eng, lambda e: e.tensor_scalar(out=out, in0=in0, scalar1=s1, scalar2=None, op0=op0),
                      reads, writes)
        else:
            self.S.op(eng, lambda e: e.tensor_scalar(out=out, in0=in0, scalar1=s1, scalar2=s2, op0=op0, op1=op1),
                      reads, writes)

    def stt(self, out, in0, scalar, in1, op0, op1, reads, writes):
        self.S.op("dve", lambda e: e.scalar_tensor_tensor(out=out, in0=in0, scalar=scalar, in1=in1,
                                                          op0=op0, op1=op1), reads, writes)

    def cp(self, out, in_, reads, writes, eng="dve"):
        if eng == "act":
            self.act(out, in_, AF.Identity, reads, writes)
        else:
            self.S.op(eng, lambda e: e.tensor_copy(out=out, in_=in_), reads, writes)

    def memset(self, ap, val, writes, eng="dve"):
        self.S.op(eng, lambda e: e.memset(ap, val), (), writes)

    def dma(self, q, out, in_, sem, reads, writes):
        self.S.op(q, lambda e: e.dma_start(out=out, in_=in_), reads, writes, dma=sem)

    def wload(self, pool, src, shape):
        i = pool["idx"] % len(pool["slots"])
        pool["idx"] += 1
        b = pool["slots"][i]
        n = int(np.prod(shape[1:]))
        flat = b.t[:, 0:n]
        view = flat.rearrange("p (a b) -> p a b", a=shape[1])
        self.dma("pool", view, src, "d_w%s%d" % (pool["name"], i), (), [b.r])
        return view, b.r

    def build(self):
        nc, S = self.nc, self.S
        NT, NPASS, NTT, NTOK = self.NT, self.NPASS, self.NTT, self.NTOK
        L = DEPTH
        FL = self.flags
        xin = self.dram_in("xin", [NTOK, D])
        cT_d = self.dram_in("cT", [128, 8])
        adaw = self.dram_in("adaw", [L, 18, 128, 8, 512])
        w13 = [self.dram_in("w13_%d" % f, [L, NJ, 128, 8, 256]) for f in (1, 2)]
        w2 = [self.dram_in("w2_%d" % f, [L, 4, 8, 128, 6, 128]) for f in (1, 2)]
        wz = self.dram_in("wz", [L, 2, 128, 8, 512])
        wx = self.dram_in("wx", [L, 12, 128, 8, 128])
        wdt = self.dram_in("wdt", [L, 128, 8, 16])
        wqk = self.dram_in("wqk", [L, 16, 128, 8, 128])
        wv = self.dram_in("wv", [L, 2, 128, 8, 512])
        wo = self.dram_in("wo", [L, 2, 2, 128, 8, 512])
        small_d = self.dram_in("small", [L, 128, NS])
        ssmn_d = self.dram_in("ssmn", [L, 128, 1024])
        gsmall_d = self.dram_in("gsmall", [128, NG])
        out_d = nc.dram_tensor("out", [NTOK, D], F32, kind="ExternalOutput").ap()
        ksb = self.dram_scr("ksb", [L, 4, 128, NTOK], BF16)
        kdf = self.dram_scr("kdf", [L, 4, 128, NTOK], BF16)
        vsb = self.dram_scr("vsb", [L, NTOK, 1024], BF16)
        vdf = self.dram_scr("vdf", [L, NTOK, 512], BF16)
        qsb = self.dram_scr("qsb", [4, 128, NT], BF16)
        qdf = self.dram_scr("qdf", [4, 128, NT], BF16)
        sst = self.dram_scr("sst", [L, 128, 1024], F32)
        ctl = self.dram_scr("ctl", [L, 128, 36], F32)
        R_kv = [Res("kv%d" % l) for l in range(L)]
        R_q = Res("qscr")
        R_st = [Res("sst%d" % l) for l in range(L)]
        R_out = Res("out")

        with ExitStack() as top:
            PS = top.enter_context(nc.psum_tensor("ps", [128, 8, 512], F32))
            RB = [Res("bank%d" % i) for i in range(8)]

            def bank(i):
                return PS[:, i, :]

            def bank2(i):
                return PS[:, i:i + 2, :].rearrange("p a b -> p (a b)")

            def bankb(i):
                return PS[:, i, :].bitcast(BF16)

            xT = self.sb(top, "xT", [128, KD, NT])
            RX = [[Res("xT_%d_%d" % (c, t)) for t in range(NTT)] for c in range(KD)]
            identb = self.sb(top, "identb", [128, 128], BF16)
            identf = self.sb(top, "identf", [128, 128])
            nidentb = self.sb(top, "nidentb", [128, 128], BF16)
            triN = self.sb(top, "triN", [128, 128], BF16)
            negones = self.sb(top, "negones", [128, 128], BF16)
            onesb = self.sb(top, "onesb", [128, 128], BF16)
            ones_d = self.sb(top, "ones_d", [128, 128], BF16)
            ones_v = self.sb(top, "ones_v", [128, 128], BF16)
            U32 = self.sb(top, "U32", [128, 128])
            onesf = self.sb(top, "onesf", [128, 128])
            m_lt = self.sb(top, "m_lt", [128, 128])
            D0 = self.sb(top, "D0", [128, 4, 128])
            D1 = self.sb(top, "D1", [128, 4, 128])
            small = [self.sb(top, "small%d" % l, [128, NS]) for l in range(L)]
            gsm = self.sb(top, "gsm", [128, NG])
            mv = [self.sb(top, "mv%d" % l, [128, 9, 8]) for l in range(L)]
            abc = [self.sb(top, "abc%d" % l, [128, 16]) for l in range(L)]
            neglam = [self.sb(top, "neglam%d" % l, [128, 1]) for l in range(L)]
            subg = [self.sb(top, "subg%d" % l, [128, 1]) for l in range(L)]
            epsc = self.sb(top, "epsc", [128, 1])
            WA = dict(name="a", idx=0, slots=[self.sb(top, "wa%d" % i, [128, 4096], BF16) for i in range(3)])
            WB = dict(name="b", idx=0, slots=[self.sb(top, "wb%d" % i, [128, 6 * 128], BF16) for i in range(3)])

            def sel(h):
                return identb.t[0:16, h:h + 1].to_broadcast([16, 128])

            def nsel(h):
                return nidentb.t[0:16, h:h + 1].to_broadcast([16, 128])

            def aff(buf, init, pattern, cmp_op, fill, base, cm):
                ap = buf.t[:]
                self.memset(ap, init, [buf.r], eng="pool")
                S.op("pool", lambda e: e.affine_select(out=ap, in_=ap, pattern=pattern, compare_op=cmp_op,
                                                       fill=fill, base=base, channel_multiplier=cm),
                     [buf.r], [buf.r])

            with ExitStack() as pst:
                aff(identf, 0.0, [[-1, 128]], ALU.not_equal, 1.0, 0, 1)
                self.cp(identb.t[:], identf.t[:], [identf.r], [identb.r])
                self.ts(nidentb.t[:], identf.t[:], -1.0, ALU.mult, [identf.r], [nidentb.r])
                tmpf = self.sb(pst, "tmpf", [128, 128])
                aff(tmpf, -1.0, [[-1, 128]], ALU.is_ge, 0.0, 0, 1)
                self.cp(triN.t[:], tmpf.t[:], [tmpf.r], [triN.r])
                self.memset(negones.t[:], -1.0, [negones.r])
                self.memset(onesb.t[:], 1.0, [onesb.r])
                self.memset(ones_d.t[:], 1.0 / 1024.0, [ones_d.r])
                self.memset(ones_v.t[:], 1.0 / 128.0, [ones_v.r])
                self.memset(onesf.t[:], 1.0, [onesf.r])
                self.memset(epsc.t[:], EPS, [epsc.r])
                aff(U32, 1.0, [[1, 128]], ALU.is_ge, 0.0, 0, -1)
                aff(m_lt, 1.0, [[1, 128]], ALU.is_ge, 0.0, -1, -1)

                for l in range(L):
                    self.dma("sp", small[l].t[:], small_d[l], "d_c%d" % l, (), [small[l].r])
                self.dma("sp", gsm.t[:], gsmall_d, "d_cg", (), [gsm.r])
                cT = self.sb(pst, "cT", [128, 8])
                self.dma("sp", cT.t[:], cT_d, "d_cc", (), [cT.r])
                scT = self.sb(pst, "scT", [128, 8], BF16)
                self.act(scT.t[:], cT.t[:], AF.Silu, [cT.r], [scT.r])

                rbd = self.sb(pst, "rbd", [128, 32, 4])
                rbv = gsm.t[:, G_RB:G_RB + 128].rearrange("p (b h) -> p b h", h=4)
                self.tt_(rbd.t[:], rbv, gsm.t[:, G_RB + 124:G_RB + 128].unsqueeze(1).to_broadcast([128, 32, 4]),
                         ALU.subtract, [gsm.r], [rbd.r])
                tmpd = self.sb(pst, "tmpd", [128, 128])
                for (Dt, gcol) in ((D0, G_B0), (D1, G_B1)):
                    bidx = gsm.t[:, gcol:gcol + 128]
                    for h in range(4):
                        dst = Dt.t[:, h, :]
                        self.ts(dst, bidx, 32.0, ALU.is_equal, [gsm.r], [Dt.r], s2=NEG, op1=ALU.mult)
                        for b in range(31):
                            self.ts(tmpd.t[:], bidx, float(b), ALU.is_equal, [gsm.r, rbd.r], [tmpd.r],
                                    s2=rbd.t[:, b, h:h + 1], op1=ALU.mult)
                            self.tt_(dst, dst, tmpd.t[:], ALU.add, [tmpd.r, Dt.r], [Dt.r])

                for l in range(L):
                    modps = bank(0)
                    for blk in range(18):
                        wv_, wr = self.wload(WA, adaw[l, blk], [128, 8, 512])
                        for oc in range(4):
                            j = blk * 4 + oc
                            for kc in range(KD):
                                self.mm(modps[:, j:j + 1], wv_[:, kc, oc * 128:(oc + 1) * 128], scT.t[:, kc:kc + 1],
                                        kc == 0, kc == KD - 1, [wr, scT.r], [RB[0]])
                    modT = self.sb(pst, "modT%d" % l, [128, 9, 8])
                    self.tt_(modT.t[:].rearrange("p a b -> p (a b)"), modps[:, 0:72],
                             small[l].t[:, C_ADAB:C_ADAB + 72], ALU.add, [RB[0], small[l].r], [modT.r])
                    for s, cn in enumerate((C_N1, C_N2, C_N3)):
                        self.stt(mv[l].t[:, 3 * s, :], modT.t[:, 3 * s + 1, :], 1.0, small[l].t[:, cn:cn + 8],
                                 ALU.add, ALU.mult, [modT.r, small[l].r], [mv[l].r])
                        self.cp(mv[l].t[:, 3 * s + 1, :], modT.t[:, 3 * s, :], [modT.r], [mv[l].r])
                        self.ts(mv[l].t[:, 3 * s + 2, :], modT.t[:, 3 * s + 2, :], 1.0 if s == 1 else 0.5, ALU.mult,
                                [modT.r], [mv[l].r])
                    self.act(abc[l].t[:], small[l].t[:, C_ALOG:C_ALOG + 16], AF.Exp, [small[l].r], [abc[l].r])
                    self.ts(abc[l].t[:], abc[l].t[:], -1.0, ALU.mult, [abc[l].r], [abc[l].r])
                    lam_init = 0.8 - 0.6 * math.exp(-0.3 * l)
                    lt = self.sb(pst, "lt%d" % l, [128, 64])
                    l2 = self.sb(pst, "l2_%d" % l, [128, 2])
                    for i, (ca, cb) in enumerate(((C_LQ1, C_LK1), (C_LQ2, C_LK2))):
                        self.tt_(lt.t[:], small[l].t[:, ca:ca + 64], small[l].t[:, cb:cb + 64], ALU.mult,
                                 [small[l].r], [lt.r])
                        S.op("dve", (lambda o, i_: (lambda e: e.tensor_reduce(out=o, in_=i_, axis=AX.X, op=ALU.add)))(
                            l2.t[:, i:i + 1], lt.t[:]), [lt.r], [l2.r])
                    self.act(l2.t[:], l2.t[:], AF.Exp, [l2.r], [l2.r])
                    self.tt_(neglam[l].t[:], l2.t[:, 1:2], l2.t[:, 0:1], ALU.subtract, [l2.r], [neglam[l].r])
                    self.ts(neglam[l].t[:], neglam[l].t[:], -lam_init, ALU.add, [neglam[l].r], [neglam[l].r])
                    self.ts(subg[l].t[:], small[l].t[:, C_SUBLN:C_SUBLN + 1], 1.0 - lam_init, ALU.mult,
                            [small[l].r], [subg[l].r])
            S.barrier()

            def rms_tile(scr, tt, gain_fn, shift_fn, dst_fn, dst_res, pbank):
                sl = slice(tt * TT, (tt + 1) * TT)
                sq = scr[0:2]
                rstd_ap, rstd_r = scr[2]
                tmp = scr[3:5]
                for c in range(KD):
                    q_ap, q_r = sq[c % 2]
                    self.tt_(q_ap, xT.t[:, c, sl], xT.t[:, c, sl], ALU.mult, [RX[c][tt]], [q_r])
                    self.mm(bank(pbank), ones_d.t[:], q_ap, c == 0, c == KD - 1, [ones_d.r, q_r], [RB[pbank]])
                self.act(rstd_ap, bank(pbank), AF.Ln, [RB[pbank], epsc.r], [rstd_r], bias=epsc.t[:], scale=1.0)
                self.act(rstd_ap, rstd_ap, AF.Exp, [rstd_r], [rstd_r], scale=-0.5)
                for c in range(KD):
                    t_ap, t_r = tmp[c % 2]
                    self.tt_(t_ap, xT.t[:, c, sl], rstd_ap, ALU.mult, [RX[c][tt], rstd_r], [t_r])
                    self.act(dst_fn(c), t_ap, AF.Identity, [t_r], [dst_res], bias=shift_fn(c), scale=gain_fn(c))

            def rms_all(sq, rstds, tmp, gain_fn, shift_fn, dst_fn, dst_res_fn):
                for tt in range(NTT):
                    sl = slice(tt * TT, (tt + 1) * TT)
                    pb = 4 + (tt % 4)
                    for c in range(KD):
                        q = sq[c % 2]
                        self.tt_(q.t[:], xT.t[:, c, sl], xT.t[:, c, sl], ALU.mult, [RX[c][tt]], [q.r])
                        self.mm(bank(pb), ones_d.t[:], q.t[:], c == 0, c == KD - 1, [ones_d.r, q.r], [RB[pb]])
                for tt in range(NTT):
                    pb = 4 + (tt % 4)
                    r_ = rstds[tt]
                    self.act(r_.t[:], bank(pb), AF.Ln, [RB[pb], epsc.r], [r_.r], bias=epsc.t[:], scale=1.0)
                    self.act(r_.t[:], r_.t[:], AF.Exp, [r_.r], [r_.r], scale=-0.5)
                k = 0
                for tt in range(NTT):
                    sl = slice(tt * TT, (tt + 1) * TT)
                    r_ = rstds[tt]
                    for c in range(KD):
                        t_ = tmp[k % len(tmp)]
                        k += 1
                        self.tt_(t_.t[:], xT.t[:, c, sl], r_.t[:], ALU.mult, [RX[c][tt], r_.r], [t_.r])
                        self.act(dst_fn(tt, c), t_.t[:], AF.Identity, [t_.r], [dst_res_fn(tt, c)], bias=shift_fn(c), scale=gain_fn(c))

            def resid_update(tt, m, ps_ap, ps_res, gate_ap, extra_reads=()):
                sl = slice(tt * TT, (tt + 1) * TT)
                self.stt(xT.t[:, m, sl], ps_ap, gate_ap, xT.t[:, m, sl], ALU.mult, ALU.add,
                         [ps_res, RX[m][tt]] + list(extra_reads), [RX[m][tt]])

            def ffn(l, f):
                s = 0 if f == 0 else 2
                with ExitStack() as st:
                    hT = self.sb(st, "hT", [128, KD, NT], BF16)
                    Rh = [Res("hT%d" % t) for t in range(NTT)]
                    gT = self.sb(st, "gT", [128, 6, NT], BF16)
                    Rg = [[Res("gT%d_%d" % (j, t)) for t in range(NTT)] for j in range(6)]
                    sil = [self.sb(st, "sil%d" % i, [128, TT]) for i in range(2)]
                    sqb = [self.sb(st, "sqb%d" % i, [128, TT], BF16) for i in range(2)]
                    rstds = [self.sb(st, "rstd%d" % i, [128, TT]) for i in range(NTT)]
                    ntm = [self.sb(st, "ntm%d" % i, [128, TT]) for i in range(3)]
                    rms_all(sqb, rstds, ntm, lambda c: mv[l].t[:, 3 * s, c:c + 1], lambda c: mv[l].t[:, 3 * s + 1, c:c + 1],
                            lambda tt, c: hT.t[:, c, tt * TT:(tt + 1) * TT], lambda tt, c: Rh[tt])
                    k = 0
                    for H in range(4):
                        nj = FPARTS[H]
                        for jj in range(nj):
                            j = FOFF[H] + jj
                            wv_, wr = self.wload(WA, w13[f][l, j], [128, 8, 256])
                            for tt in range(NTT):
                                sl = slice(tt * TT, (tt + 1) * TT)
                                ba, bu = (k % 2) * 2, (k % 2) * 2 + 1
                                k += 1
                                for kc in range(KD):
                                    self.mm(bank(ba), wv_[:, kc, 0:128], hT.t[:, kc, sl], kc == 0, kc == KD - 1,
                                            [wr, Rh[tt]], [RB[ba]])
                                for kc in range(KD):
                                    self.mm(bank(bu), wv_[:, kc, 128:256], hT.t[:, kc, sl], kc == 0, kc == KD - 1,
                                            [wr, Rh[tt]], [RB[bu]])
                                sb_ = sil[k % 2]
                                self.act(sb_.t[:], bank(ba), AF.Silu, [RB[ba]], [sb_.r])
                                self.tt_(gT.t[:, jj, sl], sb_.t[:], bank(bu), ALU.mult, [sb_.r, RB[bu]], [Rg[jj][tt]])
                        for m in range(KD):
                            wv_, wr = self.wload(WB, w2[f][l, H, m, :, 0:nj, :], [128, nj, 128])
                            for tt in range(NTT):
                                sl = slice(tt * TT, (tt + 1) * TT)
                                bo = 4 + (k % 4)
                                k += 1
                                for jj in range(nj):
                                    self.mm(bank(bo), wv_[:, jj, :], gT.t[:, jj, sl], jj == 0, jj == nj - 1,
                                            [wr, Rg[jj][tt]], [RB[bo]])
                                resid_update(tt, m, bank(bo), RB[bo], mv[l].t[:, 3 * s + 2, m:m + 1], [mv[l].r])
                S.barrier()

            def mixer(l, p):
                g0 = p * NT
                sm = small[l]
                with ExitStack() as st:
                    hbs = [self.sb(st, "hTt%d" % i, [128, KD, TT], BF16) for i in range(2 if NTT % 2 == 0 else 1)]
                    stage = [self.sb(st, "stage%d" % i, [128, 3 + TT]) for i in range(2)]
                    stage_c = [Res("stage_c0"), Res("stage_c1")]
                    carry = self.sb(st, "carry", [128, 12, 3])
                    xact = [self.sb(st, "xact%d" % i, [128, 12, TT], BF16) for i in range(2 if NTT % 2 == 0 else 1)]
                    RxaL = [[Res("xact%d_%d" % (i, c)) for c in range(12)] for i in range(2)]
                    qst = [self.sb(st, "qst%d" % i, [128, TT], BF16) for i in range(4)]
                    vpad = [self.sb(st, "vpad%d" % i, [128, 1024], BF16) for i in range(1)]
                    vst = [self.sb(st, "vst%d" % i, [128, 512], BF16) for i in range(1)]
                    szc4 = self.sb(st, "szc4", [128, 4, 1024], BF16)
                    x_tm = self.sb(st, "x_tm", [128, 1024], BF16)
                    B_tm = self.sb(st, "B_tm", [128, 256], BF16)
                    dtb = self.sb(st, "dtb", [128, 8, 16])
                    dt_ = self.sb(st, "dt", [128, 8, 16])
                    da = self.sb(st, "da", [128, 8, 16])
                    acum = self.sb(st, "acum", [128, 16])
                    ea = self.sb(st, "ea", [128, 16])
                    cd = self.sb(st, "cd", [128, 16])
                    dte = self.sb(st, "dte", [128, 16])
                    w2s = self.sb(st, "w2s", [128, 16])
                    afh = self.sb(st, "afh", [16, 128], BF16)
                    afl = self.sb(st, "afl", [16, 128], BF16)
                    dec = [self.sb(st, "dec%d" % i, [128, 4, 128], BF16) for i in range(2)]
                    CBm = self.sb(st, "CBm", [128, 2, 128])
                    MT = [self.sb(st, "MT%d" % i, [128, 4, 128], BF16) for i in range(2)]
                    xd = self.sb(st, "xd", [128, 1024], BF16)
                    xds = self.sb(st, "xds", [128, 1024], BF16)
                    t1 = self.sb(st, "t1", [128, 1024])
                    t1h = [Res("t1a"), Res("t1b")]
                    t2 = self.sb(st, "t2", [128, 1024])
                    ss = self.sb(st, "ss", [128, 2])
                    rs = self.sb(st, "rs", [128, 2])
                    yn = self.sb(st, "yn", [128, 1024], BF16)
                    yTs = self.sb(st, "yTs", [128, 8, TT], BF16)
                    prevf = self.sb(st, "prevf", [128, 1024])
                    prevb = self.sb(st, "prevb", [128, 1024], BF16)
                    ssmn = self.sb(st, "ssmn", [128, 1024])
                    scr = [(yn.t[:, 0:512], yn.r), (yn.t[:, 512:1024], yn.r), (t2.t[:, 0:512], t2.r),
                           (t1.t[:, 0:512], t1h[0]), (t1.t[:, 512:1024], t1h[1])]

                    if FL["ssd"]:
                        self.dma("sp", ssmn.t[:], ssmn_d[l], "d_sn", (), [ssmn.r])
                        if p == 0:
                            self.memset(prevf.t[:], 0.0, [prevf.r])
                            self.memset(carry.t[:], 0.0, [carry.r])
                        else:
                            self.dma("sp", prevf.t[:], sst[l], "d_st0", [R_st[l]], [prevf.r])
                            self.dma("sp", carry.t[:].rearrange("p a b -> p (a b)"), ctl[l], "d_st1", [R_st[l]], [carry.r])
                        self.cp(prevb.t[:], prevf.t[:], [prevf.r], [prevb.r], eng="act")
                    self.memset(vpad[0].t[:], 0.0, [vpad[0].r])

                    TP = 2 if NTT % 2 == 0 else 1
                    for gp in range(NTT // TP):
                        tiles = [gp * TP + ti for ti in range(TP)]
                        for ti, tt in enumerate(tiles):
                            sl = slice(tt * TT, (tt + 1) * TT)
                            for c in range(KD):
                                q_ap = yn.t[:, (c % 2) * 512:(c % 2 + 1) * 512]
                                self.tt_(q_ap, xT.t[:, c, sl], xT.t[:, c, sl], ALU.mult, [RX[c][tt]], [yn.r])
                                self.mm(bank(6 + ti), ones_d.t[:], q_ap, c == 0, c == KD - 1, [ones_d.r, yn.r], [RB[6 + ti]])
                        for ti, tt in enumerate(tiles):
                            r_ap = t2.t[:, ti * 512:(ti + 1) * 512]
                            self.act(r_ap, bank(6 + ti), AF.Ln, [RB[6 + ti], epsc.r], [t2.r], bias=epsc.t[:], scale=1.0)
                            self.act(r_ap, r_ap, AF.Exp, [t2.r], [t2.r], scale=-0.5)
                        for ti, tt in enumerate(tiles):
                            sl = slice(tt * TT, (tt + 1) * TT)
                            r_ap = t2.t[:, ti * 512:(ti + 1) * 512]
                            for c in range(KD):
                                t_ap = t1.t[:, (c % 2) * 512:(c % 2 + 1) * 512]
                                self.tt_(t_ap, xT.t[:, c, sl], r_ap, ALU.mult, [RX[c][tt], t2.r], [t1h[c % 2]])
                                self.act(hbs[ti].t[:, c, :], t_ap, AF.Identity, [t1h[c % 2]], [hbs[ti].r],
                                         bias=mv[l].t[:, 4, c:c + 1], scale=mv[l].t[:, 3, c:c + 1])
                        kb = 0
                        vpad2 = [(vpad[0].t[:], [vpad[0].r]), (stage[0].t[:, 0:512].bitcast(BF16), [stage[0].r, stage_c[0]])]
                        vst2 = [(vst[0].t[:], [vst[0].r]), (stage[1].t[:, 0:256].bitcast(BF16), [stage[1].r, stage_c[1]])]
                        if FL["att"]:
                            self.memset(vpad2[1][0], 0.0, vpad2[1][1])
                            for cc in range(16):
                                wv_, wr = self.wload(WA, wqk[l, cc], [128, 8, 128])
                                for ti, tt in enumerate(tiles):
                                    sl = slice(tt * TT, (tt + 1) * TT)
                                    gt0 = g0 + tt * TT
                                    hb = hbs[ti]
                                    b = kb % 4
                                    kb += 1
                                    for kc in range(KD):
                                        self.mm(bank(b), wv_[:, kc, :], hb.t[:, kc, :], kc == 0, kc == KD - 1,
                                                [wr, hb.r], [RB[b]])
                                    qs_ = qst[kb % 4]
                                    isq = cc < 4 or 8 <= cc < 12
                                    if kb % 2 == 0:
                                        self.act(qs_.t[:], bank(b), AF.Identity, [RB[b]], [qs_.r], scale=0.125 if isq else 1.0)
                                    else:
                                        self.ts(qs_.t[:], bank(b), 0.125 if isq else 1.0, ALU.mult, [RB[b]], [qs_.r])
                                    if cc < 4:
                                        self.dma("sp", qsb[cc, :, sl], qs_.t[:], "d_sq%d" % (kb % 4), [qs_.r], [R_q])
                                    elif cc < 8:
                                        self.dma("sp", ksb[l, cc - 4, :, gt0:gt0 + TT], qs_.t[:], "d_sq%d" % (kb % 4), [qs_.r], [R_kv[l]])
                                    elif cc < 12:
                                        self.dma("sp", qdf[cc - 8, :, sl], qs_.t[:], "d_sq%d" % (kb % 4), [qs_.r], [R_q])
                                    else:
                                        self.dma("sp", kdf[l, cc - 12, :, gt0:gt0 + TT], qs_.t[:], "d_sq%d" % (kb % 4), [qs_.r], [R_kv[l]])
                            for wi in range(2):
                                wv_, wr = self.wload(WA, wv[l, wi], [128, 8, 512])
                                for ti, tt in enumerate(tiles):
                                    gt0 = g0 + tt * TT
                                    hb = hbs[ti]
                                    for t4 in range(4):
                                        b = kb % 4
                                        kb += 1
                                        for kc in range(KD):
                                            self.mm(bank(b), hb.t[:, kc, t4 * 128:(t4 + 1) * 128], wv_[:, kc, :],
                                                    kc == 0, kc == KD - 1, [wr, hb.r], [RB[b]])
                                        r0 = gt0 + t4 * 128
                                        if wi == 0:
                                            vp_ap, vp_r = vpad2[t4 % 2]
                                            src = bank(b).rearrange("p (h e d) -> p h e d", h=4, e=2)
                                            dst = vp_ap.rearrange("p (h e x d) -> p h e x d", h=4, e=2, x=2)
                                            self.cp(dst[:, :, 0, 0, :], src[:, :, 0, :], [RB[b]], vp_r, eng="act")
                                            self.cp(dst[:, :, 1, 1, :], src[:, :, 1, :], [RB[b]], vp_r, eng="dve")
                                            self.dma("sp", vsb[l, r0:r0 + 128, :], vp_ap, "d_svp%d" % (t4 % 2), vp_r, [R_kv[l]])
                                        else:
                                            vs_ap, vs_r = vst2[t4 % 2]
                                            self.cp(vs_ap, bank(b), [RB[b]], vs_r, eng="act")
                                            self.dma("sp", vdf[l, r0:r0 + 128, :], vs_ap, "d_svs%d" % (t4 % 2), vs_r, [R_kv[l]])
                        if not FL["ssd"]:
                            continue
                        xun = [(c, ti) for c in range(12) for ti in range(TP)]
                        xb_bank = {}
                        xw_ = {}

                        def xP(u):
                            c, ti = xun[u]
                            if ti == 0:
                                xw_[c] = self.wload(WA, wx[l, c], [128, 8, 128])
                            wv_, wr = xw_[c]
                            hb = hbs[ti]
                            b = (kb + u) % 4
                            xb_bank[u] = b
                            for kc in range(KD):
                                self.mm(bank(b), wv_[:, kc, :], hb.t[:, kc, :], kc == 0, kc == KD - 1, [wr, hb.r], [RB[b]])

                        def xE(u):
                            c, ti = xun[u]
                            b = xb_bank[u]
                            sg = stage[u % 2]
                            self.cp(sg.t[:, 3:3 + TT], bank(b), [RB[b]], [sg.r], eng="act")
                            self.cp(sg.t[:, 0:3], carry.t[:, c, :], [carry.r], [stage_c[u % 2]])
                            self.cp(carry.t[:, c, :], sg.t[:, TT:TT + 3], [sg.r], [carry.r])

                        def xV(u):
                            c, ti = xun[u]
                            sg = stage[u % 2]
                            ct_ap = t1.t[:, (u % 2) * 512:(u % 2 + 1) * 512]
                            cw = C_CONVW + c * 4
                            self.act(ct_ap, sg.t[:, 0:TT], AF.Identity, [sg.r, stage_c[u % 2], sm.r], [t1h[u % 2]],
                                     scale=sm.t[:, cw:cw + 1])
                            for kk in range(1, 4):
                                self.stt(ct_ap, sg.t[:, kk:kk + TT], sm.t[:, cw + kk:cw + kk + 1], ct_ap,
                                         ALU.mult, ALU.add, [sg.r, stage_c[u % 2], t1h[u % 2], sm.r], [t1h[u % 2]])

                        def xS(u):
                            c, ti = xun[u]
                            ct_ap = t1.t[:, (u % 2) * 512:(u % 2 + 1) * 512]
                            self.act(xact[ti].t[:, c, :], ct_ap, AF.Silu, [t1h[u % 2], sm.r], [RxaL[ti][c]],
                                     bias=sm.t[:, C_CONVB + c:C_CONVB + c + 1], scale=1.0)

                        for k_ in range(len(xun) + 2):
                            for stg, sk in ((xP, 0), (xE, 0), (xV, 1), (xS, 2)):
                                u_ = k_ - sk
                                if 0 <= u_ < len(xun):
                                    stg(u_)
                        kb += len(xun)
                        wv_, wr = self.wload(WA, wdt[l], [128, 8, 16])
                        nch = 4 * TP
                        for ti in range(TP):
                            hb = hbs[ti]
                            for t4 in range(4):
                                t4g = ti * 4 + t4
                                for kc in range(KD):
                                    self.mm(bank(0)[:, t4g * 16:(t4g + 1) * 16], hb.t[:, kc, t4 * 128:(t4 + 1) * 128],
                                            wv_[:, kc, :], kc == 0, kc == KD - 1, [wr, hb.r], [RB[0]])
                        self.tt_(dtb.t[:, 0:nch, :], bank(0)[:, 0:16 * nch].rearrange("p (a b) -> p a b", a=nch),
                                 sm.t[:, C_DTB:C_DTB + 16].unsqueeze(1).to_broadcast([128, nch, 16]), ALU.add,
                                 [RB[0], sm.r], [dtb.r])
                        self.act(dtb.t[:, 0:nch, :], dtb.t[:, 0:nch, :], AF.Exp, [dtb.r], [dtb.r])
                        self.act(dt_.t[:, 0:nch, :], dtb.t[:, 0:nch, :], AF.Ln, [dtb.r], [dt_.r], bias=1.0, scale=1.0)
                        self.tt_(da.t[:, 0:nch, :], dt_.t[:, 0:nch, :], abc[l].t[:].unsqueeze(1).to_broadcast([128, nch, 16]), ALU.mult,
                                 [dt_.r, abc[l].r], [da.r])
                        for ti, tt in enumerate(tiles):
                            sl = slice(tt * TT, (tt + 1) * TT)
                            hb = hbs[ti]
                            xa_ = xact[ti]
                            Rxa_ = RxaL[ti]
                            wz_ = [self.wload(WA, wz[l, hf], [128, 8, 512]) for hf in range(2)]
                            for t4 in range(4):
                                for hf in range(2):
                                    wzv, wzr = wz_[hf]
                                    bz_ = 4 + ((t4 * 2 + hf) % 4)
                                    for kc in range(KD):
                                        self.mm(bank(bz_), hb.t[:, kc, t4 * 128:(t4 + 1) * 128], wzv[:, kc, :], kc == 0, kc == KD - 1,
                                                [wzr, hb.r], [RB[bz_]])
                                    self.act(szc4.t[:, t4, hf * 512:(hf + 1) * 512], bank(bz_), AF.Silu, [RB[bz_]], [szc4.r])
                            for t4 in range(4):
                                tsl = slice(t4 * 128, (t4 + 1) * 128)
                                t4g = ti * 4 + t4
                                pb = bankb(3)
                                for c in range(8):
                                    self.tr(pb[:, c * 128:(c + 1) * 128], xa_.t[:, c, tsl], identb.t[:], [Rxa_[c], identb.r], [RB[3]])
                                self.cp(x_tm.t[:, 0:512], pb[:, 0:512], [RB[3]], [x_tm.r], eng="act")
                                self.cp(x_tm.t[:, 512:1024], pb[:, 512:1024], [RB[3]], [x_tm.r], eng="dve")
                                for c in range(2):
                                    self.tr(pb[:, c * 128:(c + 1) * 128], xa_.t[:, 8 + c, tsl], identb.t[:], [Rxa_[8 + c], identb.r], [RB[3]])
                                self.cp(B_tm.t[:], pb[:, 0:256], [RB[3]], [B_tm.r], eng="act")
                                pA = bank(0)
                                self.mm(pA[:, 64:80], U32.t[:], da.t[:, t4g, :], True, True, [U32.r, da.r], [RB[0]])
                                self.mm(pA[:, 80:96], onesf.t[:], da.t[:, t4g, :], True, True, [onesf.r, da.r], [RB[0]])
                                self.mm(pA[0:16, 128:256], da.t[:, t4g, :], U32.t[:], True, True, [U32.r, da.r], [RB[0]])
                                self.cp(acum.t[:], pA[:, 64:80], [RB[0]], [acum.r])
                                self.act(ea.t[:], pA[:, 64:80], AF.Exp, [RB[0]], [ea.r])
                                self.act(cd.t[:], pA[:, 80:96], AF.Exp, [RB[0]], [cd.r])
                                self.tt_(dte.t[:], pA[:, 80:96], acum.t[:], ALU.subtract, [RB[0], acum.r], [dte.r])
                                self.act(dte.t[:], dte.t[:], AF.Exp, [dte.r], [dte.r])
                                self.cp(afh.t[:], pA[0:16, 128:256], [RB[0]], [afh.r])
                                self.tt_(afl.t[:], pA[0:16, 128:256], afh.t[:], ALU.subtract, [RB[0], afh.r], [afl.r])
                                for g in range(2):
                                    self.mm(pA[:, 256 + g * 128:256 + (g + 1) * 128], xa_.t[:, 8 + g, tsl], xa_.t[:, 10 + g, tsl],
                                            True, True, [Rxa_[8 + g], Rxa_[10 + g]], [RB[0]])
                                self.tt_(CBm.t[:], pA[:, 256:512].rearrange("p (g l) -> p g l", g=2),
                                         U32.t[:].unsqueeze(1).to_broadcast([128, 2, 128]), ALU.mult,
                                         [RB[0], U32.r], [CBm.r])
                                xv = x_tm.t[:].rearrange("p (h d) -> p h d", h=16)
                                self.tt_(xd.t[:].rearrange("p (h d) -> p h d", h=16), xv,
                                         dt_.t[:, t4g, :].unsqueeze(2).to_broadcast([128, 16, 64]), ALU.mult,
                                         [x_tm.r, dt_.r], [xd.r])
                                self.tt_(w2s.t[:], dt_.t[:, t4g, :], dte.t[:], ALU.mult, [dt_.r, dte.r], [w2s.r])
                                self.tt_(xds.t[:].rearrange("p (h d) -> p h d", h=16), xv,
                                         w2s.t[:].unsqueeze(2).to_broadcast([128, 16, 64]), ALU.mult,
                                         [x_tm.r, w2s.r], [xds.r])
                                pY = bank2(4)

                                def r1_(rd):
                                    b = 1 + (rd % 2)
                                    pS = bank(b)
                                    for hh in range(4):
                                        h = rd * 4 + hh
                                        o_ = pS[:, hh * 128:(hh + 1) * 128]
                                        self.mm(o_, sel(h), afh.t[:], True, False, [afh.r, identb.r], [RB[b]])
                                        self.mm(o_, sel(h), afl.t[:], False, False, [afl.r, identb.r], [RB[b]])
                                        self.mm(o_, afh.t[:], nsel(h), False, False, [afh.r, nidentb.r], [RB[b]])
                                        self.mm(o_, afl.t[:], nsel(h), False, True, [afl.r, nidentb.r], [RB[b]])

                                def r2_(rd):
                                    b = 1 + (rd % 2)
                                    dc_, mt_ = dec[rd % 2], MT[rd % 2]
                                    self.act(bank(b), bank(b), AF.Relu, [RB[b]], [RB[b]], scale=-1.0)
                                    self.act(dc_.t[:].rearrange("p a b -> p (a b)"), bank(b), AF.Exp, [RB[b]], [dc_.r], scale=-1.0)
                                    g = rd // 2
                                    self.stt(mt_.t[:], dc_.t[:], 1.0, CBm.t[:, g, :].unsqueeze(1).to_broadcast([128, 4, 128]),
                                             ALU.min, ALU.mult, [dc_.r, CBm.r], [mt_.r])

                                def r3_(rd):
                                    mt_ = MT[rd % 2]
                                    for hh in range(4):
                                        h = rd * 4 + hh
                                        self.mm(pY[:, h * 64:(h + 1) * 64], mt_.t[:, hh, :], xd.t[:, h * 64:(h + 1) * 64],
                                                True, True, [mt_.r, xd.r], [RB[4 + h // 8]])

                                for k_ in range(4 + 2):
                                    for stg, sk in ((r1_, 0), (r2_, 1), (r3_, 2)):
                                        u_ = k_ - sk
                                        if 0 <= u_ < 4:
                                            stg(u_)
                                pO = bank2(6)
                                for g in range(2):
                                    self.mm(pO[:, g * 512:(g + 1) * 512], xa_.t[:, 10 + g, tsl], prevb.t[:, g * 512:(g + 1) * 512],
                                            True, True, [Rxa_[10 + g], prevb.r], [RB[6 + g]])
                                pSt = bank2(1)
                                for g in range(2):
                                    self.mm(pSt[:, g * 512:(g + 1) * 512], B_tm.t[:, g * 128:(g + 1) * 128],
                                            xds.t[:, g * 512:(g + 1) * 512], True, True, [B_tm.r, xds.r], [RB[1 + g]])
                                self.tt_(t1.t[:].rearrange("p (h d) -> p h d", h=16), pO.rearrange("p (h d) -> p h d", h=16),
                                         ea.t[:].unsqueeze(2).to_broadcast([128, 16, 64]), ALU.mult,
                                         [RB[6], RB[7], ea.r], [t1h[0], t1h[1]])
                                self.tt_(t1.t[:], t1.t[:], pY, ALU.add, [t1h[0], t1h[1], RB[4], RB[5]], [t1h[0], t1h[1]])
                                self.tt_(t2.t[:].rearrange("p (h d) -> p h d", h=16), xv,
                                         sm.t[:, C_DSK:C_DSK + 16].unsqueeze(2).to_broadcast([128, 16, 64]), ALU.mult,
                                         [x_tm.r, sm.r], [t2.r])
                                self.tt_(t1.t[:], t1.t[:], t2.t[:], ALU.add, [t1h[0], t1h[1], t2.r], [t1h[0], t1h[1]])
                                self.tt_(t2.t[:], t1.t[:], szc4.t[:, t4, :], ALU.mult, [t1h[0], t1h[1], szc4.r], [t2.r])
                                for g in range(2):
                                    self.act(yn.t[:, g * 512:(g + 1) * 512], t2.t[:, g * 512:(g + 1) * 512], AF.Square, [t2.r], [yn.r, ss.r],
                                             accum_out=ss.t[:, g:g + 1])
                                self.act(rs.t[:], ss.t[:], AF.Ln, [ss.r, epsc.r], [rs.r], bias=epsc.t[:], scale=1.0 / 512.0)
                                self.act(rs.t[:], rs.t[:], AF.Exp, [rs.r], [rs.r], scale=-0.5)
                                for g in range(2):
                                    gs = slice(g * 512, (g + 1) * 512)
                                    self.stt(yn.t[:, gs], t2.t[:, gs], rs.t[:, g:g + 1], ssmn.t[:, gs],
                                             ALU.mult, ALU.mult, [t2.r, rs.r, ssmn.r], [yn.r])
                                pv = prevf.t[:].rearrange("p (h d) -> p h d", h=16)
                                self.tt_(pv, pv, cd.t[:].unsqueeze(2).to_broadcast([128, 16, 64]), ALU.mult, [prevf.r, cd.r], [prevf.r])
                                self.tt_(prevf.t[:], prevf.t[:], pSt, ALU.add, [prevf.r, RB[1], RB[2]], [prevf.r])
                                self.cp(prevb.t[:], prevf.t[:], [prevf.r], [prevb.r], eng="act")
                                for c8 in range(8):
                                    self.tr(pb[:, c8 * 128:(c8 + 1) * 128], yn.t[:, c8 * 128:(c8 + 1) * 128], identb.t[:],
                                            [yn.r, identb.r], [RB[3]])
                                self.cp(yTs.t[:, :, tsl], pb.rearrange("p (c t) -> p c t", c=8), [RB[3]], [yTs.r], eng="act")
                            for mh in range(2):
                                wov, wor = self.wload(WA, wo[l, 0, mh], [128, 8, 512])
                                for mm_ in range(4):
                                    m = mh * 4 + mm_
                                    b = 4 + (m % 4)
                                    for fc in range(8):
                                        self.mm(bank(b), wov[:, fc, mm_ * 128:(mm_ + 1) * 128], yTs.t[:, fc, :], fc == 0, fc == 7,
                                                [wor, yTs.r], [RB[b]])
                                    resid_update(tt, m, bank(b), RB[b], mv[l].t[:, 5, m:m + 1], [mv[l].r])
                    if FL["ssd"] and p < NPASS - 1:
                        self.dma("sp", sst[l], prevf.t[:], "d_ss0", [prevf.r], [R_st[l]])
                        self.dma("sp", ctl[l], carry.t[:].rearrange("p a b -> p (a b)"), "d_ss1", [carry.r], [R_st[l]])
                S.barrier()
                if not FL["att"]:
                    return
                S.barrier(engs=("pool",))
                with ExitStack() as st:
                    nkmax = g0 + NT
                    qq = self.sb(st, "qq", [128, 4, TT], BF16)
                    ksl = [self.sb(st, "ksl%d" % i, [128, nkmax], BF16) for i in range(2)]
                    vsl = [self.sb(st, "vsl%d" % i, [128, nkmax // 128, 256], BF16) for i in range(2)]
                    SS = self.sb(st, "SS", [128, 2, TT])
                    SSb2 = [self.sb(st, "SSb%d" % i, [128, 2, TT], BF16) for i in range(2)]
                    SSb = SSb2[0]
                    etmp = [self.sb(st, "etmp%d" % i, [128, 2, TT]) for i in range(2)]
                    spb = [self.sb(st, "spb%d" % i, [128, 2, TT], BF16) for i in range(3)]
                    Wt = [self.sb(st, "Wt%d" % i, [128, 2, TT], BF16) for i in range(3)]
                    Esum = self.sb(st, "Esum", [128, 2, TT])
                    REs = [Res("Esum0"), Res("Esum1")]
                    yTa = self.sb(st, "yTa", [128, 8, TT], BF16)
                    RyT = [Res("yTa%d" % i) for i in range(8)]
                    D0b = self.sb(st, "D0b", [128, 4, 128], BF16)
                    D1b = self.sb(st, "D1b", [128, 4, 128], BF16)
                    self.cp(D0b.t[:], D0.t[:], [D0.r], [D0b.r])
                    self.cp(D1b.t[:], D1.t[:], [D1.r], [D1b.r])
                    Eb = Wt
                    kvi = 0
                    ZP = (0, 4, 6)

                    def pair(b, cs):
                        return PS[:, b:b + 2, cs]

                    def pipeline(n, stages, skews):
                        for k in range(n + max(skews)):
                            for stg, sk in zip(stages, skews):
                                u = k - sk
                                if 0 <= u < n:
                                    stg(u)

                    mlt2 = m_lt.t[:].unsqueeze(1).to_broadcast([128, 2, 128])
                    for tt in range(NTT):
                        sl = slice(tt * TT, (tt + 1) * TT)
                        gt0 = g0 + tt * TT
                        i0 = gt0 // 128
                        Jmax = i0 + 3
                        nk = (Jmax + 1) * 128
                        nJ = Jmax + 1

                        def load_kv(kind, idx, slot):
                            ks_, vs_ = ksl[slot], vsl[slot]
                            if kind == 0:
                                self.dma("sp", ks_.t[:, 0:nk], ksb[l, idx, :, 0:nk], "d_k%d" % slot, [R_kv[l]], [ks_.r])
                                self.dma("sp", vs_.t[:, 0:nJ, :],
                                         vsb[l, 0:nk, idx * 256:(idx + 1) * 256].rearrange("(j p) f -> p j f", p=128),
                                         "d_v%d" % slot, [R_kv[l]], [vs_.r])
                            else:
                                self.dma("sp", ks_.t[:, 0:nk], kdf[l, idx, :, 0:nk], "d_k%d" % slot, [R_kv[l]], [ks_.r])
                                self.dma("sp", vs_.t[:, 0:nJ, 0:128],
                                         vdf[l, 0:nk, idx * 128:(idx + 1) * 128].rearrange("(j p) f -> p j f", p=128),
                                         "d_v%d" % slot, [R_kv[l]], [vs_.r])

                        self.dma("sp", qq.t[:], qsb[:, :, sl].rearrange("c p t -> p c t"), "d_qs", [R_q], [qq.r])
                        units = [(c, J) for c in range(4) for J in range(Jmax, -1, -1)]
                        slot_of = {c: (kvi + c) % 2 for c in range(4)}
                        load_kv(0, 0, slot_of[0])

                        def geom(u):
                            c, J = units[u]
                            d = J - i0
                            c0 = max(d, 0) * 128
                            return c, J, d, c0, slice(c0, TT)

                        def sb1(u):
                            c, J, d, c0, cs = geom(u)
                            ks_ = ksl[slot_of[c]]
                            b = ZP[u % 3]
                            for e_ in range(2):
                                ps_ = slice(e_ * 64, (e_ + 1) * 64)
                                self.mm(bank(b + e_)[:, cs], ks_.t[ps_, J * 128:(J + 1) * 128], qq.t[ps_, c, cs], True, True,
                                        [ks_.r, qq.r], [RB[b + e_]])

                        def sb2(u):
                            c, J, d, c0, cs = geom(u)
                            b = ZP[u % 3]
                            et, sp_ = etmp[u % 2], spb[u % 3]
                            self.act(et.t[:, :, cs], pair(b, cs), AF.Exp, [RB[b], RB[b + 1]], [et.r])
                            self.act(sp_.t[:, :, cs], et.t[:, :, cs], AF.Ln, [et.r], [sp_.r], bias=1.0, scale=1.0)
                            if d >= 0:
                                dsl = slice(c0, c0 + 128)
                                self.tt_(sp_.t[:, :, dsl], sp_.t[:, :, dsl], mlt2, ALU.mult, [sp_.r, m_lt.r], [sp_.r])

                        def sb3(u):
                            c, J, d, c0, cs = geom(u)
                            b = ZP[u % 3]
                            sp_ = spb[u % 3]
                            for e_ in range(2):
                                self.mm(bank(b + e_)[:, cs], triN.t[:], sp_.t[:, e_, cs], False, True, [triN.r, sp_.r], [RB[b + e_]], skip=True)
                                if J < Jmax:
                                    ssb_ = SSb2[u % 2]
                                    self.mm(bank(b + e_)[:, cs], negones.t[:], ssb_.t[:, e_, cs], False, True, [negones.r, ssb_.r],
                                            [RB[b + e_]], skip=True)

                        def sbU(u):
                            c, J, d, c0, cs = geom(u)
                            sp_ = spb[u % 3]
                            if J == 0:
                                return
                            nxt = SSb2[(u + 1) % 2]
                            if J == Jmax:
                                self.memset(SS.t[:], 0.0, [SS.r])
                            self.tt_(SS.t[:, :, cs], SS.t[:, :, cs], sp_.t[:, :, cs], ALU.add, [SS.r, sp_.r], [SS.r])
                            self.cp(nxt.t[:], SS.t[:], [SS.r], [nxt.r])

                        def sb4(u):
                            c, J, d, c0, cs = geom(u)
                            b = ZP[u % 3]
                            wt_ = Wt[u % 3]
                            self.act(wt_.t[:, :, cs], pair(b, cs), AF.Exp, [RB[b], RB[b + 1]], [wt_.r])
                            if d >= 0:
                                dsl = slice(c0, c0 + 128)
                                self.tt_(wt_.t[:, :, dsl], wt_.t[:, :, dsl], mlt2, ALU.mult, [wt_.r, m_lt.r], [wt_.r])

                        def sb5(u):
                            c, J, d, c0, cs = geom(u)
                            wt_, sp_ = Wt[u % 3], spb[u % 3]
                            vs_ = vsl[slot_of[c]]
                            bo = 2 + (c % 2)
                            for e_ in range(2):
                                first = (e_ == 0 and J == Jmax)
                                self.mm(bank(bo)[:, cs], vs_.t[:, J, e_ * 128:(e_ + 1) * 128], wt_.t[:, e_, cs], first, True,
                                        [vs_.r, wt_.r], [RB[bo]], skip=not first)
                            if J == Jmax and c + 1 < 4:
                                load_kv(0, c + 1, slot_of[c + 1])
                            if J == 0:
                                self.cp(yTa.t[:, c, :], bank(bo), [RB[bo]], [RyT[c]], eng="dve")

                        pipeline(len(units), (sb1, sb2, sb3, sbU, sb4, sb5), (0, 0, 1, 1, 1, 2))
                        kvi += 4

                        self.dma("sp", qq.t[:], qdf[:, :, sl].rearrange("c p t -> p c t"), "d_qs", [R_q], [qq.r])
                        dunits = [(h, J) for h in range(4) for J in range(0, Jmax + 1)]
                        dslot = {h: (kvi + h) % 2 for h in range(4)}
                        load_kv(1, 0, dslot[0])
                        tS = SS.t[:].rearrange("p a (b c) -> p (a b) c", b=2)[:, 0:2, :]
                        r1, r2 = etmp[0].t[:, 0, :], etmp[0].t[:, 1, :]
                        ot, ou = etmp[1].t[:, 0, :], etmp[1].t[:, 1, :]
                        rsd = SS.t[:, 1, :]
                        sqd = SSb.t[:, 0, :]

                        def dgeom(u):
                            h, J = dunits[u]
                            d = J - i0
                            c0 = max(d, 0) * 128
                            return h, J, d, c0, slice(c0, TT)

                        DP = (0, 4)

                        def df1(u):
                            h, J, d, c0, cs = dgeom(u)
                            ks_ = ksl[dslot[h]]
                            b = DP[u % 2]
                            for m_ in range(2):
                                ps_ = slice(m_ * 64, (m_ + 1) * 64)
                                self.mm(bank(b + m_)[:, cs], ks_.t[ps_, J * 128:(J + 1) * 128], qq.t[ps_, h, cs], True, True,
                                        [ks_.r, qq.r], [RB[b + m_]])
                                if d >= 0:
                                    self.mm(bank(b + m_)[:, c0:c0 + 128], identb.t[:], D0b.t[:, h, :], False, True,
                                            [identb.r, D0b.r], [RB[b + m_]], skip=True)
                                    if c0 + 256 <= TT:
                                        self.mm(bank(b + m_)[:, c0 + 128:c0 + 256], identb.t[:], D1b.t[:, h, :], False, True,
                                                [identb.r, D1b.r], [RB[b + m_]], skip=True)
                                elif d == -1:
                                    self.mm(bank(b + m_)[:, 0:128], identb.t[:], D1b.t[:, h, :], False, True,
                                            [identb.r, D1b.r], [RB[b + m_]], skip=True)

                        def df2(u):
                            h, J, d, c0, cs = dgeom(u)
                            b = DP[u % 2]
                            eb = Eb[u % 3]
                            self.act(eb.t[:, :, cs], pair(b, cs), AF.Exp, [RB[b], RB[b + 1]], [eb.r])

                        def df3(u):
                            h, J, d, c0, cs = dgeom(u)
                            eb = Eb[u % 3]
                            vs_ = vsl[dslot[h]]
                            for m_ in range(2):
                                self.mm(bank(2 + m_)[:, cs], vs_.t[:, J, 0:128], eb.t[:, m_, cs], J == 0, True,
                                        [vs_.r, eb.r], [RB[2 + m_]], skip=J > 0)
                            if J == 0:
                                self.cp(Esum.t[:, 0, :], eb.t[:, 0, :], [eb.r], [REs[0]])
                            else:
                                self.tt_(Esum.t[:, 0, cs], Esum.t[:, 0, cs], eb.t[:, 0, cs], ALU.add, [eb.r, REs[0]], [REs[0]])
                            self.mm(bank(7)[:, cs], onesb.t[:], eb.t[:, 1, cs], J == 0, True, [onesb.r, eb.r], [RB[7]], skip=J > 0)
                            if J == 0 and h + 1 < 4:
                                load_kv(1, h + 1, dslot[h + 1])
                            if J == Jmax:
                                self.mm(bank(6), onesf.t[:], Esum.t[:, 0, :], True, True, [onesf.r, REs[0]], [RB[6]])

                        def df4(u):
                            h, J, d, c0, cs = dgeom(u)
                            if J != Jmax:
                                return
                            bs = 6
                            self.act(r1, bank(6), AF.Ln, [RB[6]], [etmp[0].r])
                            self.act(r1, r1, AF.Exp, [etmp[0].r], [etmp[0].r], scale=-1.0)
                            self.act(r2, bank(7), AF.Ln, [RB[7]], [etmp[0].r])
                            self.act(r2, r2, AF.Exp, [etmp[0].r], [etmp[0].r], scale=-1.0)
                            self.tt_(ot, bank(2), r1, ALU.mult, [RB[2], etmp[0].r], [etmp[1].r])
                            self.tt_(ou, bank(3), r2, ALU.mult, [RB[3], etmp[0].r], [etmp[1].r])
                            self.stt(ot, ou, neglam[l].t[:, 0:1], ot, ALU.mult, ALU.add, [etmp[1].r, neglam[l].r], [etmp[1].r])
                            self.tt_(sqd, ot, ot, ALU.mult, [etmp[1].r], [SSb.r])
                            self.mm(bank(bs), ones_v.t[:], sqd, True, True, [ones_v.r, SSb.r], [RB[bs]])
                            self.act(rsd, bank(bs), AF.Ln, [RB[bs], epsc.r], [SS.r], bias=epsc.t[:], scale=1.0)
                            self.act(rsd, rsd, AF.Exp, [SS.r], [SS.r], scale=-0.5)
                            self.stt(yTa.t[:, 4 + h, :], ot, subg[l].t[:, 0:1], rsd, ALU.mult, ALU.mult,
                                     [etmp[1].r, SS.r, subg[l].r], [RyT[4 + h]])

                        pipeline(len(dunits), (df1, df2, df3, df4), (0, 0, 2, 2))
                        kvi += 4
                        for mh in range(2):
                            wov, wor = self.wload(WA, wo[l, 1, mh], [128, 8, 512])
                            for mm_ in range(4):
                                m = mh * 4 + mm_
                                b = 6 + (m % 2)
                                for fc in range(8):
                                    self.mm(bank(b), wov[:, fc, mm_ * 128:(mm_ + 1) * 128], yTa.t[:, fc, :], fc == 0, fc == 7,
                                            [wor, RyT[fc]], [RB[b]])
                                resid_update(tt, m, bank(b), RB[b], mv[l].t[:, 5, m:m + 1], [mv[l].r])
                S.barrier()

            for p in range(NPASS):
                with ExitStack() as st:
                    xl = [self.sb(st, "xl%d" % i, [128, D]) for i in range(2)]
                    for t16 in range(NT // 128):
                        xb_ = xl[t16 % 2]
                        r0 = p * NT + t16 * 128
                        self.dma("sp", xb_.t[:], xin[r0:r0 + 128, :], "d_x%d" % (t16 % 2), (), [xb_.r])
                        tt = t16 // 4
                        for half in range(2):
                            b = (t16 * 2 + half) % 4
                            for i in range(4):
                                c = half * 4 + i
                                self.tr(bank(b)[:, i * 128:(i + 1) * 128], xb_.t[:, c * 128:(c + 1) * 128], identf.t[:],
                                        [xb_.r, identf.r], [RB[b]])
                            wr_ = [RX[half * 4 + i][tt] for i in range(4)]
                            self.cp(xT.t[:, half * 4:half * 4 + 4, t16 * 128:(t16 + 1) * 128],
                                    bank(b).rearrange("p (c t) -> p c t", c=4), [RB[b]], wr_,
                                    eng="act" if half == 0 else "dve")
                S.barrier()
                for l in range(self.layers):
                    if FL["ffn1"]:
                        ffn(l, 0)
                    if FL["ssd"] or FL["att"]:
                        mixer(l, p)
                    if FL["ffn2"]:
                        ffn(l, 1)
                with ExitStack() as st:
                    osb = [self.sb(st, "osb%d" % i, [128, D]) for i in range(2)]
                    sqb = [self.sb(st, "fsq%d" % i, [128, TT], BF16) for i in range(2)]
                    rstds = [self.sb(st, "frstd%d" % i, [128, TT]) for i in range(NTT)]
                    ntm = [self.sb(st, "fnt%d" % i, [128, TT]) for i in range(3)]
                    if FL["fn"]:
                        rms_all(sqb, rstds, ntm, lambda c: gsm.t[:, G_FN + c:G_FN + c + 1], lambda c: 0.0,
                                lambda tt, c: xT.t[:, c, tt * TT:(tt + 1) * TT], lambda tt, c: RX[c][tt])
                    for tt in range(NTT):
                        for t4 in range(4):
                            ob = osb[t4 % 2]
                            c0_ = tt * TT + t4 * 128
                            for half in range(2):
                                b = (t4 * 2 + half) % 4
                                for i in range(4):
                                    c = half * 4 + i
                                    self.tr(bank(b)[:, i * 128:(i + 1) * 128], xT.t[:, c, c0_:c0_ + 128], identf.t[:],
                                            [RX[c][tt], identf.r], [RB[b]])
                                self.cp(ob.t[:, half * 512:(half + 1) * 512], bank(b), [RB[b]], [ob.r],
                                        eng="act" if half == 0 else "dve")
                            r0 = p * NT + tt * TT + t4 * 128
                            self.dma("sp", out_d[r0:r0 + 128, :], ob.t[:], "d_out%d" % (t4 % 2), [ob.r], [R_out])
                S.barrier()
            S.wait_all("sp", [k for k in S.cnt if k.startswith("d_out") or k.startswith("d_s")])
            S.emit()
        return nc


def _t5_bucket_np(dist):
    max_exact = 16
    d = np.maximum(dist.astype(np.float32), np.float32(max_exact))
    large = max_exact + (np.log(d / np.float32(max_exact)) / np.float32(math.log(128 / max_exact))
                         * np.float32(32 - max_exact)).astype(np.int32)
    large = np.minimum(large, 31)
    return np.where(dist < max_exact, dist, large)


def _bucket_tiles():
    k = np.arange(128)[:, None]
    q = np.arange(128)[None, :]
    d0 = q - k
    b0 = np.where(d0 >= 0, _t5_bucket_np(np.maximum(d0, 0)), 32).astype(np.float32)
    d1 = q - k + 128
    b1 = _t5_bucket_np(d1).astype(np.float32)
    return b0, b1


def pack_weights(inp):
    f = np.float32
    L = DEPTH
    A = lambda a: np.ascontiguousarray(a, dtype=f)

    def kblk(w):
        return w.reshape(8, 128, -1).transpose(1, 0, 2)

    out = {}
    aw = inp["ada_w"]
    out["adaw"] = A(np.stack([np.stack([kblk(aw[l][:, b * 512:(b + 1) * 512]) for b in range(18)]) for l in range(L)]))
    for fi, (n13, n2) in enumerate((("ffn1_w13", "ffn1_w2"), ("ffn2_w13", "ffn2_w2")), start=1):
        w13 = inp[n13]
        blocks = []
        for l in range(L):
            bl = []
            for j in range(NJ):
                a = kblk(w13[l][:, j * 128:(j + 1) * 128])
                u = kblk(w13[l][:, DFF + j * 128:DFF + (j + 1) * 128])
                bl.append(np.concatenate([a, u], axis=2))
            blocks.append(np.stack(bl))
        out["w13_%d" % fi] = A(np.stack(blocks))
        w2 = inp[n2]
        t = np.zeros((L, 4, 8, 128, 6, 128), f)
        for l in range(L):
            for H in range(4):
                for jj in range(FPARTS[H]):
                    j = FOFF[H] + jj
                    t[l, H, :, :, jj, :] = w2[l][j * 128:(j + 1) * 128, :].reshape(128, 8, 128).transpose(1, 0, 2)
        out["w2_%d" % fi] = t
    wi = inp["w_in"]
    out["wz"] = A(np.stack([np.stack([kblk(wi[l][:, h * 512:(h + 1) * 512]) for h in range(2)]) for l in range(L)]))
    out["wx"] = A(np.stack([np.stack([kblk(wi[l][:, 1024 + c * 128:1024 + (c + 1) * 128]) for c in range(12)]) for l in range(L)]))
    out["wdt"] = A(np.stack([kblk(wi[l][:, 2560:2576]) for l in range(L)]))
    qk_off = [2576 + i * 128 for i in range(4)] + [3088 + i * 128 for i in range(4)] + \
             [4112 + i * 128 for i in range(4)] + [4624 + i * 128 for i in range(4)]
    out["wqk"] = A(np.stack([np.stack([kblk(wi[l][:, o:o + 128]) for o in qk_off]) for l in range(L)]))
    out["wv"] = A(np.stack([np.stack([kblk(wi[l][:, 3600:4112]), kblk(wi[l][:, 5136:5648])]) for l in range(L)]))
    wo = inp["w_out"]
    out["wo"] = A(np.stack([np.stack([np.stack([kblk(wo[l][pt * 1024:(pt + 1) * 1024, mh * 512:(mh + 1) * 512])
                                                for mh in range(2)]) for pt in range(2)]) for l in range(L)]))
    small = np.zeros((L, 128, NS), f)
    fm = lambda v: v.reshape(-1, 128).T
    for l in range(L):
        s = small[l]
        s[:, C_N1:C_N1 + 8] = fm(inp["ffn1_norm"][l])
        s[:, C_N2:C_N2 + 8] = fm(inp["mix_norm"][l])
        s[:, C_N3:C_N3 + 8] = fm(inp["ffn2_norm"][l])
        s[:, C_ADAB:C_ADAB + 72] = fm(inp["ada_b"][l])
        cw = inp["ssm_conv_w"][l].reshape(12, 128, 4).transpose(1, 0, 2).reshape(128, 48)
        s[:, C_CONVW:C_CONVW + 48] = cw
        s[:, C_CONVB:C_CONVB + 12] = fm(inp["ssm_conv_b"][l])
        s[:, C_DTB:C_DTB + 16] = inp["ssm_dt_bias"][l][None, :]
        s[:, C_ALOG:C_ALOG + 16] = inp["ssm_a_log"][l][None, :]
        s[:, C_DSK:C_DSK + 16] = inp["ssm_d"][l][None, :]
        s[:, C_SUBLN] = inp["diff_subln"][l]
        s[:, C_LQ1:C_LQ1 + 64] = inp["diff_lambda_q1"][l][None, :]
        s[:, C_LK1:C_LK1 + 64] = inp["diff_lambda_k1"][l][None, :]
        s[:, C_LQ2:C_LQ2 + 64] = inp["diff_lambda_q2"][l][None, :]
        s[:, C_LK2:C_LK2 + 64] = inp["diff_lambda_k2"][l][None, :]
    out["small"] = small
    out["ssmn"] = A(np.broadcast_to(inp["ssm_norm"][:, None, :], (L, 128, 1024)))
    gs = np.zeros((128, NG), f)
    gs[:, G_FN:G_FN + 8] = fm(inp["final_norm"])
    gs[:, G_RB:G_RB + 128] = inp["rel_bias"].reshape(1, 128)
    b0, b1 = _bucket_tiles()
    gs[:, G_B0:G_B0 + 128] = b0
    gs[:, G_B1:G_B1 + 128] = b1
    out["gsmall"] = gs
    return out


_PROG_CACHE = {}


def run(inputs, NT=2048, NPASS=2, n_batch=4, **flags):
    inp = {k: np.asarray(v, dtype=np.float32) for k, v in inputs.items()}
    key = (NT, NPASS, tuple(sorted(flags.items())))
    if key not in _PROG_CACHE:
        _PROG_CACHE[key] = Builder(NT, NPASS, **flags).build()
    nc = _PROG_CACHE[key]
    shared = pack_weights(inp)
    ntok = NT * NPASS
    in_maps = []
    for core in range(8):
        b = core % n_batch
        m = dict(shared)
        m["xin"] = np.ascontiguousarray(inp["x"][b, :ntok, :])
        m["cT"] = np.ascontiguousarray(inp["c"][b].reshape(8, 128).T)
        in_maps.append(m)
    res = run_bass_kernel_spmd(nc, in_maps, core_ids=list(range(8)))
    return np.stack([res.results[b]["out"] for b in range(n_batch)], axis=0)


def kernel(**inputs):
    out = run(inputs, NT=2048, NPASS=2)
    return out.astype(np.float32)
```

```python
import math
from contextlib import ExitStack

import numpy as np
import concourse.bass as bass
import concourse.mybir as mybir
from concourse.bass_utils import run_bass_kernel_spmd

F32 = mybir.dt.float32
BF16 = mybir.dt.bfloat16
AF = mybir.ActivationFunctionType
ALU = mybir.AluOpType
AX = mybir.AxisListType

D = 1024
KD = 8
DFF = 2816
NJ = 22
NJH = 11
SEQ = 4096
DEPTH = 2
TT = 512
EPS = 1e-6
NEG = -30000.0

C_N1, C_N2, C_N3 = 0, 8, 16
C_ADAB = 24
C_CONVW = 96
C_CONVB = 144
C_DTB, C_ALOG, C_DSK = 156, 172, 188
C_SUBLN = 204
C_LQ1, C_LK1, C_LQ2, C_LK2 = 205, 269, 333, 397
NS = 461
G_FN, G_RB, G_B0, G_B1 = 0, 8, 136, 264
NG = 392


class Res:
    __slots__ = ("name", "w", "r")

    def __init__(self, name):
        self.name = name
        self.w = None
        self.r = {}


class Sched:
    ENG = ("pe", "act", "dve", "pool", "sp")

    def __init__(self, nc):
        self.nc = nc
        self.prog = {k: [] for k in self.ENG}
        self.cnt = {}
        self.seen = {k: {} for k in self.ENG}
        self.semkeys = []
        self.owner = {}

    def _sem(self, key):
        if key not in self.cnt:
            self.cnt[key] = 0
            self.semkeys.append(key)

    def _need(self, eng, dep, kind):
        if dep is None:
            return
        key, val = dep
        if key == "c_" + eng:
            if eng == "pe":
                return
        if self.owner.get(key) == eng:
            val = self.cnt[key]
        if self.seen[eng].get(key, 0) >= val:
            return
        self.seen[eng][key] = val
        self.prog[eng].append(("wait", key, val))

    def op(self, eng, fn, reads=(), writes=(), dma=None):
        for r in reads:
            self._need(eng, r.w, "raw")
        for w in writes:
            self._need(eng, w.w, "waw")
            for k, v in list(w.r.items()):
                self._need(eng, (k, v), "war")
        if dma is None:
            key, inc = "c_" + eng, 1
        else:
            key, inc = dma, 16
        self._sem(key)
        if dma is not None:
            self.owner[key] = eng
        self.cnt[key] += inc
        me = (key, self.cnt[key])
        self.prog[eng].append(("op", fn, key, inc))
        for r in reads:
            if r.r.get(key, 0) < me[1]:
                r.r[key] = me[1]
        for w in writes:
            w.w = me
            w.r = {}
        return me

    def wait_all(self, eng, keys):
        for k in keys:
            if k in self.cnt:
                self._need(eng, (k, self.cnt[k]), "raw")

    def barrier(self, engs=("pe", "act", "dve", "sp")):
        for e in engs:
            for k in list(self.cnt.keys()):
                if k == "c_" + e or k.startswith("d_w"):
                    continue
                v = self.cnt[k]
                if self.seen[e].get(k, 0) < v:
                    self.seen[e][k] = v
                    self.prog[e].append(("wait", k, v))

    def emit(self):
        nc = self.nc
        sems = {k: nc.alloc_semaphore(name="s_" + k) for k in self.semkeys}
        prog = self.prog

        def replay(ek, e):
            for it in prog[ek]:
                if it[0] == "wait":
                    e.wait_ge(sems[it[1]], it[2])
                else:
                    it[1](e).then_inc(sems[it[2]], it[3])

        with nc.Block() as block:
            @block.tensor
            def _(e):
                replay("pe", e)

            @block.scalar
            def _(e):
                replay("act", e)

            @block.vector
            def _(e):
                replay("dve", e)

            @block.gpsimd
            def _(e):
                replay("pool", e)

            @block.sync
            def _(e):
                replay("sp", e)


class Buf:
    __slots__ = ("t", "r")

    def __init__(self, t, r):
        self.t = t
        self.r = r


FPARTS = (6, 6, 5, 5)
FOFF = (0, 6, 12, 17)


class Builder:
    def __init__(self, NT, NPASS, layers=DEPTH, do_ffn1=True, do_ssd=True, do_att=True,
                 do_ffn2=True, final_norm=True):
        self.NT, self.NPASS = NT, NPASS
        self.NTT = NT // TT
        self.NTOK = NT * NPASS
        self.layers = layers
        self.flags = dict(ffn1=do_ffn1, ssd=do_ssd, att=do_att, ffn2=do_ffn2, fn=final_norm)
        self.nc = bass.Bass("TRN2", target_bir_lowering=False)
        self.S = Sched(self.nc)
        self.uid = 0

    def sb(self, st, name, shape, dt=F32):
        self.uid += 1
        nm = "%s_%d" % (name, self.uid)
        t = st.enter_context(self.nc.sbuf_tensor(nm, list(shape), dt))
        return Buf(t, Res(nm))

    def dram_in(self, name, shape, dt=F32):
        return self.nc.dram_tensor(name, list(shape), dt, kind="ExternalInput").ap()

    def dram_scr(self, name, shape, dt):
        return self.nc.dram_tensor(name, list(shape), dt, kind="Internal").ap()

    def mm(self, out, lhsT, rhs, start, stop, reads, writes, skip=False):
        kw = dict(skip_group_check=True) if skip else {}
        self.S.op("pe", lambda e: e.matmul(out, lhsT=lhsT, rhs=rhs, start=start, stop=stop, **kw),
                  reads, writes)

    def tr(self, out, in_, ident, reads, writes):
        self.S.op("pe", lambda e: e.transpose(out=out, in_=in_, identity=ident), reads, writes)

    def act(self, out, in_, func, reads, writes, bias=None, scale=None, accum_out=None):
        kw = {}
        if bias is not None:
            kw["bias"] = bias
        if scale is not None:
            kw["scale"] = scale
        if accum_out is not None:
            kw["accum_out"] = accum_out
        self.S.op("act", lambda e: e.activation(out=out, in_=in_, func=func, **kw), reads, writes)

    def tt_(self, out, in0, in1, op, reads, writes, eng="dve"):
        self.S.op(eng, lambda e: e.tensor_tensor(out=out, in0=in0, in1=in1, op=op), reads, writes)

    def ts(self, out, in0, s1, op0, reads, writes, s2=None, op1=None, eng="dve"):
        if op1 is None:
            self.S.op(eng, lambda e: e.tensor_scalar(out=out, in0=in0, scalar1=s1, scalar2=None, op0=op0),
                      reads, writes)
        else:
            self.S.op(eng, lambda e: e.tensor_scalar(out=out, in0=in0, scalar1=s1, scalar2=s2, op0=op0, op1=op1),
                      reads, writes)

    def stt(self, out, in0, scalar, in1, op0, op1, reads, writes):
        self.S.op("dve", lambda e: e.scalar_tensor_tensor(out=out, in0=in0, scalar=scalar, in1=in1,
                                                          op0=op0, op1=op1), reads, writes)

    def cp(self, out, in_, reads, writes, eng="dve"):
        if eng == "act":
            self.act(out, in_, AF.Identity, reads, writes)
        else:
            self.S.op(eng, lambda e: e.tensor_copy(out=out, in_=in_), reads, writes)

    def memset(self, ap, val, writes, eng="dve"):
        self.S.op(eng, lambda e: e.memset(ap, val), (), writes)

    def dma(self, q, out, in_, sem, reads, writes):
        self.S.op(q, lambda e: e.dma_start(out=out, in_=in_), reads, writes, dma=sem)

    def wload(self, pool, src, shape):
        i = pool["idx"] % len(pool["slots"])
        pool["idx"] += 1
        b = pool["slots"][i]
        n = int(np.prod(shape[1:]))
        flat = b.t[:, 0:n]
        view = flat.rearrange("p (a b) -> p a b", a=shape[1])
        self.dma("pool", view, src, "d_w%s%d" % (pool["name"], i), (), [b.r])
        return view, b.r

    def build(self):
        nc, S = self.nc, self.S
        NT, NPASS, NTT, NTOK = self.NT, self.NPASS, self.NTT, self.NTOK
        L = DEPTH
        FL = self.flags
        xin = self.dram_in("xin", [NTOK, D])
        cT_d = self.dram_in("cT", [128, 8])
        adaw = self.dram_in("adaw", [L, 18, 128, 8, 512])
        w13 = [self.dram_in("w13_%d" % f, [L, NJ, 128, 8, 256]) for f in (1, 2)]
        w2 = [self.dram_in("w2_%d" % f, [L, 4, 8, 128, 6, 128]) for f in (1, 2)]
        wz = self.dram_in("wz", [L, 2, 128, 8, 512])
        wx = self.dram_in("wx", [L, 12, 128, 8, 128])
        wdt = self.dram_in("wdt", [L, 128, 8, 16])
        wqk = self.dram_in("wqk", [L, 16, 128, 8, 128])
        wv = self.dram_in("wv", [L, 2, 128, 8, 512])
        wo = self.dram_in("wo", [L, 2, 2, 128, 8, 512])
        small_d = self.dram_in("small", [L, 128, NS])
        ssmn_d = self.dram_in("ssmn", [L, 128, 1024])
        gsmall_d = self.dram_in("gsmall", [128, NG])
        out_d = nc.dram_tensor("out", [NTOK, D], F32, kind="ExternalOutput").ap()
        ksb = self.dram_scr("ksb", [L, 4, 128, NTOK], BF16)
        kdf = self.dram_scr("kdf", [L, 4, 128, NTOK], BF16)
        vsb = self.dram_scr("vsb", [L, NTOK, 1024], BF16)
        vdf = self.dram_scr("vdf", [L, NTOK, 512], BF16)
        qsb = self.dram_scr("qsb", [4, 128, NT], BF16)
        qdf = self.dram_scr("qdf", [4, 128, NT], BF16)
        sst = self.dram_scr("sst", [L, 128, 1024], F32)
        ctl = self.dram_scr("ctl", [L, 128, 36], F32)
        R_kv = [Res("kv%d" % l) for l in range(L)]
        R_q = Res("qscr")
        R_st = [Res("sst%d" % l) for l in range(L)]
        R_out = Res("out")

        with ExitStack() as top:
            PS = top.enter_context(nc.psum_tensor("ps", [128, 8, 512], F32))
            RB = [Res("bank%d" % i) for i in range(8)]

            def bank(i):
                return PS[:, i, :]

            def bank2(i):
                return PS[:, i:i + 2, :].rearrange("p a b -> p (a b)")

            def bankb(i):
                return PS[:, i, :].bitcast(BF16)

            xT = self.sb(top, "xT", [128, KD, NT])
            RX = [[Res("xT_%d_%d" % (c, t)) for t in range(NTT)] for c in range(KD)]
            identb = self.sb(top, "identb", [128, 128], BF16)
            identf = self.sb(top, "identf", [128, 128])
            nidentb = self.sb(top, "nidentb", [128, 128], BF16)
            triN = self.sb(top, "triN", [128, 128], BF16)
            negones = self.sb(top, "negones", [128, 128], BF16)
            onesb = self.sb(top, "onesb", [128, 128], BF16)
            ones_d = self.sb(top, "ones_d", [128, 128], BF16)
            ones_v = self.sb(top, "ones_v", [128, 128], BF16)
            U32 = self.sb(top, "U32", [128, 128])
            onesf = self.sb(top, "onesf", [128, 128])
            m_lt = self.sb(top, "m_lt", [128, 128])
            D0 = self.sb(top, "D0", [128, 4, 128])
            D1 = self.sb(top, "D1", [128, 4, 128])
            small = [self.sb(top, "small%d" % l, [128, NS]) for l in range(L)]
            gsm = self.sb(top, "gsm", [128, NG])
            mv = [self.sb(top, "mv%d" % l, [128, 9, 8]) for l in range(L)]
            abc = [self.sb(top, "abc%d" % l, [128, 16]) for l in range(L)]
            neglam = [self.sb(top, "neglam%d" % l, [128, 1]) for l in range(L)]
            subg = [self.sb(top, "subg%d" % l, [128, 1]) for l in range(L)]
            epsc = self.sb(top, "epsc", [128, 1])
            WA = dict(name="a", idx=0, slots=[self.sb(top, "wa%d" % i, [128, 4096], BF16) for i in range(3)])
            WB = dict(name="b", idx=0, slots=[self.sb(top, "wb%d" % i, [128, 6 * 128], BF16) for i in range(3)])

            def sel(h):
                return identb.t[0:16, h:h + 1].to_broadcast([16, 128])

            def nsel(h):
                return nidentb.t[0:16, h:h + 1].to_broadcast([16, 128])

            def aff(buf, init, pattern, cmp_op, fill, base, cm):
                ap = buf.t[:]
                self.memset(ap, init, [buf.r], eng="pool")
                S.op("pool", lambda e: e.affine_select(out=ap, in_=ap, pattern=pattern, compare_op=cmp_op,
                                                       fill=fill, base=base, channel_multiplier=cm),
                     [buf.r], [buf.r])

            with ExitStack() as pst:
                aff(identf, 0.0, [[-1, 128]], ALU.not_equal, 1.0, 0, 1)
                self.cp(identb.t[:], identf.t[:], [identf.r], [identb.r])
                self.ts(nidentb.t[:], identf.t[:], -1.0, ALU.mult, [identf.r], [nidentb.r])
                tmpf = self.sb(pst, "tmpf", [128, 128])
                aff(tmpf, -1.0, [[-1, 128]], ALU.is_ge, 0.0, 0, 1)
                self.cp(triN.t[:], tmpf.t[:], [tmpf.r], [triN.r])
                self.memset(negones.t[:], -1.0, [negones.r])
                self.memset(onesb.t[:], 1.0, [onesb.r])
                self.memset(ones_d.t[:], 1.0 / 1024.0, [ones_d.r])
                self.memset(ones_v.t[:], 1.0 / 128.0, [ones_v.r])
                self.memset(onesf.t[:], 1.0, [onesf.r])
                self.memset(epsc.t[:], EPS, [epsc.r])
                aff(U32, 1.0, [[1, 128]], ALU.is_ge, 0.0, 0, -1)
                aff(m_lt, 1.0, [[1, 128]], ALU.is_ge, 0.0, -1, -1)

                for l in range(L):
                    self.dma("sp", small[l].t[:], small_d[l], "d_c%d" % l, (), [small[l].r])
                self.dma("sp", gsm.t[:], gsmall_d, "d_cg", (), [gsm.r])
                cT = self.sb(pst, "cT", [128, 8])
                self.dma("sp", cT.t[:], cT_d, "d_cc", (), [cT.r])
                scT = self.sb(pst, "scT", [128, 8], BF16)
                self.act(scT.t[:], cT.t[:], AF.Silu, [cT.r], [scT.r])

                rbd = self.sb(pst, "rbd", [128, 32, 4])
                rbv = gsm.t[:, G_RB:G_RB + 128].rearrange("p (b h) -> p b h", h=4)
                self.tt_(rbd.t[:], rbv, gsm.t[:, G_RB + 124:G_RB + 128].unsqueeze(1).to_broadcast([128, 32, 4]),
                         ALU.subtract, [gsm.r], [rbd.r])
                tmpd = self.sb(pst, "tmpd", [128, 128])
                for (Dt, gcol) in ((D0, G_B0), (D1, G_B1)):
                    bidx = gsm.t[:, gcol:gcol + 128]
                    for h in range(4):
                        dst = Dt.t[:, h, :]
                        self.ts(dst, bidx, 32.0, ALU.is_equal, [gsm.r], [Dt.r], s2=NEG, op1=ALU.mult)
                        for b in range(31):
                            self.ts(tmpd.t[:], bidx, float(b), ALU.is_equal, [gsm.r, rbd.r], [tmpd.r],
                                    s2=rbd.t[:, b, h:h + 1], op1=ALU.mult)
                            self.tt_(dst, dst, tmpd.t[:], ALU.add, [tmpd.r, Dt.r], [Dt.r])

                for l in range(L):
                    modps = bank(0)
                    for blk in range(18):
                        wv_, wr = self.wload(WA, adaw[l, blk], [128, 8, 512])
                        for oc in range(4):
                            j = blk * 4 + oc
                            for kc in range(KD):
                                self.mm(modps[:, j:j + 1], wv_[:, kc, oc * 128:(oc + 1) * 128], scT.t[:, kc:kc + 1],
                                        kc == 0, kc == KD - 1, [wr, scT.r], [RB[0]])
                    modT = self.sb(pst, "modT%d" % l, [128, 9, 8])
                    self.tt_(modT.t[:].rearrange("p a b -> p (a b)"), modps[:, 0:72],
                             small[l].t[:, C_ADAB:C_ADAB + 72], ALU.add, [RB[0], small[l].r], [modT.r])
                    for s, cn in enumerate((C_N1, C_N2, C_N3)):
                        self.stt(mv[l].t[:, 3 * s, :], modT.t[:, 3 * s + 1, :], 1.0, small[l].t[:, cn:cn + 8],
                                 ALU.add, ALU.mult, [modT.r, small[l].r], [mv[l].r])
                        self.cp(mv[l].t[:, 3 * s + 1, :], modT.t[:, 3 * s, :], [modT.r], [mv[l].r])
                        self.ts(mv[l].t[:, 3 * s + 2, :], modT.t[:, 3 * s + 2, :], 1.0 if s == 1 else 0.5, ALU.mult,
                                [modT.r], [mv[l].r])
                    self.act(abc[l].t[:], small[l].t[:, C_ALOG:C_ALOG + 16], AF.Exp, [small[l].r], [abc[l].r])
                    self.ts(abc[l].t[:], abc[l].t[:], -1.0, ALU.mult, [abc[l].r], [abc[l].r])
                    lam_init = 0.8 - 0.6 * math.exp(-0.3 * l)
                    lt = self.sb(pst, "lt%d" % l, [128, 64])
                    l2 = self.sb(pst, "l2_%d" % l, [128, 2])
                    for i, (ca, cb) in enumerate(((C_LQ1, C_LK1), (C_LQ2, C_LK2))):
                        self.tt_(lt.t[:], small[l].t[:, ca:ca + 64], small[l].t[:, cb:cb + 64], ALU.mult,
                                 [small[l].r], [lt.r])
                        S.op("dve", (lambda o, i_: (lambda e: e.tensor_reduce(out=o, in_=i_, axis=AX.X, op=ALU.add)))(
                            l2.t[:, i:i + 1], lt.t[:]), [lt.r], [l2.r])
                    self.act(l2.t[:], l2.t[:], AF.Exp, [l2.r], [l2.r])
                    self.tt_(neglam[l].t[:], l2.t[:, 1:2], l2.t[:, 0:1], ALU.subtract, [l2.r], [neglam[l].r])
                    self.ts(neglam[l].t[:], neglam[l].t[:], -lam_init, ALU.add, [neglam[l].r], [neglam[l].r])
                    self.ts(subg[l].t[:], small[l].t[:, C_SUBLN:C_SUBLN + 1], 1.0 - lam_init, ALU.mult,
                            [small[l].r], [subg[l].r])
            S.barrier()

            def rms_tile(scr, tt, gain_fn, shift_fn, dst_fn, dst_res, pbank):
                sl = slice(tt * TT, (tt + 1) * TT)
                sq = scr[0:2]
                rstd_ap, rstd_r = scr[2]
                tmp = scr[3:5]
                for c in range(KD):
                    q_ap, q_r = sq[c % 2]
                    self.tt_(q_ap, xT.t[:, c, sl], xT.t[:, c, sl], ALU.mult, [RX[c][tt]], [q_r])
                    self.mm(bank(pbank), ones_d.t[:], q_ap, c == 0, c == KD - 1, [ones_d.r, q_r], [RB[pbank]])
                self.act(rstd_ap, bank(pbank), AF.Ln, [RB[pbank], epsc.r], [rstd_r], bias=epsc.t[:], scale=1.0)
                self.act(rstd_ap, rstd_ap, AF.Exp, [rstd_r], [rstd_r], scale=-0.5)
                for c in range(KD):
                    t_ap, t_r = tmp[c % 2]
                    self.tt_(t_ap, xT.t[:, c, sl], rstd_ap, ALU.mult, [RX[c][tt], rstd_r], [t_r])
                    self.act(dst_fn(c), t_ap, AF.Identity, [t_r], [dst_res], bias=shift_fn(c), scale=gain_fn(c))

            def rms_all(sq, rstds, tmp, gain_fn, shift_fn, dst_fn, dst_res_fn):
                for tt in range(NTT):
                    sl = slice(tt * TT, (tt + 1) * TT)
                    pb = 4 + (tt % 4)
                    for c in range(KD):
                        q = sq[c % 2]
                        self.tt_(q.t[:], xT.t[:, c, sl], xT.t[:, c, sl], ALU.mult, [RX[c][tt]], [q.r])
                        self.mm(bank(pb), ones_d.t[:], q.t[:], c == 0, c == KD - 1, [ones_d.r, q.r], [RB[pb]])
                for tt in range(NTT):
                    pb = 4 + (tt % 4)
                    r_ = rstds[tt]
                    self.act(r_.t[:], bank(pb), AF.Ln, [RB[pb], epsc.r], [r_.r], bias=epsc.t[:], scale=1.0)
                    self.act(r_.t[:], r_.t[:], AF.Exp, [r_.r], [r_.r], scale=-0.5)
                k = 0
                for tt in range(NTT):
                    sl = slice(tt * TT, (tt + 1) * TT)
                    r_ = rstds[tt]
                    for c in range(KD):
                        t_ = tmp[k % len(tmp)]
                        k += 1
                        self.tt_(t_.t[:], xT.t[:, c, sl], r_.t[:], ALU.mult, [RX[c][tt], r_.r], [t_.r])
                        self.act(dst_fn(tt, c), t_.t[:], AF.Identity, [t_.r], [dst_res_fn(tt, c)], bias=shift_fn(c), scale=gain_fn(c))

            def resid_update(tt, m, ps_ap, ps_res, gate_ap, extra_reads=()):
                sl = slice(tt * TT, (tt + 1) * TT)
                self.stt(xT.t[:, m, sl], ps_ap, gate_ap, xT.t[:, m, sl], ALU.mult, ALU.add,
                         [ps_res, RX[m][tt]] + list(extra_reads), [RX[m][tt]])

            def ffn(l, f):
                s = 0 if f == 0 else 2
                with ExitStack() as st:
                    hT = self.sb(st, "hT", [128, KD, NT], BF16)
                    Rh = [Res("hT%d" % t) for t in range(NTT)]
                    gT = self.sb(st, "gT", [128, 6, NT], BF16)
                    Rg = [[Res("gT%d_%d" % (j, t)) for t in range(NTT)] for j in range(6)]
                    sil = [self.sb(st, "sil%d" % i, [128, TT]) for i in range(2)]
                    sqb = [self.sb(st, "sqb%d" % i, [128, TT], BF16) for i in range(2)]
                    rstds = [self.sb(st, "rstd%d" % i, [128, TT]) for i in range(NTT)]
                    ntm = [self.sb(st, "ntm%d" % i, [128, TT]) for i in range(3)]
                    rms_all(sqb, rstds, ntm, lambda c: mv[l].t[:, 3 * s, c:c + 1], lambda c: mv[l].t[:, 3 * s + 1, c:c + 1],
                            lambda tt, c: hT.t[:, c, tt * TT:(tt + 1) * TT], lambda tt, c: Rh[tt])
                    k = 0
                    for H in range(4):
                        nj = FPARTS[H]
                        for jj in range(nj):
                            j = FOFF[H] + jj
                            wv_, wr = self.wload(WA, w13[f][l, j], [128, 8, 256])
                            for tt in range(NTT):
                                sl = slice(tt * TT, (tt + 1) * TT)
                                ba, bu = (k % 2) * 2, (k % 2) * 2 + 1
                                k += 1
                                for kc in range(KD):
                                    self.mm(bank(ba), wv_[:, kc, 0:128], hT.t[:, kc, sl], kc == 0, kc == KD - 1,
                                            [wr, Rh[tt]], [RB[ba]])
                                for kc in range(KD):
                                    self.mm(bank(bu), wv_[:, kc, 128:256], hT.t[:, kc, sl], kc == 0, kc == KD - 1,
                                            [wr, Rh[tt]], [RB[bu]])
                                sb_ = sil[k % 2]
                                self.act(sb_.t[:], bank(ba), AF.Silu, [RB[ba]], [sb_.r])
                                self.tt_(gT.t[:, jj, sl], sb_.t[:], bank(bu), ALU.mult, [sb_.r, RB[bu]], [Rg[jj][tt]])
                        for m in range(KD):
                            wv_, wr = self.wload(WB, w2[f][l, H, m, :, 0:nj, :], [128, nj, 128])
                            for tt in range(NTT):
                                sl = slice(tt * TT, (tt + 1) * TT)
                                bo = 4 + (k % 4)
                                k += 1
                                for jj in range(nj):
                                    self.mm(bank(bo), wv_[:, jj, :], gT.t[:, jj, sl], jj == 0, jj == nj - 1,
                                            [wr, Rg[jj][tt]], [RB[bo]])
                                resid_update(tt, m, bank(bo), RB[bo], mv[l].t[:, 3 * s + 2, m:m + 1], [mv[l].r])
                S.barrier()

            def mixer(l, p):
                g0 = p * NT
                sm = small[l]
                with ExitStack() as st:
                    hbs = [self.sb(st, "hTt%d" % i, [128, KD, TT], BF16) for i in range(2 if NTT % 2 == 0 else 1)]
                    stage = [self.sb(st, "stage%d" % i, [128, 3 + TT]) for i in range(2)]
                    stage_c = [Res("stage_c0"), Res("stage_c1")]
                    carry = self.sb(st, "carry", [128, 12, 3])
                    xact = [self.sb(st, "xact%d" % i, [128, 12, TT], BF16) for i in range(2 if NTT % 2 == 0 else 1)]
                    RxaL = [[Res("xact%d_%d" % (i, c)) for c in range(12)] for i in range(2)]
                    qst = [self.sb(st, "qst%d" % i, [128, TT], BF16) for i in range(4)]
                    vpad = [self.sb(st, "vpad%d" % i, [128, 1024], BF16) for i in range(1)]
                    vst = [self.sb(st, "vst%d" % i, [128, 512], BF16) for i in range(1)]
                    szc4 = self.sb(st, "szc4", [128, 4, 1024], BF16)
                    x_tm = self.sb(st, "x_tm", [128, 1024], BF16)
                    B_tm = self.sb(st, "B_tm", [128, 256], BF16)
                    dtb = self.sb(st, "dtb", [128, 8, 16])
                    dt_ = self.sb(st, "dt", [128, 8, 16])
                    da = self.sb(st, "da", [128, 8, 16])
                    acum = self.sb(st, "acum", [128, 16])
                    ea = self.sb(st, "ea", [128, 16])
                    cd = self.sb(st, "cd", [128, 16])
                    dte = self.sb(st, "dte", [128, 16])
                    w2s = self.sb(st, "w2s", [128, 16])
                    afh = self.sb(st, "afh", [16, 128], BF16)
                    afl = self.sb(st, "afl", [16, 128], BF16)
                    dec = [self.sb(st, "dec%d" % i, [128, 4, 128], BF16) for i in range(2)]
                    CBm = self.sb(st, "CBm", [128, 2, 128])
                    MT = [self.sb(st, "MT%d" % i, [128, 4, 128], BF16) for i in range(2)]
                    xd = self.sb(st, "xd", [128, 1024], BF16)
                    xds = self.sb(st, "xds", [128, 1024], BF16)
                    t1 = self.sb(st, "t1", [128, 1024])
                    t1h = [Res("t1a"), Res("t1b")]
                    t2 = self.sb(st, "t2", [128, 1024])
                    ss = self.sb(st, "ss", [128, 2])
                    rs = self.sb(st, "rs", [128, 2])
                    yn = self.sb(st, "yn", [128, 1024], BF16)
                    yTs = self.sb(st, "yTs", [128, 8, TT], BF16)
                    prevf = self.sb(st, "prevf", [128, 1024])
                    prevb = self.sb(st, "prevb", [128, 1024], BF16)
                    ssmn = self.sb(st, "ssmn", [128, 1024])
                    scr = [(yn.t[:, 0:512], yn.r), (yn.t[:, 512:1024], yn.r), (t2.t[:, 0:512], t2.r),
                           (t1.t[:, 0:512], t1h[0]), (t1.t[:, 512:1024], t1h[1])]

                    if FL["ssd"]:
                        self.dma("sp", ssmn.t[:], ssmn_d[l], "d_sn", (), [ssmn.r])
                        if p == 0:
                            self.memset(prevf.t[:], 0.0, [prevf.r])
                            self.memset(carry.t[:], 0.0, [carry.r])
                        else:
                            self.dma("sp", prevf.t[:], sst[l], "d_st0", [R_st[l]], [prevf.r])
                            self.dma("sp", carry.t[:].rearrange("p a b -> p (a b)"), ctl[l], "d_st1", [R_st[l]], [carry.r])
                        self.cp(prevb.t[:], prevf.t[:], [prevf.r], [prevb.r], eng="act")
                    self.memset(vpad[0].t[:], 0.0, [vpad[0].r])

                    TP = 2 if NTT % 2 == 0 else 1
                    for gp in range(NTT // TP):
                        tiles = [gp * TP + ti for ti in range(TP)]
                        for ti, tt in enumerate(tiles):
                            sl = slice(tt * TT, (tt + 1) * TT)
                            for c in range(KD):
                                q_ap = yn.t[:, (c % 2) * 512:(c % 2 + 1) * 512]
                                self.tt_(q_ap, xT.t[:, c, sl], xT.t[:, c, sl], ALU.mult, [RX[c][tt]], [yn.r])
                                self.mm(bank(6 + ti), ones_d.t[:], q_ap, c == 0, c == KD - 1, [ones_d.r, yn.r], [RB[6 + ti]])
                        for ti, tt in enumerate(tiles):
                            r_ap = t2.t[:, ti * 512:(ti + 1) * 512]
                            self.act(r_ap, bank(6 + ti), AF.Ln, [RB[6 + ti], epsc.r], [t2.r], bias=epsc.t[:], scale=1.0)
                            self.act(r_ap, r_ap, AF.Exp, [t2.r], [t2.r], scale=-0.5)
                        for ti, tt in enumerate(tiles):
                            sl = slice(tt * TT, (tt + 1) * TT)
                            r_ap = t2.t[:, ti * 512:(ti + 1) * 512]
                            for c in range(KD):
                                t_ap = t1.t[:, (c % 2) * 512:(c % 2 + 1) * 512]
                                self.tt_(t_ap, xT.t[:, c, sl], r_ap, ALU.mult, [RX[c][tt], t2.r], [t1h[c % 2]])
                                self.act(hbs[ti].t[:, c, :], t_ap, AF.Identity, [t1h[c % 2]], [hbs[ti].r],
                                         bias=mv[l].t[:, 4, c:c + 1], scale=mv[l].t[:, 3, c:c + 1])
                        kb = 0
                        vpad2 = [(vpad[0].t[:], [vpad[0].r]), (stage[0].t[:, 0:512].bitcast(BF16), [stage[0].r, stage_c[0]])]
                        vst2 = [(vst[0].t[:], [vst[0].r]), (stage[1].t[:, 0:256].bitcast(BF16), [stage[1].r, stage_c[1]])]
                        if FL["att"]:
                            self.memset(vpad2[1][0], 0.0, vpad2[1][1])
                            for cc in range(16):
                                wv_, wr = self.wload(WA, wqk[l, cc], [128, 8, 128])
                                for ti, tt in enumerate(tiles):
                                    sl = slice(tt * TT, (tt + 1) * TT)
                                    gt0 = g0 + tt * TT
                                    hb = hbs[ti]
                                    b = kb % 4
                                    kb += 1
                                    for kc in range(KD):
                                        self.mm(bank(b), wv_[:, kc, :], hb.t[:, kc, :], kc == 0, kc == KD - 1,
                                                [wr, hb.r], [RB[b]])
                                    qs_ = qst[kb % 4]
                                    isq = cc < 4 or 8 <= cc < 12
                                    if kb % 2 == 0:
                                        self.act(qs_.t[:], bank(b), AF.Identity, [RB[b]], [qs_.r], scale=0.125 if isq else 1.0)
                                    else:
                                        self.ts(qs_.t[:], bank(b), 0.125 if isq else 1.0, ALU.mult, [RB[b]], [qs_.r])
                                    if cc < 4:
                                        self.dma("sp", qsb[cc, :, sl], qs_.t[:], "d_sq%d" % (kb % 4), [qs_.r], [R_q])
                                    elif cc < 8:
                                        self.dma("sp", ksb[l, cc - 4, :, gt0:gt0 + TT], qs_.t[:], "d_sq%d" % (kb % 4), [qs_.r], [R_kv[l]])
                                    elif cc < 12:
                                        self.dma("sp", qdf[cc - 8, :, sl], qs_.t[:], "d_sq%d" % (kb % 4), [qs_.r], [R_q])
                                    else:
                                        self.dma("sp", kdf[l, cc - 12, :, gt0:gt0 + TT], qs_.t[:], "d_sq%d" % (kb % 4), [qs_.r], [R_kv[l]])
                            for wi in range(2):
                                wv_, wr = self.wload(WA, wv[l, wi], [128, 8, 512])
                                for ti, tt in enumerate(tiles):
                                    gt0 = g0 + tt * TT
                                    hb = hbs[ti]
                                    for t4 in range(4):
                                        b = kb % 4
                                        kb += 1
                                        for kc in range(KD):
                                            self.mm(bank(b), hb.t[:, kc, t4 * 128:(t4 + 1) * 128], wv_[:, kc, :],
                                                    kc == 0, kc == KD - 1, [wr, hb.r], [RB[b]])
                                        r0 = gt0 + t4 * 128
                                        if wi == 0:
                                            vp_ap, vp_r = vpad2[t4 % 2]
                                            src = bank(b).rearrange("p (h e d) -> p h e d", h=4, e=2)
                                            dst = vp_ap.rearrange("p (h e x d) -> p h e x d", h=4, e=2, x=2)
                                            self.cp(dst[:, :, 0, 0, :], src[:, :, 0, :], [RB[b]], vp_r, eng="act")
                                            self.cp(dst[:, :, 1, 1, :], src[:, :, 1, :], [RB[b]], vp_r, eng="dve")
                                            self.dma("sp", vsb[l, r0:r0 + 128, :], vp_ap, "d_svp%d" % (t4 % 2), vp_r, [R_kv[l]])
                                        else:
                                            vs_ap, vs_r = vst2[t4 % 2]
                                            self.cp(vs_ap, bank(b), [RB[b]], vs_r, eng="act")
                                            self.dma("sp", vdf[l, r0:r0 + 128, :], vs_ap, "d_svs%d" % (t4 % 2), vs_r, [R_kv[l]])
                        if not FL["ssd"]:
                            continue
                        xun = [(c, ti) for c in range(12) for ti in range(TP)]
                        xb_bank = {}
                        xw_ = {}

                        def xP(u):
                            c, ti = xun[u]
                            if ti == 0:
                                xw_[c] = self.wload(WA, wx[l, c], [128, 8, 128])
                            wv_, wr = xw_[c]
                            hb = hbs[ti]
                            b = (kb + u) % 4
                            xb_bank[u] = b
                            for kc in range(KD):
                                self.mm(bank(b), wv_[:, kc, :], hb.t[:, kc, :], kc == 0, kc == KD - 1, [wr, hb.r], [RB[b]])

                        def xE(u):
                            c, ti = xun[u]
                            b = xb_bank[u]
                            sg = stage[u % 2]
                            self.cp(sg.t[:, 3:3 + TT], bank(b), [RB[b]], [sg.r], eng="act")
                            self.cp(sg.t[:, 0:3], carry.t[:, c, :], [carry.r], [stage_c[u % 2]])
                            self.cp(carry.t[:, c, :], sg.t[:, TT:TT + 3], [sg.r], [carry.r])

                        def xV(u):
                            c, ti = xun[u]
                            sg = stage[u % 2]
                            ct_ap = t1.t[:, (u % 2) * 512:(u % 2 + 1) * 512]
                            cw = C_CONVW + c * 4
                            self.ts(ct_ap, sg.t[:, 0:TT], sm.t[:, cw:cw + 1], ALU.mult, [sg.r, stage_c[u % 2], sm.r], [t1h[u % 2]])
                            for kk in range(1, 4):
                                self.stt(ct_ap, sg.t[:, kk:kk + TT], sm.t[:, cw + kk:cw + kk + 1], ct_ap,
                                         ALU.mult, ALU.add, [sg.r, stage_c[u % 2], t1h[u % 2], sm.r], [t1h[u % 2]])

                        def xS(u):
                            c, ti = xun[u]
                            ct_ap = t1.t[:, (u % 2) * 512:(u % 2 + 1) * 512]
                            self.act(xact[ti].t[:, c, :], ct_ap, AF.Silu, [t1h[u % 2], sm.r], [RxaL[ti][c]],
                                     bias=sm.t[:, C_CONVB + c:C_CONVB + c + 1], scale=1.0)

                        for k_ in range(len(xun) + 2):
                            for stg, sk in ((xP, 0), (xE, 0), (xV, 1), (xS, 2)):
                                u_ = k_ - sk
                                if 0 <= u_ < len(xun):
                                    stg(u_)
                        kb += len(xun)
                        wv_, wr = self.wload(WA, wdt[l], [128, 8, 16])
                        nch = 4 * TP
                        for ti in range(TP):
                            hb = hbs[ti]
                            for t4 in range(4):
                                t4g = ti * 4 + t4
                                for kc in range(KD):
                                    self.mm(bank(0)[:, t4g * 16:(t4g + 1) * 16], hb.t[:, kc, t4 * 128:(t4 + 1) * 128],
                                            wv_[:, kc, :], kc == 0, kc == KD - 1, [wr, hb.r], [RB[0]])
                        self.tt_(dtb.t[:, 0:nch, :], bank(0)[:, 0:16 * nch].rearrange("p (a b) -> p a b", a=nch),
                                 sm.t[:, C_DTB:C_DTB + 16].unsqueeze(1).to_broadcast([128, nch, 16]), ALU.add,
                                 [RB[0], sm.r], [dtb.r])
                        self.act(dtb.t[:, 0:nch, :], dtb.t[:, 0:nch, :], AF.Exp, [dtb.r], [dtb.r])
                        self.act(dt_.t[:, 0:nch, :], dtb.t[:, 0:nch, :], AF.Ln, [dtb.r], [dt_.r], bias=1.0, scale=1.0)
                        self.tt_(da.t[:, 0:nch, :], dt_.t[:, 0:nch, :], abc[l].t[:].unsqueeze(1).to_broadcast([128, nch, 16]), ALU.mult,
                                 [dt_.r, abc[l].r], [da.r])
                        for ti, tt in enumerate(tiles):
                            sl = slice(tt * TT, (tt + 1) * TT)
                            hb = hbs[ti]
                            xa_ = xact[ti]
                            Rxa_ = RxaL[ti]
                            wz_ = [self.wload(WA, wz[l, hf], [128, 8, 512]) for hf in range(2)]
                            for t4 in range(4):
                                for hf in range(2):
                                    wzv, wzr = wz_[hf]
                                    bz_ = 4 + ((t4 * 2 + hf) % 4)
                                    for kc in range(KD):
                                        self.mm(bank(bz_), hb.t[:, kc, t4 * 128:(t4 + 1) * 128], wzv[:, kc, :], kc == 0, kc == KD - 1,
                                                [wzr, hb.r], [RB[bz_]])
                                    self.act(szc4.t[:, t4, hf * 512:(hf + 1) * 512], bank(bz_), AF.Silu, [RB[bz_]], [szc4.r])
                            for t4 in range(4):
                                tsl = slice(t4 * 128, (t4 + 1) * 128)
                                t4g = ti * 4 + t4
                                pb = bankb(3)
                                for c in range(8):
                                    self.tr(pb[:, c * 128:(c + 1) * 128], xa_.t[:, c, tsl], identb.t[:], [Rxa_[c], identb.r], [RB[3]])
                                self.cp(x_tm.t[:, 0:512], pb[:, 0:512], [RB[3]], [x_tm.r], eng="act")
                                self.cp(x_tm.t[:, 512:1024], pb[:, 512:1024], [RB[3]], [x_tm.r], eng="dve")
                                for c in range(2):
                                    self.tr(pb[:, c * 128:(c + 1) * 128], xa_.t[:, 8 + c, tsl], identb.t[:], [Rxa_[8 + c], identb.r], [RB[3]])
                                self.cp(B_tm.t[:], pb[:, 0:256], [RB[3]], [B_tm.r], eng="act")
                                pA = bank(0)
                                for g in range(2):
                                    self.mm(pA[:, 256 + g * 128:256 + (g + 1) * 128], xa_.t[:, 8 + g, tsl], xa_.t[:, 10 + g, tsl],
                                            True, True, [Rxa_[8 + g], Rxa_[10 + g]], [RB[0]])
                                self.tt_(CBm.t[:], pA[:, 256:512].rearrange("p (g l) -> p g l", g=2),
                                         U32.t[:].unsqueeze(1).to_broadcast([128, 2, 128]), ALU.mult,
                                         [RB[0], U32.r], [CBm.r])
                                pA = bank(0)
                                self.mm(pA[:, 64:80], U32.t[:], da.t[:, t4g, :], True, True, [U32.r, da.r], [RB[0]])
                                self.mm(pA[:, 80:96], onesf.t[:], da.t[:, t4g, :], True, True, [onesf.r, da.r], [RB[0]])
                                self.mm(pA[0:16, 128:256], da.t[:, t4g, :], U32.t[:], True, True, [U32.r, da.r], [RB[0]])
                                self.cp(acum.t[:], pA[:, 64:80], [RB[0]], [acum.r])
                                self.act(ea.t[:], pA[:, 64:80], AF.Exp, [RB[0]], [ea.r])
                                self.act(cd.t[:], pA[:, 80:96], AF.Exp, [RB[0]], [cd.r])
                                self.tt_(dte.t[:], pA[:, 80:96], acum.t[:], ALU.subtract, [RB[0], acum.r], [dte.r])
                                self.act(dte.t[:], dte.t[:], AF.Exp, [dte.r], [dte.r])
                                self.cp(afh.t[:], pA[0:16, 128:256], [RB[0]], [afh.r])
                                self.tt_(afl.t[:], pA[0:16, 128:256], afh.t[:], ALU.subtract, [RB[0], afh.r], [afl.r])
                                xv = x_tm.t[:].rearrange("p (h d) -> p h d", h=16)
                                self.tt_(xd.t[:].rearrange("p (h d) -> p h d", h=16), xv,
                                         dt_.t[:, t4g, :].unsqueeze(2).to_broadcast([128, 16, 64]), ALU.mult,
                                         [x_tm.r, dt_.r], [xd.r])
                                self.tt_(w2s.t[:], dt_.t[:, t4g, :], dte.t[:], ALU.mult, [dt_.r, dte.r], [w2s.r])
                                self.tt_(xds.t[:].rearrange("p (h d) -> p h d", h=16), xv,
                                         w2s.t[:].unsqueeze(2).to_broadcast([128, 16, 64]), ALU.mult,
                                         [x_tm.r, w2s.r], [xds.r])
                                pY = bank2(4)

                                def r1_(rd):
                                    b = 1 + (rd % 2)
                                    pS = bank(b)
                                    for hh in range(4):
                                        h = rd * 4 + hh
                                        o_ = pS[:, hh * 128:(hh + 1) * 128]
                                        self.mm(o_, sel(h), afh.t[:], True, False, [afh.r, identb.r], [RB[b]])
                                        self.mm(o_, sel(h), afl.t[:], False, False, [afl.r, identb.r], [RB[b]])
                                        self.mm(o_, afh.t[:], nsel(h), False, False, [afh.r, nidentb.r], [RB[b]])
                                        self.mm(o_, afl.t[:], nsel(h), False, True, [afl.r, nidentb.r], [RB[b]])

                                def r2_(rd):
                                    b = 1 + (rd % 2)
                                    dc_, mt_ = dec[rd % 2], MT[rd % 2]
                                    self.act(bank(b), bank(b), AF.Relu, [RB[b]], [RB[b]], scale=-1.0)
                                    self.act(dc_.t[:].rearrange("p a b -> p (a b)"), bank(b), AF.Exp, [RB[b]], [dc_.r], scale=-1.0)
                                    g = rd // 2
                                    self.stt(mt_.t[:], dc_.t[:], 1.0, CBm.t[:, g, :].unsqueeze(1).to_broadcast([128, 4, 128]),
                                             ALU.min, ALU.mult, [dc_.r, CBm.r], [mt_.r])

                                def r3_(rd):
                                    mt_ = MT[rd % 2]
                                    for hh in range(4):
                                        h = rd * 4 + hh
                                        self.mm(pY[:, h * 64:(h + 1) * 64], mt_.t[:, hh, :], xd.t[:, h * 64:(h + 1) * 64],
                                                True, True, [mt_.r, xd.r], [RB[4 + h // 8]])

                                for k_ in range(4 + 2):
                                    for stg, sk in ((r1_, 0), (r2_, 1), (r3_, 2)):
                                        u_ = k_ - sk
                                        if 0 <= u_ < 4:
                                            stg(u_)
                                pO = bank2(6)
                                for g in range(2):
                                    self.mm(pO[:, g * 512:(g + 1) * 512], xa_.t[:, 10 + g, tsl], prevb.t[:, g * 512:(g + 1) * 512],
                                            True, True, [Rxa_[10 + g], prevb.r], [RB[6 + g]])
                                pSt = bank2(1)
                                for g in range(2):
                                    self.mm(pSt[:, g * 512:(g + 1) * 512], B_tm.t[:, g * 128:(g + 1) * 128],
                                            xds.t[:, g * 512:(g + 1) * 512], True, True, [B_tm.r, xds.r], [RB[1 + g]])
                                self.tt_(t1.t[:].rearrange("p (h d) -> p h d", h=16), pO.rearrange("p (h d) -> p h d", h=16),
                                         ea.t[:].unsqueeze(2).to_broadcast([128, 16, 64]), ALU.mult,
                                         [RB[6], RB[7], ea.r], [t1h[0], t1h[1]])
                                self.tt_(t1.t[:], t1.t[:], pY, ALU.add, [t1h[0], t1h[1], RB[4], RB[5]], [t1h[0], t1h[1]])
                                self.tt_(t2.t[:].rearrange("p (h d) -> p h d", h=16), xv,
                                         sm.t[:, C_DSK:C_DSK + 16].unsqueeze(2).to_broadcast([128, 16, 64]), ALU.mult,
                                         [x_tm.r, sm.r], [t2.r])
                                self.tt_(t1.t[:], t1.t[:], t2.t[:], ALU.add, [t1h[0], t1h[1], t2.r], [t1h[0], t1h[1]])
                                self.tt_(t2.t[:], t1.t[:], szc4.t[:, t4, :], ALU.mult, [t1h[0], t1h[1], szc4.r], [t2.r])
                                for g in range(2):
                                    self.act(yn.t[:, g * 512:(g + 1) * 512], t2.t[:, g * 512:(g + 1) * 512], AF.Square, [t2.r], [yn.r, ss.r],
                                             accum_out=ss.t[:, g:g + 1])
                                self.act(rs.t[:], ss.t[:], AF.Ln, [ss.r, epsc.r], [rs.r], bias=epsc.t[:], scale=1.0 / 512.0)
                                self.act(rs.t[:], rs.t[:], AF.Exp, [rs.r], [rs.r], scale=-0.5)
                                for g in range(2):
                                    gs = slice(g * 512, (g + 1) * 512)
                                    self.stt(yn.t[:, gs], t2.t[:, gs], rs.t[:, g:g + 1], ssmn.t[:, gs],
                                             ALU.mult, ALU.mult, [t2.r, rs.r, ssmn.r], [yn.r])
                                pv = prevf.t[:].rearrange("p (h d) -> p h d", h=16)
                                self.tt_(pv, pv, cd.t[:].unsqueeze(2).to_broadcast([128, 16, 64]), ALU.mult, [prevf.r, cd.r], [prevf.r])
                                self.tt_(prevf.t[:], prevf.t[:], pSt, ALU.add, [prevf.r, RB[1], RB[2]], [prevf.r])
                                self.cp(prevb.t[:], prevf.t[:], [prevf.r], [prevb.r], eng="act")
                                for c8 in range(8):
                                    self.tr(pb[:, c8 * 128:(c8 + 1) * 128], yn.t[:, c8 * 128:(c8 + 1) * 128], identb.t[:],
                                            [yn.r, identb.r], [RB[3]])
                                self.cp(yTs.t[:, :, tsl], pb.rearrange("p (c t) -> p c t", c=8), [RB[3]], [yTs.r], eng="act")
                            for mh in range(2):
                                wov, wor = self.wload(WA, wo[l, 0, mh], [128, 8, 512])
                                for mm_ in range(4):
                                    m = mh * 4 + mm_
                                    b = 4 + (m % 4)
                                    for fc in range(8):
                                        self.mm(bank(b), wov[:, fc, mm_ * 128:(mm_ + 1) * 128], yTs.t[:, fc, :], fc == 0, fc == 7,
                                                [wor, yTs.r], [RB[b]])
                                    resid_update(tt, m, bank(b), RB[b], mv[l].t[:, 5, m:m + 1], [mv[l].r])
                    if FL["ssd"] and p < NPASS - 1:
                        self.dma("sp", sst[l], prevf.t[:], "d_ss0", [prevf.r], [R_st[l]])
                        self.dma("sp", ctl[l], carry.t[:].rearrange("p a b -> p (a b)"), "d_ss1", [carry.r], [R_st[l]])
                S.barrier()
                if not FL["att"]:
                    return
                S.barrier(engs=("pool",))
                with ExitStack() as st:
                    nkmax = g0 + NT
                    qq = self.sb(st, "qq", [128, 4, TT], BF16)
                    ksl = [self.sb(st, "ksl%d" % i, [128, nkmax], BF16) for i in range(2)]
                    vsl = [self.sb(st, "vsl%d" % i, [128, nkmax // 128, 256], BF16) for i in range(2)]
                    SS = self.sb(st, "SS", [128, 2, TT])
                    SSb2 = [self.sb(st, "SSb%d" % i, [128, 2, TT], BF16) for i in range(2)]
                    SSb = SSb2[0]
                    etmp = [self.sb(st, "etmp%d" % i, [128, 2, TT]) for i in range(2)]
                    spb = [self.sb(st, "spb%d" % i, [128, 2, TT], BF16) for i in range(3)]
                    Wt = [self.sb(st, "Wt%d" % i, [128, 2, TT], BF16) for i in range(3)]
                    Esum = self.sb(st, "Esum", [128, 2, TT])
                    REs = [Res("Esum0"), Res("Esum1")]
                    yTa = self.sb(st, "yTa", [128, 8, TT], BF16)
                    RyT = [Res("yTa%d" % i) for i in range(8)]
                    D0b = self.sb(st, "D0b", [128, 4, 128], BF16)
                    D1b = self.sb(st, "D1b", [128, 4, 128], BF16)
                    self.cp(D0b.t[:], D0.t[:], [D0.r], [D0b.r])
                    self.cp(D1b.t[:], D1.t[:], [D1.r], [D1b.r])
                    Eb = Wt
                    kvi = 0
                    ZP = (0, 4, 6)

                    def pair(b, cs):
                        return PS[:, b:b + 2, cs]

                    def pipeline(n, stages, skews):
                        for k in range(n + max(skews)):
                            for stg, sk in zip(stages, skews):
                                u = k - sk
                                if 0 <= u < n:
                                    stg(u)

                    mlt2 = m_lt.t[:].unsqueeze(1).to_broadcast([128, 2, 128])
                    for tt in range(NTT):
                        sl = slice(tt * TT, (tt + 1) * TT)
                        gt0 = g0 + tt * TT
                        i0 = gt0 // 128
                        Jmax = i0 + 3
                        nk = (Jmax + 1) * 128
                        nJ = Jmax + 1

                        def load_kv(kind, idx, slot):
                            ks_, vs_ = ksl[slot], vsl[slot]
                            if kind == 0:
                                self.dma("sp", ks_.t[:, 0:nk], ksb[l, idx, :, 0:nk], "d_k%d" % slot, [R_kv[l]], [ks_.r])
                                self.dma("sp", vs_.t[:, 0:nJ, :],
                                         vsb[l, 0:nk, idx * 256:(idx + 1) * 256].rearrange("(j p) f -> p j f", p=128),
                                         "d_v%d" % slot, [R_kv[l]], [vs_.r])
                            else:
                                self.dma("sp", ks_.t[:, 0:nk], kdf[l, idx, :, 0:nk], "d_k%d" % slot, [R_kv[l]], [ks_.r])
                                self.dma("sp", vs_.t[:, 0:nJ, 0:128],
                                         vdf[l, 0:nk, idx * 128:(idx + 1) * 128].rearrange("(j p) f -> p j f", p=128),
                                         "d_v%d" % slot, [R_kv[l]], [vs_.r])

                        self.dma("sp", qq.t[:], qsb[:, :, sl].rearrange("c p t -> p c t"), "d_qs", [R_q], [qq.r])
                        units = [(c, J) for c in range(4) for J in range(Jmax, -1, -1)]
                        slot_of = {c: (kvi + c) % 2 for c in range(4)}
                        load_kv(0, 0, slot_of[0])

                        def geom(u):
                            c, J = units[u]
                            d = J - i0
                            c0 = max(d, 0) * 128
                            return c, J, d, c0, slice(c0, TT)

                        def sb1(u):
                            c, J, d, c0, cs = geom(u)
                            ks_ = ksl[slot_of[c]]
                            b = ZP[u % 3]
                            for e_ in range(2):
                                ps_ = slice(e_ * 64, (e_ + 1) * 64)
                                self.mm(bank(b + e_)[:, cs], ks_.t[ps_, J * 128:(J + 1) * 128], qq.t[ps_, c, cs], True, True,
                                        [ks_.r, qq.r], [RB[b + e_]])

                        def sb2(u):
                            c, J, d, c0, cs = geom(u)
                            b = ZP[u % 3]
                            et, sp_ = etmp[u % 2], spb[u % 3]
                            self.act(et.t[:, :, cs], pair(b, cs), AF.Exp, [RB[b], RB[b + 1]], [et.r])
                            self.act(sp_.t[:, :, cs], et.t[:, :, cs], AF.Ln, [et.r], [sp_.r], bias=1.0, scale=1.0)
                            if d >= 0:
                                dsl = slice(c0, c0 + 128)
                                self.tt_(sp_.t[:, :, dsl], sp_.t[:, :, dsl], mlt2, ALU.mult, [sp_.r, m_lt.r], [sp_.r])

                        def sb3(u):
                            c, J, d, c0, cs = geom(u)
                            b = ZP[u % 3]
                            sp_ = spb[u % 3]
                            for e_ in range(2):
                                self.mm(bank(b + e_)[:, cs], triN.t[:], sp_.t[:, e_, cs], False, True, [triN.r, sp_.r], [RB[b + e_]], skip=True)
                                if J < Jmax:
                                    ssb_ = SSb2[u % 2]
                                    self.mm(bank(b + e_)[:, cs], negones.t[:], ssb_.t[:, e_, cs], False, True, [negones.r, ssb_.r],
                                            [RB[b + e_]], skip=True)

                        def sbU(u):
                            c, J, d, c0, cs = geom(u)
                            sp_ = spb[u % 3]
                            if J == 0:
                                return
                            nxt = SSb2[(u + 1) % 2]
                            if J == Jmax:
                                self.memset(SS.t[:], 0.0, [SS.r])
                            self.tt_(SS.t[:, :, cs], SS.t[:, :, cs], sp_.t[:, :, cs], ALU.add, [SS.r, sp_.r], [SS.r])
                            self.cp(nxt.t[:], SS.t[:], [SS.r], [nxt.r])

                        def sb4(u):
                            c, J, d, c0, cs = geom(u)
                            b = ZP[u % 3]
                            wt_ = Wt[u % 3]
                            self.act(wt_.t[:, :, cs], pair(b, cs), AF.Exp, [RB[b], RB[b + 1]], [wt_.r])
                            if d >= 0:
                                dsl = slice(c0, c0 + 128)
                                self.tt_(wt_.t[:, :, dsl], wt_.t[:, :, dsl], mlt2, ALU.mult, [wt_.r, m_lt.r], [wt_.r])

                        def sb5(u):
                            c, J, d, c0, cs = geom(u)
                            wt_, sp_ = Wt[u % 3], spb[u % 3]
                            vs_ = vsl[slot_of[c]]
                            bo = 2 + (c % 2)
                            for e_ in range(2):
                                first = (e_ == 0 and J == Jmax)
                                self.mm(bank(bo)[:, cs], vs_.t[:, J, e_ * 128:(e_ + 1) * 128], wt_.t[:, e_, cs], first, True,
                                        [vs_.r, wt_.r], [RB[bo]], skip=not first)
                            if J == Jmax and c + 1 < 4:
                                load_kv(0, c + 1, slot_of[c + 1])
                            if J == 0:
                                self.cp(yTa.t[:, c, :], bank(bo), [RB[bo]], [RyT[c]], eng="dve")

                        pipeline(len(units), (sb1, sb2, sb3, sbU, sb4, sb5), (0, 0, 1, 1, 1, 2))
                        kvi += 4

                        self.dma("sp", qq.t[:], qdf[:, :, sl].rearrange("c p t -> p c t"), "d_qs", [R_q], [qq.r])
                        dunits = [(h, J) for h in range(4) for J in range(0, Jmax + 1)]
                        dslot = {h: (kvi + h) % 2 for h in range(4)}
                        load_kv(1, 0, dslot[0])
                        tS = SS.t[:].rearrange("p a (b c) -> p (a b) c", b=2)[:, 0:2, :]
                        r1, r2 = etmp[0].t[:, 0, :], etmp[0].t[:, 1, :]
                        ot, ou = etmp[1].t[:, 0, :], etmp[1].t[:, 1, :]
                        rsd = SS.t[:, 1, :]
                        sqd = SSb.t[:, 0, :]

                        def dgeom(u):
                            h, J = dunits[u]
                            d = J - i0
                            c0 = max(d, 0) * 128
                            return h, J, d, c0, slice(c0, TT)

                        DP = (0, 4)

                        def df1(u):
                            h, J, d, c0, cs = dgeom(u)
                            ks_ = ksl[dslot[h]]
                            b = DP[u % 2]
                            for m_ in range(2):
                                ps_ = slice(m_ * 64, (m_ + 1) * 64)
                                self.mm(bank(b + m_)[:, cs], ks_.t[ps_, J * 128:(J + 1) * 128], qq.t[ps_, h, cs], True, True,
                                        [ks_.r, qq.r], [RB[b + m_]])
                                if d >= 0:
                                    self.mm(bank(b + m_)[:, c0:c0 + 128], identb.t[:], D0b.t[:, h, :], False, True,
                                            [identb.r, D0b.r], [RB[b + m_]], skip=True)
                                    if c0 + 256 <= TT:
                                        self.mm(bank(b + m_)[:, c0 + 128:c0 + 256], identb.t[:], D1b.t[:, h, :], False, True,
                                                [identb.r, D1b.r], [RB[b + m_]], skip=True)
                                elif d == -1:
                                    self.mm(bank(b + m_)[:, 0:128], identb.t[:], D1b.t[:, h, :], False, True,
                                            [identb.r, D1b.r], [RB[b + m_]], skip=True)

                        def df2(u):
                            h, J, d, c0, cs = dgeom(u)
                            b = DP[u % 2]
                            eb = Eb[u % 3]
                            self.act(eb.t[:, :, cs], pair(b, cs), AF.Exp, [RB[b], RB[b + 1]], [eb.r])

                        def df3(u):
                            h, J, d, c0, cs = dgeom(u)
                            eb = Eb[u % 3]
                            vs_ = vsl[dslot[h]]
                            for m_ in range(2):
                                self.mm(bank(2 + m_)[:, cs], vs_.t[:, J, 0:128], eb.t[:, m_, cs], J == 0, True,
                                        [vs_.r, eb.r], [RB[2 + m_]], skip=J > 0)
                            if J == 0:
                                self.cp(Esum.t[:, 0, :], eb.t[:, 0, :], [eb.r], [REs[0]])
                            else:
                                self.tt_(Esum.t[:, 0, cs], Esum.t[:, 0, cs], eb.t[:, 0, cs], ALU.add, [eb.r, REs[0]], [REs[0]])
                            self.mm(bank(7)[:, cs], onesb.t[:], eb.t[:, 1, cs], J == 0, True, [onesb.r, eb.r], [RB[7]], skip=J > 0)
                            if J == 0 and h + 1 < 4:
                                load_kv(1, h + 1, dslot[h + 1])
                            if J == Jmax:
                                self.mm(bank(6), onesf.t[:], Esum.t[:, 0, :], True, True, [onesf.r, REs[0]], [RB[6]])

                        def df4(u):
                            h, J, d, c0, cs = dgeom(u)
                            if J != Jmax:
                                return
                            bs = 6
                            self.act(r1, bank(6), AF.Ln, [RB[6]], [etmp[0].r])
                            self.act(r1, r1, AF.Exp, [etmp[0].r], [etmp[0].r], scale=-1.0)
                            self.act(r2, bank(7), AF.Ln, [RB[7]], [etmp[0].r])
                            self.act(r2, r2, AF.Exp, [etmp[0].r], [etmp[0].r], scale=-1.0)
                            self.tt_(ot, bank(2), r1, ALU.mult, [RB[2], etmp[0].r], [etmp[1].r])
                            self.tt_(ou, bank(3), r2, ALU.mult, [RB[3], etmp[0].r], [etmp[1].r])
                            self.stt(ot, ou, neglam[l].t[:, 0:1], ot, ALU.mult, ALU.add, [etmp[1].r, neglam[l].r], [etmp[1].r])
                            self.tt_(sqd, ot, ot, ALU.mult, [etmp[1].r], [SSb.r])
                            self.mm(bank(bs), ones_v.t[:], sqd, True, True, [ones_v.r, SSb.r], [RB[bs]])
                            self.act(rsd, bank(bs), AF.Ln, [RB[bs], epsc.r], [SS.r], bias=epsc.t[:], scale=1.0)
                            self.act(rsd, rsd, AF.Exp, [SS.r], [SS.r], scale=-0.5)
                            self.stt(yTa.t[:, 4 + h, :], ot, subg[l].t[:, 0:1], rsd, ALU.mult, ALU.mult,
                                     [etmp[1].r, SS.r, subg[l].r], [RyT[4 + h]])

                        pipeline(len(dunits), (df1, df2, df3, df4), (0, 0, 2, 2))
                        kvi += 4
                        for mh in range(2):
                            wov, wor = self.wload(WA, wo[l, 1, mh], [128, 8, 512])
                            for mm_ in range(4):
                                m = mh * 4 + mm_
                                b = 6 + (m % 2)
                                for fc in range(8):
                                    self.mm(bank(b), wov[:, fc, mm_ * 128:(mm_ + 1) * 128], yTa.t[:, fc, :], fc == 0, fc == 7,
                                            [wor, RyT[fc]], [RB[b]])
                                resid_update(tt, m, bank(b), RB[b], mv[l].t[:, 5, m:m + 1], [mv[l].r])
                S.barrier()

            for p in range(NPASS):
                with ExitStack() as st:
                    xl = [self.sb(st, "xl%d" % i, [128, D]) for i in range(2)]
                    for t16 in range(NT // 128):
                        xb_ = xl[t16 % 2]
                        r0 = p * NT + t16 * 128
                        self.dma("sp", xb_.t[:], xin[r0:r0 + 128, :], "d_x%d" % (t16 % 2), (), [xb_.r])
                        tt = t16 // 4
                        for half in range(2):
                            b = (t16 * 2 + half) % 4
                            for i in range(4):
                                c = half * 4 + i
                                self.tr(bank(b)[:, i * 128:(i + 1) * 128], xb_.t[:, c * 128:(c + 1) * 128], identf.t[:],
                                        [xb_.r, identf.r], [RB[b]])
                            wr_ = [RX[half * 4 + i][tt] for i in range(4)]
                            self.cp(xT.t[:, half * 4:half * 4 + 4, t16 * 128:(t16 + 1) * 128],
                                    bank(b).rearrange("p (c t) -> p c t", c=4), [RB[b]], wr_,
                                    eng="act" if half == 0 else "dve")
                S.barrier()
                for l in range(self.layers):
                    if FL["ffn1"]:
                        ffn(l, 0)
                    if FL["ssd"] or FL["att"]:
                        mixer(l, p)
                    if FL["ffn2"]:
                        ffn(l, 1)
                with ExitStack() as st:
                    osb = [self.sb(st, "osb%d" % i, [128, D]) for i in range(2)]
                    sqb = [self.sb(st, "fsq%d" % i, [128, TT], BF16) for i in range(2)]
                    rstds = [self.sb(st, "frstd%d" % i, [128, TT]) for i in range(NTT)]
                    ntm = [self.sb(st, "fnt%d" % i, [128, TT]) for i in range(3)]
                    if FL["fn"]:
                        rms_all(sqb, rstds, ntm, lambda c: gsm.t[:, G_FN + c:G_FN + c + 1], lambda c: 0.0,
                                lambda tt, c: xT.t[:, c, tt * TT:(tt + 1) * TT], lambda tt, c: RX[c][tt])
                    for tt in range(NTT):
                        for t4 in range(4):
                            ob = osb[t4 % 2]
                            c0_ = tt * TT + t4 * 128
                            for half in range(2):
                                b = (t4 * 2 + half) % 4
                                for i in range(4):
                                    c = half * 4 + i
                                    self.tr(bank(b)[:, i * 128:(i + 1) * 128], xT.t[:, c, c0_:c0_ + 128], identf.t[:],
                                            [RX[c][tt], identf.r], [RB[b]])
                                self.cp(ob.t[:, half * 512:(half + 1) * 512], bank(b), [RB[b]], [ob.r],
                                        eng="act" if half == 0 else "dve")
                            r0 = p * NT + tt * TT + t4 * 128
                            self.dma("sp", out_d[r0:r0 + 128, :], ob.t[:], "d_out%d" % (t4 % 2), [ob.r], [R_out])
                S.barrier()
            S.wait_all("sp", [k for k in S.cnt if k.startswith("d_out") or k.startswith("d_s")])
            S.emit()
        return nc


def _t5_bucket_np(dist):
    max_exact = 16
    d = np.maximum(dist.astype(np.float32), np.float32(max_exact))
    large = max_exact + (np.log(d / np.float32(max_exact)) / np.float32(math.log(128 / max_exact))
                         * np.float32(32 - max_exact)).astype(np.int32)
    large = np.minimum(large, 31)
    return np.where(dist < max_exact, dist, large)


def _bucket_tiles():
    k = np.arange(128)[:, None]
    q = np.arange(128)[None, :]
    d0 = q - k
    b0 = np.where(d0 >= 0, _t5_bucket_np(np.maximum(d0, 0)), 32).astype(np.float32)
    d1 = q - k + 128
    b1 = _t5_bucket_np(d1).astype(np.float32)
    return b0, b1


def pack_weights(inp):
    f = np.float32
    L = DEPTH
    A = lambda a: np.ascontiguousarray(a, dtype=f)

    def kblk(w):
        return w.reshape(8, 128, -1).transpose(1, 0, 2)

    out = {}
    aw = inp["ada_w"]
    out["adaw"] = A(np.stack([np.stack([kblk(aw[l][:, b * 512:(b + 1) * 512]) for b in range(18)]) for l in range(L)]))
    for fi, (n13, n2) in enumerate((("ffn1_w13", "ffn1_w2"), ("ffn2_w13", "ffn2_w2")), start=1):
        w13 = inp[n13]
        blocks = []
        for l in range(L):
            bl = []
            for j in range(NJ):
                a = kblk(w13[l][:, j * 128:(j + 1) * 128])
                u = kblk(w13[l][:, DFF + j * 128:DFF + (j + 1) * 128])
                bl.append(np.concatenate([a, u], axis=2))
            blocks.append(np.stack(bl))
        out["w13_%d" % fi] = A(np.stack(blocks))
        w2 = inp[n2]
        t = np.zeros((L, 4, 8, 128, 6, 128), f)
        for l in range(L):
            for H in range(4):
                for jj in range(FPARTS[H]):
                    j = FOFF[H] + jj
                    t[l, H, :, :, jj, :] = w2[l][j * 128:(j + 1) * 128, :].reshape(128, 8, 128).transpose(1, 0, 2)
        out["w2_%d" % fi] = t
    wi = inp["w_in"]
    out["wz"] = A(np.stack([np.stack([kblk(wi[l][:, h * 512:(h + 1) * 512]) for h in range(2)]) for l in range(L)]))
    out["wx"] = A(np.stack([np.stack([kblk(wi[l][:, 1024 + c * 128:1024 + (c + 1) * 128]) for c in range(12)]) for l in range(L)]))
    out["wdt"] = A(np.stack([kblk(wi[l][:, 2560:2576]) for l in range(L)]))
    qk_off = [2576 + i * 128 for i in range(4)] + [3088 + i * 128 for i in range(4)] + \
             [4112 + i * 128 for i in range(4)] + [4624 + i * 128 for i in range(4)]
    out["wqk"] = A(np.stack([np.stack([kblk(wi[l][:, o:o + 128]) for o in qk_off]) for l in range(L)]))
    out["wv"] = A(np.stack([np.stack([kblk(wi[l][:, 3600:4112]), kblk(wi[l][:, 5136:5648])]) for l in range(L)]))
    wo = inp["w_out"]
    out["wo"] = A(np.stack([np.stack([np.stack([kblk(wo[l][pt * 1024:(pt + 1) * 1024, mh * 512:(mh + 1) * 512])
                                                for mh in range(2)]) for pt in range(2)]) for l in range(L)]))
    small = np.zeros((L, 128, NS), f)
    fm = lambda v: v.reshape(-1, 128).T
    for l in range(L):
        s = small[l]
        s[:, C_N1:C_N1 + 8] = fm(inp["ffn1_norm"][l])
        s[:, C_N2:C_N2 + 8] = fm(inp["mix_norm"][l])
        s[:, C_N3:C_N3 + 8] = fm(inp["ffn2_norm"][l])
        s[:, C_ADAB:C_ADAB + 72] = fm(inp["ada_b"][l])
        cw = inp["ssm_conv_w"][l].reshape(12, 128, 4).transpose(1, 0, 2).reshape(128, 48)
        s[:, C_CONVW:C_CONVW + 48] = cw
        s[:, C_CONVB:C_CONVB + 12] = fm(inp["ssm_conv_b"][l])
        s[:, C_DTB:C_DTB + 16] = inp["ssm_dt_bias"][l][None, :]
        s[:, C_ALOG:C_ALOG + 16] = inp["ssm_a_log"][l][None, :]
        s[:, C_DSK:C_DSK + 16] = inp["ssm_d"][l][None, :]
        s[:, C_SUBLN] = inp["diff_subln"][l]
        s[:, C_LQ1:C_LQ1 + 64] = inp["diff_lambda_q1"][l][None, :]
        s[:, C_LK1:C_LK1 + 64] = inp["diff_lambda_k1"][l][None, :]
        s[:, C_LQ2:C_LQ2 + 64] = inp["diff_lambda_q2"][l][None, :]
        s[:, C_LK2:C_LK2 + 64] = inp["diff_lambda_k2"][l][None, :]
    out["small"] = small
    out["ssmn"] = A(np.broadcast_to(inp["ssm_norm"][:, None, :], (L, 128, 1024)))
    gs = np.zeros((128, NG), f)
    gs[:, G_FN:G_FN + 8] = fm(inp["final_norm"])
    gs[:, G_RB:G_RB + 128] = inp["rel_bias"].reshape(1, 128)
    b0, b1 = _bucket_tiles()
    gs[:, G_B0:G_B0 + 128] = b0
    gs[:, G_B1:G_B1 + 128] = b1
    out["gsmall"] = gs
    return out


_PROG_CACHE = {}


def run(inputs, NT=2048, NPASS=2, n_batch=4, **flags):
    inp = {k: np.asarray(v, dtype=np.float32) for k, v in inputs.items()}
    key = (NT, NPASS, tuple(sorted(flags.items())))
    if key not in _PROG_CACHE:
        _PROG_CACHE[key] = Builder(NT, NPASS, **flags).build()
    nc = _PROG_CACHE[key]
    shared = pack_weights(inp)
    ntok = NT * NPASS
    in_maps = []
    for core in range(8):
        b = core % n_batch
        m = dict(shared)
        m["xin"] = np.ascontiguousarray(inp["x"][b, :ntok, :])
        m["cT"] = np.ascontiguousarray(inp["c"][b].reshape(8, 128).T)
        in_maps.append(m)
    res = run_bass_kernel_spmd(nc, in_maps, core_ids=list(range(8)))
    return np.stack([res.results[b]["out"] for b in range(n_batch)], axis=0)


def kernel(**inputs):
    out = run(inputs, NT=2048, NPASS=2)
    return out.astype(np.float32)
```
